# Optimizing a Trainium2 kernel written in Bass

```python
import math
import jax, jax.numpy as jnp
from jax import lax
import numpy as np


D_MODEL = 1024
BATCH = 1
SEQ = 16384
DEPTH = 4
DEC_BATCH = 8
DEC_SEQ = 2048
PAST_LEN = 128

GRID_W = 64
N_HEADS = 8
N_KV_HEADS = 2
HEAD_DIM = 64
GROUP = N_HEADS // N_KV_HEADS
ROPE_THETA = 10000.0
ROPE_AXIS_DIM = HEAD_DIM // 2
Q_BLOCK = 128
ATT_W = N_HEADS * HEAD_DIM
KV_W = N_KV_HEADS * HEAD_DIM
HY_W = D_MODEL // 2
HY_EMB = 33
HY_BANDS = (HY_EMB - 1) // 2
HY_FILTER_W = 64
HY_TARGET = 1e-2
HY_FAST = 0.3
HY_SLOW = 1.5
GLA_HEADS = 4
GLA_DK = 64
GLA_DV = 128
GLA_K = GLA_HEADS * GLA_DK
GLA_V = GLA_HEADS * GLA_DV
GLA_RANK = 16
GLA_TAU = 16.0
GLA_CHUNK = 64
N_BRANCH = 3
MIX_W = 512
D_FF = 2816
EPS = 1e-6
IN_SIZES = (ATT_W, KV_W, KV_W, 3 * HY_W, GLA_K, GLA_K, GLA_V, GLA_V, 2 * GLA_RANK, N_BRANCH * D_MODEL)
IN_COLS = ATT_W + 2 * KV_W + 3 * HY_W + 2 * GLA_K + 2 * GLA_V + 2 * GLA_RANK + N_BRANCH * D_MODEL

kernel_name = 'hybrid_bidir_encoder'


def rms_norm(x, g):
    xf = x.astype(jnp.float32)
    y = xf * lax.rsqrt(jnp.mean(xf * xf, axis=-1, keepdims=True) + EPS)
    return (y * g.astype(jnp.float32)).astype(x.dtype)


def split_cols(z):
    out = []
    off = 0
    for s in IN_SIZES:
        out.append(z[..., off:off + s])
        off += s
    return out


def axial_rope(L):
    rows = L // GRID_W
    r = jnp.repeat(jnp.arange(rows, dtype=jnp.float32), GRID_W)
    c = jnp.tile(jnp.arange(GRID_W, dtype=jnp.float32), rows)
    inv = ROPE_THETA ** (-jnp.arange(0, ROPE_AXIS_DIM, 2, dtype=jnp.float32) / ROPE_AXIS_DIM)
    ang = jnp.concatenate([r[:, None] * inv, c[:, None] * inv], axis=-1)
    return jnp.cos(ang), jnp.sin(ang)


def apply_rope(x, cos, sin):
    half = HEAD_DIM // 2
    x1, x2 = x[..., :half], x[..., half:]
    return jnp.concatenate([x1 * cos - x2 * sin, x1 * sin + x2 * cos], axis=-1)


def gqa_attention(q, k, v, q_norm_g, k_norm_g, cos, sin):
    B, L, _ = q.shape
    q = rms_norm(q.astype(jnp.float32).reshape(B, L, N_KV_HEADS, GROUP, HEAD_DIM), q_norm_g)
    k = rms_norm(k.astype(jnp.float32).reshape(B, L, N_KV_HEADS, HEAD_DIM), k_norm_g)
    v = v.reshape(B, L, N_KV_HEADS, HEAD_DIM)
    q = apply_rope(q, cos[None, :, None, None, :], sin[None, :, None, None, :])
    k = apply_rope(k, cos[None, :, None, :], sin[None, :, None, :])
    nb = L // Q_BLOCK
    qb = jnp.moveaxis(q.reshape(B, nb, Q_BLOCK, N_KV_HEADS, GROUP, HEAD_DIM), 1, 0)
    scale = HEAD_DIM ** -0.5

    def block(qi):
        s = jnp.einsum('bqkgd,bskd->bkgqs', qi, k) * scale
        p = jax.nn.softmax(s, axis=-1)
        return jnp.einsum('bkgqs,bskd->bqkgd', p.astype(v.dtype), v)

    o = lax.map(block, qb)
    return jnp.moveaxis(o, 0, 1).reshape(B, L, ATT_W).astype(v.dtype)


def short_conv3(u, w, b):
    up = jnp.pad(u, ((0, 0), (1, 1), (0, 0)))
    return up[:, :-2] * w[0] + up[:, 1:-1] * w[1] + up[:, 2:] * w[2] + b


def hyena_filter(L, w1, b1, f1, w2, b2, f2, w3):
    f32 = jnp.float32
    t = jnp.linspace(0.0, 1.0, L, dtype=f32)[:, None]
    w = 2.0 * math.pi * jnp.arange(L, dtype=f32) / L
    fb = jnp.linspace(1e-4, HY_BANDS - 1, HY_BANDS, dtype=f32)
    ph = w[:, None] * fb
    feats = jnp.concatenate([t, jnp.cos(ph), -jnp.sin(ph)], axis=-1)
    h = jnp.sin(f1.astype(f32) * (feats @ w1.astype(f32) + b1.astype(f32)))
    h = jnp.sin(f2.astype(f32) * (h @ w2.astype(f32) + b2.astype(f32)))
    h = h @ w3.astype(f32)
    deltas = jnp.linspace(abs(math.log(HY_TARGET) / HY_SLOW), abs(math.log(HY_TARGET) / HY_FAST), HY_W, dtype=f32)
    decay = jnp.exp(-t * deltas)
    hf = h[:, :HY_W] * decay
    hb = h[:, HY_W:] * decay
    kf = jnp.concatenate([hf, jnp.zeros((1, HY_W), f32), jnp.flip(hb[1:], axis=0)], axis=0)
    return kf * lax.rsqrt(jnp.sum(kf * kf, axis=0, keepdims=True) + EPS)


def hyena_mixer(z, conv_w, conv_b, w1, b1, f1, w2, b2, f2, w3, skip):
    B, L, _ = z.shape
    zc = short_conv3(z, conv_w, conv_b)
    v, x1, x2 = jnp.split(zc, 3, axis=-1)
    kf = hyena_filter(L, w1, b1, f1, w2, b2, f2, w3)
    u = (v * x1).astype(jnp.float32)
    U = jnp.fft.rfft(u, n=2 * L, axis=1)
    K = jnp.fft.rfft(kf, n=2 * L, axis=0)
    y = jnp.fft.irfft(U * K[None], n=2 * L, axis=1)[:, :L] + u * skip.astype(jnp.float32)
    return (x2.astype(jnp.float32) * y).astype(z.dtype)


def gla_chunked(q, k, v, log_a, include_diag):
    B, L, H, DK = q.shape
    DV = v.shape[-1]
    C = GLA_CHUNK
    nc = L // C
    q = q.reshape(B, nc, C, H, DK)
    k = k.reshape(B, nc, C, H, DK)
    v = v.reshape(B, nc, C, H, DV)
    b = jnp.cumsum(log_a.reshape(B, nc, C, H, DK), axis=2)
    b_mid = b[:, :, C // 2 - 1:C // 2]
    A = jnp.einsum('bnihd,bnjhd->bnhij', q * jnp.exp(b - b_mid), k * jnp.exp(b_mid - b))
    mask = jnp.tril(jnp.ones((C, C), dtype=bool), k=0 if include_diag else -1)
    A = jnp.where(mask, A, 0.0)
    o_intra = jnp.einsum('bnhij,bnjhv->bnihv', A, v)
    b_last = b[:, :, C - 1:]
    U = jnp.einsum('bnjhd,bnjhv->nbhdv', k * jnp.exp(b_last - b), v)
    decay = jnp.moveaxis(jnp.exp(b_last[:, :, 0]), 1, 0)

    def step(S, inp):
        d, u = inp
        return d[..., None] * S + u, S

    _, S_prev = lax.scan(step, jnp.zeros((B, H, DK, DV), jnp.float32), (decay, U))
    o_inter = jnp.einsum('bnihd,nbhdv->bnihv', q * jnp.exp(b), S_prev)
    return (o_intra + o_inter).reshape(B, L, H, DV)


def gla_mixer(q, k, v, og, glow, gate_up, gate_b, norm_g):
    B, L, _ = q.shape
    f32 = jnp.float32
    q = q.astype(f32).reshape(B, L, GLA_HEADS, GLA_DK) * (GLA_DK ** -0.5)
    k = k.astype(f32).reshape(B, L, GLA_HEADS, GLA_DK)
    v = v.astype(f32).reshape(B, L, GLA_HEADS, GLA_DV)
    lo = glow.astype(f32).reshape(B, L, 2, GLA_RANK)
    logit = jnp.einsum('blnr,nrk->blnk', lo, gate_up.astype(f32)) + gate_b.astype(f32)
    log_a = jax.nn.log_sigmoid(logit) / GLA_TAU
    la_f = log_a[:, :, 0].reshape(B, L, GLA_HEADS, GLA_DK)
    la_b = log_a[:, :, 1].reshape(B, L, GLA_HEADS, GLA_DK)
    o_f = gla_chunked(q, k, v, la_f, True)
    o_b = jnp.flip(gla_chunked(jnp.flip(q, 1), jnp.flip(k, 1), jnp.flip(v, 1), jnp.flip(la_b, 1), False), axis=1)
    o = rms_norm(o_f + o_b, norm_g)
    o = o * jax.nn.silu(og.astype(f32)).reshape(B, L, GLA_HEADS, GLA_DV)
    return o.reshape(B, L, GLA_V).astype(og.dtype)


def encoder_layer(x, norm_mix_g, w_in, q_norm_g, k_norm_g, hy_conv_w, hy_conv_b, hy_w1, hy_b1, hy_f1, hy_w2, hy_b2, hy_f2, hy_w3, hy_skip, gla_gate_up, gla_gate_b, gla_norm_g, w_branch, w_out, norm_ffn_g, w_ffn_gate, w_ffn_up, w_ffn_down):
    B, L, _ = x.shape
    h = rms_norm(x, norm_mix_g)
    z = h @ w_in
    aq, ak, av, hz, gq, gk, gv, gog, glow, gates = split_cols(z)
    cos, sin = axial_rope(L)
    y_att = gqa_attention(aq, ak, av, q_norm_g, k_norm_g, cos, sin)
    y_hy = hyena_mixer(hz, hy_conv_w, hy_conv_b, hy_w1, hy_b1, hy_f1, hy_w2, hy_b2, hy_f2, hy_w3, hy_skip)
    y_gla = gla_mixer(gq, gk, gv, gog, glow, gla_gate_up, gla_gate_b, gla_norm_g)
    ys = jnp.stack([y_att, y_hy, y_gla], axis=2)
    proj = jnp.einsum('blnc,ncd->blnd', ys, w_branch)
    g = jax.nn.sigmoid(gates.reshape(B, L, N_BRANCH, D_MODEL))
    merged = jnp.sum(g * proj, axis=2)
    x = x + merged @ w_out
    h = rms_norm(x, norm_ffn_g)
    x = x + (jax.nn.silu(h @ w_ffn_gate) * (h @ w_ffn_up)) @ w_ffn_down
    return x


def setup_inputs(seed: int = 0) -> dict:
    key = jax.random.key(seed)
    ks = jax.random.split(key, 28)
    f32 = jnp.float32

    def nrm(k, shape, scale):
        return jax.random.normal(k, shape, f32) * scale

    return {
        'x_prompt': nrm(ks[0], (BATCH, SEQ, D_MODEL), 1.0),
        'x_sample': nrm(ks[1], (DEC_BATCH, DEC_SEQ, D_MODEL), 1.0),
        'norm_mix_g': 1.0 + nrm(ks[2], (DEPTH, D_MODEL), 0.01),
        'w_in': nrm(ks[3], (DEPTH, D_MODEL, IN_COLS), D_MODEL ** -0.5),
        'q_norm_g': 1.0 + nrm(ks[4], (DEPTH, HEAD_DIM), 0.01),
        'k_norm_g': 1.0 + nrm(ks[5], (DEPTH, HEAD_DIM), 0.01),
        'hy_conv_w': nrm(ks[6], (DEPTH, 3, 3 * HY_W), 3 ** -0.5),
        'hy_conv_b': nrm(ks[7], (DEPTH, 3 * HY_W), 0.01),
        'hy_w1': nrm(ks[8], (DEPTH, HY_EMB, HY_FILTER_W), HY_EMB ** -0.5),
        'hy_b1': nrm(ks[9], (DEPTH, HY_FILTER_W), 0.1),
        'hy_f1': 1.0 + nrm(ks[10], (DEPTH, HY_FILTER_W), 0.01),
        'hy_w2': nrm(ks[11], (DEPTH, HY_FILTER_W, HY_FILTER_W), HY_FILTER_W ** -0.5),
        'hy_b2': nrm(ks[12], (DEPTH, HY_FILTER_W), 0.1),
        'hy_f2': 1.0 + nrm(ks[13], (DEPTH, HY_FILTER_W), 0.01),
        'hy_w3': nrm(ks[14], (DEPTH, HY_FILTER_W, 2 * HY_W), HY_FILTER_W ** -0.5),
        'hy_skip': nrm(ks[15], (DEPTH, HY_W), 0.5),
        'gla_gate_up': nrm(ks[16], (DEPTH, 2, GLA_RANK, GLA_K), GLA_RANK ** -0.5),
        'gla_gate_b': nrm(ks[17], (DEPTH, 2, GLA_K), 0.1),
        'gla_norm_g': 1.0 + nrm(ks[18], (DEPTH, GLA_DV), 0.01),
        'w_branch': nrm(ks[19], (DEPTH, N_BRANCH, MIX_W, D_MODEL), MIX_W ** -0.5),
        'w_out': nrm(ks[20], (DEPTH, D_MODEL, D_MODEL), D_MODEL ** -0.5),
        'norm_ffn_g': 1.0 + nrm(ks[21], (DEPTH, D_MODEL), 0.01),
        'w_ffn_gate': nrm(ks[22], (DEPTH, D_MODEL, D_FF), D_MODEL ** -0.5),
        'w_ffn_up': nrm(ks[23], (DEPTH, D_MODEL, D_FF), D_MODEL ** -0.5),
        'w_ffn_down': nrm(ks[24], (DEPTH, D_FF, D_MODEL), D_FF ** -0.5),
        'final_norm_g': 1.0 + nrm(ks[25], (D_MODEL,), 0.01),
    }


def reference(x_prompt, x_sample, norm_mix_g, w_in, q_norm_g, k_norm_g, hy_conv_w, hy_conv_b, hy_w1, hy_b1, hy_f1, hy_w2, hy_b2, hy_f2, hy_w3, hy_skip, gla_gate_up, gla_gate_b, gla_norm_g, w_branch, w_out, norm_ffn_g, w_ffn_gate, w_ffn_up, w_ffn_down, final_norm_g):
    xp = x_prompt
    xs = x_sample
    for l in range(DEPTH):
        p = (norm_mix_g[l], w_in[l], q_norm_g[l], k_norm_g[l], hy_conv_w[l], hy_conv_b[l], hy_w1[l], hy_b1[l], hy_f1[l], hy_w2[l], hy_b2[l], hy_f2[l], hy_w3[l], hy_skip[l], gla_gate_up[l], gla_gate_b[l], gla_norm_g[l], w_branch[l], w_out[l], norm_ffn_g[l], w_ffn_gate[l], w_ffn_up[l], w_ffn_down[l])
        xp = encoder_layer(xp, *p)
        xs = encoder_layer(xs, *p)
    y_prompt = rms_norm(xp, final_norm_g)
    y_sample = rms_norm(xs, final_norm_g)
    return (y_prompt, y_sample)
```

```python
import os
import numpy as np
import concourse.bass as bass
import concourse.mybir as mybir

F32 = mybir.dt.float32
BF16 = mybir.dt.bfloat16
AF = mybir.ActivationFunctionType
ALU = mybir.AluOpType
AX = mybir.AxisListType


class Buf:
    __slots__ = ("name", "lw", "rd", "sem", "semv")

    def __init__(self, name, sem=None):
        self.name = name
        self.lw = None
        self.rd = {}
        self.sem = sem
        self.semv = 0


class KB:
    def __init__(self, nc):
        self.nc = nc
        self.E = {"pe": nc.tensor, "act": nc.scalar, "dve": nc.vector, "pool": nc.gpsimd, "sp": nc.sync}
        self.sems = {}
        self.cnt = {}
        self.waited = {e: {} for e in self.E}
        self._ctx = []
        for e in self.E:
            self._newsem("E_" + e)
        self.n_ins = 0
        self.epoch = 0
        self.free_d = []
        self.phase_d = []
        self.LIMIT = int(os.environ.get("KB_LIMIT", "150000"))

    def _newsem(self, key):
        while key in self.sems:
            key = key + "_"
        cm = self.nc.semaphore(key)
        h = cm.__enter__()
        self._ctx.append(cm)
        self.sems[key] = h
        self.cnt[key] = 0
        return key

    def dsem(self, name):
        if self.free_d:
            k = self.free_d.pop()
        else:
            k = self._newsem("D_" + name)
        self.phase_d.append(k)
        return k

    def phase_release(self):
        self.free_d.extend(self.phase_d)
        self.phase_d = []

    def full_wait(self):
        tgt = {k: v for k, v in self.cnt.items() if v > 0}
        for e in self.E:
            self._wait(e, dict(tgt))

    def epoch_barrier(self):
        self.full_wait()
        self.nc.all_engine_barrier()
        for k, h in self.sems.items():
            if self.cnt[k] > 0 and k.startswith("E_"):
                self.E["pool"].sem_clear(h)
                self.cnt[k] = 0
        self.nc.all_engine_barrier()
        for e in self.E:
            for k in list(self.waited[e]):
                if k.startswith("E_"):
                    del self.waited[e][k]
        self.epoch += 1

    def buf(self, name, dma=False):
        return Buf(name, self.dsem(name) if dma else None)

    def _deps(self, reads, writes):
        deps = {}
        def add(ev):
            if ev is None:
                return
            k, v, ep = ev
            if ep != self.epoch and k.startswith("E_"):
                return
            if k.startswith("D_"):
                v = self.cnt[k]
            if deps.get(k, 0) < v:
                deps[k] = v
        for b in reads:
            add(b.lw)
        for b in writes:
            add(b.lw)
            for k, (v, ep) in b.rd.items():
                add((k, v, ep))
        return deps

    def _wait(self, eng, deps):
        w = self.waited[eng]
        for k, v in deps.items():
            if w.get(k, 0) < v:
                self.E[eng].wait_ge(self.sems[k], v)
                w[k] = v

    def _mark(self, ev, reads, writes):
        k, v = ev
        ep = self.epoch
        for b in writes:
            b.lw = (k, v, ep)
            b.rd = {}
        for b in reads:
            o = b.rd.get(k)
            if o is None or o[1] != ep or o[0] < v:
                b.rd[k] = (v, ep)

    def op(self, eng, fn, reads=(), writes=()):
        if self.cnt["E_" + eng] >= self.LIMIT:
            self.epoch_barrier()
        self._wait(eng, self._deps(reads, writes))
        ins = fn(self.E[eng])
        k = "E_" + eng
        self.cnt[k] += 1
        ins.then_inc(self.sems[k], 1)
        self._mark((k, self.cnt[k]), reads, writes)
        self.n_ins += 1
        return ins

    def dma(self, q, out, in_, reads=(), writes=(), sem=None, **kw):
        self._wait(q, self._deps(reads, writes))
        ins = self.E[q].dma_start(out=out, in_=in_, **kw)
        k = sem.sem
        self.cnt[k] += 16
        ins.then_inc(self.sems[k], 16)
        self._mark((k, self.cnt[k]), reads, writes)
        self.n_ins += 1
        return ins

    def wait_all(self, eng, bufs):
        self._wait(eng, self._deps(bufs, ()))

    def barrier(self, bufs=()):
        pass

    def close(self):
        for cm in reversed(self._ctx):
            cm.__exit__(None, None, None)


import math
from contextlib import ExitStack
import numpy as np
import ml_dtypes
import concourse.bass as bass
import concourse.mybir as mybir
from concourse.bass_utils import run_bass_kernel_spmd

D = 1024
NH, NKV, HD = 8, 2, 64
D_FF = 2816
NFC = D_FF // 128
IN_COLS = 6944
EPS = 1e-6
TT = 512
P_Q, P_K0, P_K1, P_V, P_HZ, P_GQ, P_GK, P_GV, P_GOG, P_LO = 0, 512, 640, 768, 896, 2432, 2688, 2944, 3456, 3968
NCP = 4000
O_AQ, O_AK, O_AV, O_HZ, O_GQ, O_GK, O_GV, O_GOG, O_GLO, O_GATE = 0, 512, 640, 768, 2304, 2560, 2816, 3328, 3840, 3872

WNAMES = ["norm_mix_g", "w_in", "q_norm_g", "k_norm_g", "hy_conv_w", "hy_conv_b", "hy_w1", "hy_b1", "hy_f1",
          "hy_w2", "hy_b2", "hy_f2", "hy_w3", "hy_skip", "gla_gate_up", "gla_gate_b", "gla_norm_g", "w_branch",
          "w_out", "norm_ffn_g", "w_ffn_gate", "w_ffn_up", "w_ffn_down", "final_norm_g"]


class Seq:
    def __init__(self, name, L):
        self.name, self.L = name, L


class Builder:
    def __init__(self, LP, LS, depth, shapes, debug_taps=()):
        self.LP, self.LS, self.depth = LP, LS, depth
        nc = self.nc = bass.Bass("TRN2", target_bir_lowering=False)
        self.kb = KB(nc)
        self.taps = debug_taps
        self.seqs = [Seq("p", LP), Seq("s", LS)]
        self.ext = {}
        for nm, shp in shapes.items():
            self.ext[nm] = nc.dram_tensor(nm, list(shp), F32, kind="ExternalInput")
        self.out = {"p": nc.dram_tensor("y_p", [LP, D], F32, kind="ExternalOutput"),
                    "s": nc.dram_tensor("y_s", [LS, D], F32, kind="ExternalOutput")}
        self.xin = {"p": self.ext["x_p"], "s": self.ext["x_s"]}
        def scratch(name, shape, dt):
            kind = {"kind": "ExternalOutput"} if name in debug_taps else {}
            return nc.dram_tensor(name, list(shape), dt, **kind)
        self.XT = {s.name: scratch("XT_" + s.name, [8, 128, s.L], F32) for s in self.seqs}
        self.YT = {s.name: scratch("YT_" + s.name, [12, 128, s.L], BF16) for s in self.seqs}
        dp = depth
        self.NL = shapes["norm_mix_g"][0]
        self.QT = {s.name: scratch("QT_" + s.name, [4, 128, s.L], BF16) for s in self.seqs}
        self.KT = {s.name: scratch("KT_" + s.name, [2, 128, s.L], BF16) for s in self.seqs}
        self.VA = {s.name: scratch("VA_" + s.name, [s.L // 128, 128, 2, 128], BF16) for s in self.seqs}
        self.HZT = {s.name: scratch("HZT_" + s.name, [12, 128, s.L], F32) for s in self.seqs}
        self.GQT = {s.name: scratch("GQT_" + s.name, [2, 128, s.L], F32) for s in self.seqs}
        self.GKT = {s.name: scratch("GKT_" + s.name, [2, 128, s.L], F32) for s in self.seqs}
        self.GV = {s.name: scratch("GV_" + s.name, [s.L, 512], BF16) for s in self.seqs}
        self.GOG = {s.name: scratch("GOG_" + s.name, [s.L, 512], F32) for s in self.seqs}
        self.LOT = {s.name: scratch("LOT_" + s.name, [2, 16, s.L], F32) for s in self.seqs}
        self.b_P1 = {s.name: self.kb.buf("P1" + s.name) for s in self.seqs}
        self.dbg = {}
        if "DBG1" in debug_taps:
            self.dbg = {"DBG1": scratch("DBG1", [128, 256], F32), "DBG2": scratch("DBG2", [128, 1024], F32),
                        "DBG3": scratch("DBG3", [128, 1024], BF16), "DBG4": scratch("DBG4", [128, 512], BF16)}
        self.OF = {s.name: scratch("OF_" + s.name, [s.L, 512], F32) for s in self.seqs}
        self.KFT = {s.name: scratch("KFT_" + s.name, [4, 128, 2 * s.L], BF16) for s in self.seqs}
        self.KFH = {s.name: scratch("KFH_" + s.name, [512 // self.hy_cfg(s.L)[4], 128, 2, 512], BF16) for s in self.seqs}
        self.UD = {s.name: scratch("UD_" + s.name, [4, 128, s.L], BF16) for s in self.seqs}
        self.X2R = {s.name: scratch("X2R_" + s.name, [4, 128, s.L], F32) for s in self.seqs}
        self.X2S = {s.name: scratch("X2S_" + s.name, [4, 128, s.L], F32) for s in self.seqs}
        self.wb_in = scratch("wb_in", [dp, D, NCP], BF16)
        self.wb = {
            "w_gates": scratch("wb_gates", [dp, D, 3072], BF16),
            "w_branch": scratch("wb_branch", [dp, 1536, D], BF16),
            "w_out": scratch("wb_out", [dp, D, D], BF16),
            "w_ffn_gate": scratch("wb_fg", [dp, D, D_FF], BF16),
            "w_ffn_up": scratch("wb_fu", [dp, D, D_FF], BF16),
            "w_ffn_down": scratch("wb_fd", [dp, D_FF, D], BF16),
        }
        kb = self.kb
        self.b_XT = {s.name: kb.buf("XT" + s.name) for s in self.seqs}
        self.b_YT = {s.name: kb.buf("YT" + s.name) for s in self.seqs}
        self.b_wb = kb.buf("wb", dma=True)
        self.b_out = kb.buf("out")

    def sbt(self, name, shape, dt):
        self._uid = getattr(self, "_uid", 0) + 1
        return self.nc.sbuf_tensor(f"{name}_{self._uid}", shape, dt)

    def pst(self, name, shape, dt):
        self._uid = getattr(self, "_uid", 0) + 1
        return self.nc.psum_tensor(f"{name}_{self._uid}", shape, dt)

    def full_barrier(self):
        self.kb.full_wait()
        self.kb.phase_release()

    def consts(self, es):
        nc, kb = self.nc, self.kb
        dp = self.NL
        self.c_sem = kb.buf("consts", dma=True)
        c = self.c = {}
        def ld(name, shape, dt, src_ap, q="sp"):
            t = es.enter_context(self.sbt("c_" + name, shape, dt))
            kb.dma(q, t[:], src_ap, reads=(), writes=(self.c_sem,), sem=self.c_sem, allow_slow_non_contiguous=True)
            c[name] = t
            return t
        ld("ident_f", [128, 128], F32, self.ext["c_ident"][:, :])
        ld("ident_b", [128, 128], BF16, self.ext["c_ident"][:, :], q="pool")
        ld("ones_b", [128, 128], BF16, self.ext["c_ones"][:, :], q="pool")
        ld("g_mix", [128, dp, 8], F32, self.ext["norm_mix_g"].ap().rearrange("l (k p) -> p l k", p=128))
        ld("g_ffn", [128, dp, 8], F32, self.ext["norm_ffn_g"].ap().rearrange("l (k p) -> p l k", p=128))
        ld("rot", [128, 128], BF16, self.ext["c_rot"][:, :], q="pool")
        ld("bo64", [128, 128], BF16, self.ext["c_bo64"][:, :], q="pool")
        for nm, src in (("gq", "q_norm_g"), ("gk", "k_norm_g")):
            t = es.enter_context(self.sbt("c_" + nm, [128, dp], F32))
            for hh in range(2):
                kb.dma("sp", t[hh * 64:(hh + 1) * 64, :], self.ext[src].ap().rearrange("l d -> d l"), writes=(self.c_sem,),
                       sem=self.c_sem, allow_slow_non_contiguous=True)
            c[nm] = t
        ld("g_fin", [128, 8], F32, self.ext["final_norm_g"].ap().rearrange("(k p) -> p k", p=128))

    def prepass_weights(self):
        kb = self.kb
        e = self.ext
        for l in range(self.depth):
            for r in range(8):
                rs = slice(r * 128, (r + 1) * 128)
                def cp(dst0, src0, n):
                    kb.dma("pool", self.wb_in[l, rs, dst0:dst0 + n], e["w_in"][l, rs, src0:src0 + n], writes=(self.b_wb,), sem=self.b_wb)
                cp(P_Q, O_AQ, 512)
                for g in range(2):
                    cp(P_K0 + g * 128, O_AK + g * 64, 64)
                    cp(P_K0 + g * 128 + 64, O_AK + g * 64, 64)
                cp(P_V, O_AV, 128)
                cp(P_HZ, O_HZ, 1536 + 256 + 256 + 512 + 512 + 32)
                kb.dma("pool", self.wb["w_gates"][l, rs, :], e["w_in"][l, rs, O_GATE:O_GATE + 3072],
                       writes=(self.b_wb,), sem=self.b_wb)
                kb.dma("pool", self.wb["w_out"][l, rs, :], e["w_out"][l, rs, :], writes=(self.b_wb,), sem=self.b_wb)
                kb.dma("pool", self.wb["w_ffn_gate"][l, rs, :], e["w_ffn_gate"][l, rs, :], writes=(self.b_wb,), sem=self.b_wb)
                kb.dma("pool", self.wb["w_ffn_up"][l, rs, :], e["w_ffn_up"][l, rs, :], writes=(self.b_wb,), sem=self.b_wb)
            for r in range(NFC):
                rs = slice(r * 128, (r + 1) * 128)
                kb.dma("pool", self.wb["w_ffn_down"][l, rs, :], e["w_ffn_down"][l, rs, :], writes=(self.b_wb,), sem=self.b_wb)
            for n in range(3):
                for r in range(4):
                    kb.dma("pool", self.wb["w_branch"][l, n * 512 + r * 128:n * 512 + (r + 1) * 128, :],
                           e["w_branch"][l, n, r * 128:(r + 1) * 128, :], writes=(self.b_wb,), sem=self.b_wb)

    def phase_in(self, seq):
        nc, kb, c = self.nc, self.kb, self.c
        L = seq.L
        x = self.xin[seq.name]
        with ExitStack() as es:
            xt = [es.enter_context(self.sbt(f"pi_x{i}", [128, D], F32)) for i in range(2)]
            xo = [es.enter_context(self.sbt(f"pi_o{i}", [128, 8, 128], F32)) for i in range(2)]
            ps = [es.enter_context(self.pst(f"pi_ps{i}", [128, 512], F32)) for i in range(4)]
            b_xt = [kb.buf(f"pi_x{i}", dma=True) for i in range(2)]
            b_xo = [kb.buf(f"pi_o{i}", dma=True) for i in range(2)]
            b_ps = [kb.buf(f"pi_ps{i}") for i in range(4)]
            for bl in range(L // 128):
                i = bl % 2
                kb.dma("sp", xt[i][:], x[bl * 128:(bl + 1) * 128, :], writes=(b_xt[i],), sem=b_xt[i])
                for hf in range(2):
                    pb = (bl * 2 + hf) % 4
                    for k4 in range(4):
                        k = hf * 4 + k4
                        kb.op("pe", lambda e: e.transpose(out=ps[pb][:, k4 * 128:(k4 + 1) * 128],
                                                          in_=xt[i][:, k * 128:(k + 1) * 128], identity=c["ident_f"][:]),
                              reads=(b_xt[i], self.c_sem), writes=(b_ps[pb],))
                    eng = "dve" if hf == 0 else "act"
                    dst = xo[i][:, hf * 4:(hf + 1) * 4, :]
                    src = ps[pb][:].rearrange("p (k n) -> p k n", k=4)
                    if eng == "dve":
                        kb.op("dve", lambda e: e.tensor_copy(out=dst, in_=src), reads=(b_ps[pb],), writes=(b_xo[i],))
                    else:
                        kb.op("act", lambda e: e.copy(out=dst, in_=src), reads=(b_ps[pb],), writes=(b_xo[i],))
                kb.dma("pool", self.XT[seq.name][:, :, bl * 128:(bl + 1) * 128].rearrange("k p n -> p k n"), xo[i][:],
                       reads=(b_xo[i],), writes=(self.b_XT[seq.name],), sem=b_xo[i])
        self.full_barrier()

    def rmsnorm_T(self, xt, b_xt, hT, b_hT, gains, sq, b_sq, ps, b_ps, rstd, b_rstd, extra_reads=()):
        kb, c = self.kb, self.c
        kb.op("act", lambda e: e.activation(out=sq[:], in_=xt[:], func=AF.Square), reads=(b_xt,), writes=(b_sq,))
        for k in range(8):
            kb.op("pe", lambda e: e.matmul(ps[:], lhsT=c["ones_b"][:], rhs=sq[:, k, :], start=(k == 0), stop=(k == 7)),
                  reads=(b_sq, self.c_sem), writes=(b_ps,))
        kb.op("dve", lambda e: e.tensor_scalar(out=rstd[:], in0=ps[:], scalar1=1.0 / D, scalar2=EPS, op0=ALU.mult, op1=ALU.add),
              reads=(b_ps,), writes=(b_rstd,))
        kb.op("act", lambda e: e.activation(out=rstd[:], in_=rstd[:], func=AF.Ln), reads=(b_rstd,), writes=(b_rstd,))
        kb.op("act", lambda e: e.activation(out=rstd[:], in_=rstd[:], func=AF.Exp, scale=-0.5), reads=(b_rstd,), writes=(b_rstd,))
        for k in range(8):
            kb.op("dve", lambda e: e.scalar_tensor_tensor(out=hT[:, k, :], in0=xt[:, k, :], scalar=gains[:, k:k + 1], in1=rstd[:],
                                                          op0=ALU.mult, op1=ALU.mult),
                  reads=(b_xt, b_rstd, self.c_sem) + tuple(extra_reads), writes=(b_hT,))


    def phase_proj(self, seq, l):
        nc, kb, c = self.nc, self.kb, self.c
        L, sn = seq.L, seq.name
        XT, bXT = self.XT[sn], self.b_XT[sn]
        bP1 = self.b_P1[sn]
        with ExitStack() as es:
            S = lambda n, shp, dt: es.enter_context(self.sbt("pp_" + n, shp, dt))
            w = S("w", [128, 8, NCP], BF16)
            b_w = kb.buf("pp_w", dma=True)
            for k in range(8):
                kb.dma("sp", w[:, k, :], self.wb_in[l, k * 128:(k + 1) * 128, :], reads=(self.b_wb,), writes=(b_w,), sem=b_w)
            xt = [S(f"xt{i}", [128, 8, TT], F32) for i in range(2)]
            cs = [S(f"cs{i}", [128, 2, TT], F32) for i in range(2)]
            hT = S("hT", [128, 8, TT], BF16)
            sq = S("sq", [128, 8, TT], BF16)
            rstd = S("rstd", [128, TT], F32)
            NO = 4
            ob = [S(f"ob{i}", [128, 2048], F32) for i in range(NO)]
            qn = [S(f"qn{i}", [128, TT], BF16) for i in range(2)]
            q2 = [S(f"q2{i}", [128, TT], BF16) for i in range(2)]
            rq = [S(f"rq{i}", [128, TT], F32) for i in range(2)]
            t1 = [S(f"t1{i}", [128, TT], F32) for i in range(2)]
            va = [S(f"va{i}", [128, 4, 2, 128], BF16) for i in range(2)]
            ps = [es.enter_context(self.pst(f"pp_ps{i}", [128, 512], F32)) for i in range(8)]
            b_xt = [kb.buf(f"pp_xt{i}", dma=True) for i in range(2)]
            b_cs = [kb.buf(f"pp_cs{i}", dma=True) for i in range(2)]
            b_ob = [kb.buf(f"pp_ob{i}", dma=True) for i in range(NO)]
            b_va = [kb.buf(f"pp_va{i}", dma=True) for i in range(2)]
            b_ps = [kb.buf(f"pp_ps{i}") for i in range(8)]
            b_hT, b_sq, b_rstd = kb.buf("hT"), kb.buf("sq"), kb.buf("rstd")
            b_qn, b_rq, b_t1 = ([kb.buf(n + str(i)) for i in range(2)] for n in ("qn", "rq", "t1"))
            b_q2 = [kb.buf(f"pp_q2{i}", dma=True) for i in range(2)]
            for i in range(2):
                kb.op("dve", lambda e: e.memset(va[i][:], 1.0), writes=(b_va[i],))
            st = {"ps": 0, "ob": 0, "j": 0}
            def nps():
                i = st["ps"] % 7
                st["ps"] += 1
                return ps[i], b_ps[i]
            def nob():
                i = st["ob"] % NO
                st["ob"] += 1
                return ob[i], b_ob[i]
            cosT, sinT = self.ext["c_cos_" + sn], self.ext["c_sin_" + sn]
            ntile = L // TT
            def load_tile(t):
                i = t % 2
                ts = slice(t * TT, (t + 1) * TT)
                kb.dma("sp", xt[i][:], XT[:, :, ts].rearrange("k p n -> p k n"), reads=(bXT,), writes=(b_xt[i],), sem=b_xt[i])
                kb.dma("sp", cs[i][:, 0, :], cosT[:, ts], writes=(b_cs[i],), sem=b_cs[i])
                kb.dma("sp", cs[i][:, 1, :], sinT[:, ts], writes=(b_cs[i],), sem=b_cs[i])
            load_tile(0)
            for t in range(ntile):
                i = t % 2
                ts = slice(t * TT, (t + 1) * TT)
                if t + 1 < ntile:
                    load_tile(t + 1)
                self.rmsnorm_T(xt[i], b_xt[i], hT, b_hT, c["g_mix"][:, l, :], sq, b_sq, ps[7], b_ps[7], rstd, b_rstd)

                def fm_mm(col0, M=128):
                    p, bp = nps()
                    for k in range(8):
                        kb.op("pe", lambda e: e.matmul(p[0:M, :], lhsT=w[:, k, col0:col0 + M], rhs=hT[:, k, :], start=(k == 0), stop=(k == 7)),
                              reads=(b_w, b_hT), writes=(bp,))
                    return p, bp

                for ci in range(6):
                    isq = ci < 4
                    col0 = P_Q + ci * 128 if isq else P_K0 + (ci - 4) * 128
                    gain = c["gq"] if isq else c["gk"]
                    p, bp = fm_mm(col0)
                    j = st["j"] % 2
                    st["j"] += 1
                    kb.op("act", lambda e: e.activation(out=q2[j][:], in_=p[:], func=AF.Square), reads=(bp,), writes=(b_q2[j],))
                    p2, bp2 = nps()
                    kb.op("pe", lambda e: e.matmul(p2[:], lhsT=c["bo64"][:], rhs=q2[j][:], start=True, stop=True),
                          reads=(b_q2[j], self.c_sem), writes=(bp2,))
                    kb.op("dve", lambda e: e.tensor_scalar(out=rq[j][:], in0=p2[:], scalar1=1.0 / HD, scalar2=EPS, op0=ALU.mult, op1=ALU.add),
                          reads=(bp2,), writes=(b_rq[j],))
                    kb.op("act", lambda e: e.activation(out=rq[j][:], in_=rq[j][:], func=AF.Ln), reads=(b_rq[j],), writes=(b_rq[j],))
                    kb.op("act", lambda e: e.activation(out=rq[j][:], in_=rq[j][:], func=AF.Exp, scale=-0.5), reads=(b_rq[j],), writes=(b_rq[j],))
                    kb.op("dve", lambda e: e.scalar_tensor_tensor(out=qn[j][:], in0=p[:], scalar=gain[:, l:l + 1], in1=rq[j][:],
                                                                  op0=ALU.mult, op1=ALU.mult),
                          reads=(bp, b_rq[j], self.c_sem), writes=(b_qn[j],))
                    p3, bp3 = nps()
                    kb.op("pe", lambda e: e.matmul(p3[:], lhsT=c["rot"][:], rhs=qn[j][:], start=True, stop=True),
                          reads=(b_qn[j], self.c_sem), writes=(bp3,))
                    kb.op("dve", lambda e: e.tensor_tensor(out=t1[j][:], in0=qn[j][:], in1=cs[i][:, 0, :], op=ALU.mult),
                          reads=(b_qn[j], b_cs[i]), writes=(b_t1[j],))
                    kb.op("dve", lambda e: e.tensor_tensor(out=rq[j][:], in0=p3[:], in1=cs[i][:, 1, :], op=ALU.mult),
                          reads=(bp3, b_cs[i]), writes=(b_rq[j],))
                    kb.op("dve", lambda e: e.tensor_tensor(out=q2[j][:], in0=t1[j][:], in1=rq[j][:], op=ALU.add),
                          reads=(b_t1[j], b_rq[j]), writes=(b_q2[j],))
                    dst = self.QT[sn][ci, :, ts] if isq else self.KT[sn][ci - 4, :, ts]
                    kb.dma("pool", dst, q2[j][:], reads=(b_q2[j],), writes=(bP1,), sem=b_q2[j])
                for ci in range(16):
                    p, bp = fm_mm(P_HZ + ci * 128)
                    o, bo = nob()
                    eng = "act" if ci % 2 else "dve"
                    if eng == "dve":
                        kb.op("dve", lambda e: e.tensor_copy(out=o[:, 0:TT], in_=p[:]), reads=(bp,), writes=(bo,))
                    else:
                        kb.op("act", lambda e: e.copy(out=o[:, 0:TT], in_=p[:]), reads=(bp,), writes=(bo,))
                    if ci < 12:
                        dst = self.HZT[sn][ci, :, ts]
                    elif ci < 14:
                        dst = self.GQT[sn][ci - 12, :, ts]
                    else:
                        dst = self.GKT[sn][ci - 14, :, ts]
                    kb.dma("pool", dst, o[:, 0:TT], reads=(bo,), writes=(bP1,), sem=bo)
                for d in range(2):
                    p, bp = fm_mm(P_LO + d * 16, M=16)
                    o, bo = nob()
                    kb.op("dve", lambda e: e.tensor_copy(out=o[0:16, 0:TT], in_=p[0:16, :]), reads=(bp,), writes=(bo,))
                    kb.dma("pool", self.LOT[sn][d, :, ts], o[0:16, 0:TT], reads=(bo,), writes=(bP1,), sem=bo)
                pv, bpv = nps()
                for sb in range(4):
                    for k in range(8):
                        kb.op("pe", lambda e: e.matmul(pv[:, sb * 128:(sb + 1) * 128], lhsT=hT[:, k, sb * 128:(sb + 1) * 128],
                                                       rhs=w[:, k, P_V:P_V + 128], start=(k == 0), stop=(k == 7)),
                              reads=(b_w, b_hT), writes=(bpv,))
                kb.op("dve", lambda e: e.tensor_copy(out=va[i][:, :, :, 0:64], in_=pv[:].rearrange("p (s g d) -> p s g d", s=4, g=2)),
                      reads=(bpv,), writes=(b_va[i],))
                kb.dma("pool", self.VA[sn][t * 4:(t + 1) * 4].rearrange("s p g d -> p s g d"), va[i][:], reads=(b_va[i],),
                       writes=(bP1,), sem=b_va[i])
                for sb in range(4):
                    r0 = t * TT + sb * 128
                    for which in range(2):
                        col0 = P_GV if which == 0 else P_GOG
                        p, bp = nps()
                        for k in range(8):
                            kb.op("pe", lambda e: e.matmul(p[:], lhsT=hT[:, k, sb * 128:(sb + 1) * 128], rhs=w[:, k, col0:col0 + 512],
                                                           start=(k == 0), stop=(k == 7)), reads=(b_w, b_hT), writes=(bp,))
                        o, bo = nob()
                        if which == 0:
                            ovb = o[:].bitcast(BF16)[:, 0:512]
                            kb.op("act", lambda e: e.copy(out=ovb, in_=p[:]), reads=(bp,), writes=(bo,))
                            kb.dma("pool", self.GV[sn][r0:r0 + 128, :], ovb, reads=(bo,), writes=(bP1,), sem=bo)
                        else:
                            kb.op("dve", lambda e: e.tensor_copy(out=o[:, 0:512], in_=p[:]), reads=(bp,), writes=(bo,))
                            kb.dma("pool", self.GOG[sn][r0:r0 + 128, :], o[:, 0:512], reads=(bo,), writes=(bP1,), sem=bo)
        self.full_barrier()


    def phase_attn(self, seq):
        nc, kb, c = self.nc, self.kb, self.c
        L, sn = seq.L, seq.name
        bP1, bYT = self.b_P1[sn], self.b_YT[sn]
        NKB, NQG = L // 128, L // TT
        with ExitStack() as es:
            S = lambda n, shp, dt: es.enter_context(self.sbt("pa_" + n, shp, dt))
            kt = S("kt", [128, L], BF16)
            vg = S("vg", [128, NKB, 128], BF16)
            qt = [S(f"qt{i}", [128, TT], BF16) for i in range(2)]
            NP = 4
            pT = [S(f"pT{i}", [128, TT], BF16) for i in range(NP)]
            osb = [S(f"osb{i}", [128, TT], F32) for i in range(2)]
            den = [S(f"den{i}", [128, TT], F32) for i in range(2)]
            yo = [S(f"yo{i}", [128, TT], BF16) for i in range(2)]
            ps_s = [es.enter_context(self.pst(f"pa_s{i}", [128, 512], F32)) for i in range(4)]
            ps_o = [es.enter_context(self.pst(f"pa_o{i}", [128, 512], F32)) for i in range(2)]
            b_kt, b_vg = kb.buf("pa_kt", dma=True), kb.buf("pa_vg", dma=True)
            b_qt = [kb.buf(f"pa_qt{i}", dma=True) for i in range(2)]
            b_pT = [kb.buf(f"pa_pT{i}") for i in range(NP)]
            b_osb = [kb.buf(f"pa_osb{i}", dma=True) for i in range(2)]
            b_den = [kb.buf(f"pa_den{i}", dma=True) for i in range(2)]
            b_yo = [kb.buf(f"pa_yo{i}", dma=True) for i in range(2)]
            b_s = [kb.buf(f"pa_s{i}") for i in range(4)]
            b_o = [kb.buf(f"pa_o{i}") for i in range(2)]
            it = 0
            qi = 0
            for g in range(2):
                kb.dma("sp", kt[:], self.KT[sn][g], reads=(bP1,), writes=(b_kt,), sem=b_kt)
                kb.dma("sp", vg[:], self.VA[sn][:, :, g, :].rearrange("b p d -> p b d"), reads=(bP1,), writes=(b_vg,), sem=b_vg)
                for hh in range(4):
                    h = 4 * g + hh
                    ch, base = h // 2, 64 * (h % 2)
                    rows = slice(base, base + 64)
                    for qg in range(NQG):
                        qs = slice(qg * TT, (qg + 1) * TT)
                        j = qi % 2
                        qi += 1
                        kb.dma("sp", qt[j][rows, :], self.QT[sn][ch, rows, qs], reads=(bP1,), writes=(b_qt[j],), sem=b_qt[j])
                        po, bpo = ps_o[j], b_o[j]
                        for kbk in range(NKB):
                            s_i = it % 4
                            p_i = it % NP
                            it += 1
                            kb.op("pe", lambda e: e.matmul(ps_s[s_i][:], lhsT=kt[rows, kbk * 128:(kbk + 1) * 128], rhs=qt[j][rows, :],
                                                           start=True, stop=True), reads=(b_kt, b_qt[j]), writes=(b_s[s_i],))
                            kb.op("act", lambda e: e.activation(out=pT[p_i][:], in_=ps_s[s_i][:], func=AF.Exp, scale=HD ** -0.5),
                                  reads=(b_s[s_i],), writes=(b_pT[p_i],))
                            kb.op("pe", lambda e: e.matmul(po[:], lhsT=vg[:, kbk, :], rhs=pT[p_i][:], start=(kbk == 0), stop=(kbk == NKB - 1)),
                                  reads=(b_vg, b_pT[p_i]), writes=(bpo,))
                        kb.op("dve", lambda e: e.tensor_copy(out=osb[j][:], in_=po[:]), reads=(bpo,), writes=(b_osb[j],))
                        kb.dma("sp", den[j][0:64, :], osb[j][64:128, :], reads=(b_osb[j],), writes=(b_den[j],), sem=b_den[j])
                        kb.op("dve", lambda e: e.reciprocal(out=den[j][0:64, :], in_=den[j][0:64, :]), reads=(b_den[j],), writes=(b_den[j],))
                        kb.op("dve", lambda e: e.tensor_tensor(out=yo[j][0:64, :], in0=osb[j][0:64, :], in1=den[j][0:64, :], op=ALU.mult),
                              reads=(b_osb[j], b_den[j]), writes=(b_yo[j],))
                        kb.dma("pool", self.YT[sn][ch, rows, qs], yo[j][0:64, :], reads=(b_yo[j],), writes=(bYT,), sem=b_yo[j])
        self.full_barrier()


    def hy_cfg(self, L):
        NB = L // 128
        N1 = 2 * NB
        K1 = min(N1, 128)
        KC = N1 // K1
        G = 512 // N1
        return NB, N1, K1, KC, G

    def phase_hyena(self, seq, l):
        nc, kb, c = self.nc, self.kb, self.c
        L, sn = seq.L, seq.name
        NB, N1, K1, KC, G = self.hy_cfg(L)
        KB1 = min(NB, 128)
        cpb = 256 // N1
        ext = self.ext
        bP1, bYT = self.b_P1[sn], self.b_YT[sn]
        KFT, KFH, UD, X2R, X2S = self.KFT[sn], self.KFH[sn], self.UD[sn], self.X2R[sn], self.X2S[sn]
        bH = kb.buf("hy_dram" + sn)
        PI = math.pi
        with ExitStack() as es:
            S = lambda n, shp, dt: es.enter_context(self.sbt("ph_" + n, shp, dt))
            P = lambda n: es.enter_context(self.pst("ph_" + n, [128, 512], F32))
            bc = kb.buf("ph_c", dma=True)
            def ldc(name, shape, dt, src, q="sp"):
                t = S(name, shape, dt)
                kb.dma(q, t[:], src, writes=(bc,), sem=bc, allow_slow_non_contiguous=True)
                return t
            w1 = ldc("w1", [33, 64], F32, ext["hy_w1"][l])
            w2 = ldc("w2", [64, 64], F32, ext["hy_w2"][l])
            w3 = ldc("w3", [64, 1024], F32, ext["hy_w3"][l])
            pv = S("pv", [64, 4], F32)
            for j, nm in enumerate(("hy_b1", "hy_f1", "hy_b2", "hy_f2")):
                kb.dma("sp", pv[:, j:j + 1], ext[nm].ap()[l:l + 1, :].rearrange("o d -> d o"), writes=(bc,), sem=bc,
                       allow_slow_non_contiguous=True)
            ndel = ldc("ndel", [128, 4], F32, ext["c_ndelta"][:, :])
            skp = ldc("skp", [128, 4], F32, ext["hy_skip"].ap()[l].rearrange("(k p) -> p k", p=128))
            cw = ldc("cw", [128, 3, 12], F32, ext["hy_conv_w"].ap()[l].rearrange("j (k p) -> p j k", p=128))
            cb = ldc("cb", [128, 12], F32, ext["hy_conv_b"].ap()[l].rearrange("(k p) -> p k", p=128))
            fb = S("fb", [64, 2], F32)
            negpi = S("negpi", [128, 1], F32)
            b_fb = kb.buf("fb")
            kb.op("dve", lambda e: e.memset(negpi[:], -PI), writes=(b_fb,))
            kb.op("dve", lambda e: e.tensor_tensor(out=fb[:, 0:1], in0=pv[:, 0:1], in1=pv[:, 1:2], op=ALU.mult), reads=(bc,), writes=(b_fb,))
            kb.op("dve", lambda e: e.tensor_tensor(out=fb[:, 1:2], in0=pv[:, 2:3], in1=pv[:, 3:4], op=ALU.mult), reads=(bc, b_fb), writes=(b_fb,))
            kb.op("dve", lambda e: e.tensor_scalar_add(out=fb[:], in0=fb[:], scalar1=17.0 * PI), reads=(b_fb,), writes=(b_fb,))
            ps = [P(f"ps{i}") for i in range(8)]
            b_ps = [kb.buf(f"ph_ps{i}") for i in range(8)]
            NT = 2 * L // TT
            ft = [S(f"ft{i}", [33, TT], F32) for i in range(2)]
            tn = [S(f"tn{i}", [128, TT], F32) for i in range(2)]
            b_ft = [kb.buf(f"ph_ft{i}", dma=True) for i in range(2)]
            hh = [S(f"hh{i}", [64, TT], F32) for i in range(2)]
            b_hh = [kb.buf(f"hh{i}") for i in range(2)]
            ki = S("ki", [64, TT], mybir.dt.int32); b_ki = kb.buf("ki")
            kff = S("kff", [64, TT], F32); b_kff = kb.buf("kff")
            dec = S("dec", [128, TT], F32); b_dec = kb.buf("dec")
            kf = S("kf", [128, TT], F32); b_kf = kb.buf("kf")
            kfb = [S(f"kfb{i}", [128, TT], BF16) for i in range(2)]
            b_kfb = [kb.buf(f"ph_kfb{i}", dma=True) for i in range(2)]
            ssq = S("ssq", [128, 4, NT], F32); b_ssq = kb.buf("ssq")
            kb.op("dve", lambda e: e.memset(ssq[:], 0.0), writes=(b_ssq,))
            featsT, tnb = ext["c_feats_" + sn], ext["c_tnb_" + sn]
            it = 0
            for t in range(NT):
                i = t % 2
                ts = slice(t * TT, (t + 1) * TT)
                kb.dma("sp", ft[i][:], featsT[:, ts], writes=(b_ft[i],), sem=b_ft[i])
                kb.dma("sp", tn[i][:], tnb[:, ts], writes=(b_ft[i],), sem=b_ft[i])
                src, bsrc, wm, K = ft[i], b_ft[i], w1, 33
                for layer in range(2):
                    kb.op("pe", lambda e: e.matmul(ps[0][0:64, :], lhsT=wm[0:K, :], rhs=src[0:K, :], start=True, stop=True),
                          reads=(bsrc, bc), writes=(b_ps[0],))
                    h, bh = hh[layer], b_hh[layer]
                    fcol = pv[:, 1:2] if layer == 0 else pv[:, 3:4]
                    kb.op("dve", lambda e: e.tensor_scalar(out=h[:], in0=ps[0][0:64, :], scalar1=fcol, scalar2=fb[:, layer:layer + 1],
                                                           op0=ALU.mult, op1=ALU.add), reads=(b_ps[0], bc, b_fb), writes=(bh,))
                    kb.op("dve", lambda e: e.tensor_scalar(out=ki[:], in0=h[:], scalar1=1.0 / (2.0 * PI), scalar2=None, op0=ALU.mult),
                          reads=(bh,), writes=(b_ki,))
                    kb.op("dve", lambda e: e.tensor_copy(out=kff[:], in_=ki[:]), reads=(b_ki,), writes=(b_kff,))
                    kb.op("dve", lambda e: e.scalar_tensor_tensor(out=h[:], in0=kff[:], scalar=-2.0 * PI, in1=h[:], op0=ALU.mult, op1=ALU.add),
                          reads=(b_kff, bh), writes=(bh,))
                    kb.op("dve", lambda e: e.tensor_scalar(out=kff[:], in0=h[:], scalar1=PI, scalar2=-2.0 * PI, op0=ALU.is_gt, op1=ALU.mult),
                          reads=(bh,), writes=(b_kff,))
                    kb.op("dve", lambda e: e.tensor_tensor(out=h[:], in0=h[:], in1=kff[:], op=ALU.add), reads=(bh, b_kff), writes=(bh,))
                    kb.op("act", lambda e: e.activation(out=h[:], in_=h[:], func=AF.Sin, scale=-1.0), reads=(bh,), writes=(bh,))
                    src, bsrc, wm, K = h, bh, w2, 64
                half = 0 if t * TT < L else 1
                for cc in range(4):
                    pb, bpb = ps[1 + cc % 2], b_ps[1 + cc % 2]
                    col0 = half * 512 + cc * 128
                    kb.op("pe", lambda e: e.matmul(pb[:], lhsT=w3[:, col0:col0 + 128], rhs=hh[1][:], start=True, stop=True),
                          reads=(b_hh[1], bc), writes=(bpb,))
                    kb.op("act", lambda e: e.activation(out=dec[:], in_=tn[i][:], func=AF.Exp, scale=ndel[:, cc:cc + 1]),
                          reads=(b_ft[i], bc), writes=(b_dec,))
                    kb.op("dve", lambda e: e.tensor_tensor(out=kf[:], in0=pb[:], in1=dec[:], op=ALU.mult), reads=(bpb, b_dec), writes=(b_kf,))
                    j = it % 2
                    it += 1
                    kb.op("act", lambda e: e.copy(out=kfb[j][:], in_=kf[:]), reads=(b_kf,), writes=(b_kfb[j],))
                    kb.op("act", lambda e: e.activation(out=dec[:], in_=kf[:], func=AF.Square), reads=(b_kf,), writes=(b_dec,))
                    kb.op("dve", lambda e: e.reduce_sum(out=ssq[:, cc, t:t + 1], in_=dec[:], axis=AX.X), reads=(b_dec,), writes=(b_ssq,))
                    kb.dma("pool", KFT[cc, :, ts], kfb[j][:], reads=(b_kfb[j],), writes=(bH,), sem=b_kfb[j])
            rn = S("rn", [128, 4], F32); b_rn = kb.buf("rn")
            kb.op("dve", lambda e: e.reduce_sum(out=rn[:], in_=ssq[:], axis=AX.X), reads=(b_ssq,), writes=(b_rn,))
            kb.op("dve", lambda e: e.tensor_scalar_add(out=rn[:], in0=rn[:], scalar1=EPS), reads=(b_rn,), writes=(b_rn,))
            kb.op("act", lambda e: e.activation(out=rn[:], in_=rn[:], func=AF.Ln), reads=(b_rn,), writes=(b_rn,))
            kb.op("act", lambda e: e.activation(out=rn[:], in_=rn[:], func=AF.Exp, scale=-0.5), reads=(b_rn,), writes=(b_rn,))
            zt = [S(f"zt{i}", [128, 3, TT + 2], F32) for i in range(2)]
            b_zt = [kb.buf(f"ph_zt{i}", dma=True) for i in range(2)]
            zc = S("zc", [128, 3, TT], F32); b_zc = kb.buf("zc")
            ub = [S(f"ub{i}", [128, TT], BF16) for i in range(2)]
            xr = [S(f"xr{i}", [128, 2, TT], F32) for i in range(2)]
            b_ub = [kb.buf(f"ph_ub{i}", dma=True) for i in range(2)]
            b_xr = [kb.buf(f"ph_xr{i}", dma=True) for i in range(2)]
            HZT = self.HZT[sn]
            it = 0
            for cc in range(4):
                for t in range(L // TT):
                    i = it % 2
                    it += 1
                    t0 = t * TT
                    lo, hi = max(t0 - 1, 0), min(t0 + TT + 1, L)
                    if t0 == 0 or t0 + TT == L:
                        kb.op("dve", lambda e: e.memset(zt[i][:], 0.0), writes=(b_zt[i],))
                    for r in range(3):
                        kb.dma("sp", zt[i][:, r, lo - (t0 - 1):hi - (t0 - 1)], HZT[r * 4 + cc, :, lo:hi], reads=(bP1,), writes=(b_zt[i],), sem=b_zt[i])
                    for r in range(3):
                        ch = r * 4 + cc
                        kb.op("dve", lambda e: e.tensor_scalar(out=zc[:, r, :], in0=zt[i][:, r, 1:TT + 1], scalar1=cw[:, 1, ch:ch + 1],
                                                               scalar2=cb[:, ch:ch + 1], op0=ALU.mult, op1=ALU.add),
                              reads=(b_zt[i], bc), writes=(b_zc,))
                        kb.op("dve", lambda e: e.scalar_tensor_tensor(out=zc[:, r, :], in0=zt[i][:, r, 0:TT], scalar=cw[:, 0, ch:ch + 1],
                                                                      in1=zc[:, r, :], op0=ALU.mult, op1=ALU.add),
                              reads=(b_zt[i], bc, b_zc), writes=(b_zc,))
                        kb.op("dve", lambda e: e.scalar_tensor_tensor(out=zc[:, r, :], in0=zt[i][:, r, 2:TT + 2], scalar=cw[:, 2, ch:ch + 1],
                                                                      in1=zc[:, r, :], op0=ALU.mult, op1=ALU.add),
                              reads=(b_zt[i], bc, b_zc), writes=(b_zc,))
                    kb.op("dve", lambda e: e.tensor_tensor(out=ub[i][:], in0=zc[:, 0, :], in1=zc[:, 1, :], op=ALU.mult), reads=(b_zc,), writes=(b_ub[i],))
                    kb.op("dve", lambda e: e.tensor_scalar_mul(out=xr[i][:, 0, :], in0=zc[:, 2, :], scalar1=rn[:, cc:cc + 1]),
                          reads=(b_zc, b_rn), writes=(b_xr[i],))
                    kb.op("dve", lambda e: e.tensor_scalar_mul(out=xr[i][:, 1, :], in0=zc[:, 2, :], scalar1=skp[:, cc:cc + 1]),
                          reads=(b_zc, bc), writes=(b_xr[i],))
                    tsl = slice(t0, t0 + TT)
                    kb.dma("pool", UD[cc, :, tsl], ub[i][:], reads=(b_ub[i],), writes=(bH,), sem=b_ub[i])
                    kb.dma("pool", X2R[cc, :, tsl], xr[i][:, 0, :], reads=(b_xr[i],), writes=(bH,), sem=b_xr[i])
                    kb.dma("pool", X2S[cc, :, tsl], xr[i][:, 1, :], reads=(b_xr[i],), writes=(bH,), sem=b_xr[i])
        self.full_barrier()
        with ExitStack() as es:
            S = lambda n, shp, dt: es.enter_context(self.sbt("pf_" + n, shp, dt))
            P = lambda n: es.enter_context(self.pst("pf_" + n, [128, 512], F32))
            bc = kb.buf("pf_c", dma=True)
            def ldc(name, shape, dt, src, q="pool"):
                t = S(name, shape, dt)
                kb.dma(q, t[:], src, writes=(bc,), sem=bc, allow_slow_non_contiguous=True)
                return t
            F1 = ldc("F1", [K1, KC, 2 * N1], BF16, ext["c_F1_" + sn].ap().rearrange("(kc p) n -> p kc n", p=K1))
            TW = ldc("TW", [128, 2, 256], F32, ext["c_TW_" + sn].ap().rearrange("r p n -> p r n"), q="sp")
            F2 = ldc("F2", [128, 3, 128], BF16, ext["c_F2"].ap().rearrange("r p n -> p r n"))
            GA = ldc("GA", [128, 2, 256], BF16, ext["c_GA"].ap().rearrange("r p n -> p r n"))
            T2 = S("T2", [K1, 2, KC, 128], F32)
            G1 = S("G1", [K1, 2, KC, NB], BF16)
            for r in range(2):
                kb.dma("sp", T2[:, r, :, :], ext["c_T2_" + sn][r].rearrange("(kc p) n -> p kc n", p=K1), writes=(bc,), sem=bc)
                kb.dma("pool", G1[:, r, :, :], ext["c_G1_" + sn][r].rearrange("(kc p) n -> p kc n", p=K1), writes=(bc,), sem=bc)
            psA = [P(f"A{i}") for i in range(2)]
            psX = [P(f"X{i}") for i in range(2)]
            psC = [P(f"C{i}") for i in range(2)]
            psY = [P(f"Y{i}") for i in range(2)]
            b_A, b_X, b_C, b_Y = ([kb.buf(f"pf_{n}{i}") for i in range(2)] for n in "AXCY")
            dat = [S(f"dat{i}", [K1, KC, G, 128], BF16) for i in range(2)]
            b_dat = [kb.buf(f"pf_dat{i}", dma=True) for i in range(2)]
            t1, t2 = S("t1", [128, 512], F32), S("t2", [128, 512], F32)
            b_t1, b_t2 = kb.buf("t1"), kb.buf("t2")
            Bt = S("Bt", [128, 2, 512], BF16); b_B = kb.buf("B")
            Kf = [S(f"Kf{i}", [128, 2, 512], BF16) for i in range(2)]
            b_Kf = [kb.buf(f"pf_Kf{i}", dma=True) for i in range(2)]
            Xo = [S(f"Xo{i}", [128, 2, 512], BF16) for i in range(2)]
            b_Xo = [kb.buf(f"pf_Xo{i}", dma=True) for i in range(2)]
            Z = S("Z", [128, 2, 512], BF16); b_Z = kb.buf("Z")
            Dt = S("Dt", [K1, 2, KC, G * 128], BF16); b_D = kb.buf("D")
            x2 = [S(f"x2{i}", [KB1, 2, G, 128], F32) for i in range(2)]
            b_x2 = [kb.buf(f"pf_x2{i}", dma=True) for i in range(2)]
            yo = [S(f"yo{i}", [KB1, G, 128], BF16) for i in range(2)]
            b_yo = [kb.buf(f"pf_yo{i}", dma=True) for i in range(2)]

            def fwd(d, bd, Kp, kcs):
                for g in range(G):
                    bank, off = g // cpb, (g % cpb) * 2 * N1
                    for kc in range(kcs):
                        kb.op("pe", lambda e: e.matmul(psA[bank][:, off:off + 2 * N1], lhsT=d[0:Kp, kc, g, :], rhs=F1[0:Kp, kc, :],
                                                       start=(kc == 0), stop=(kc == kcs - 1)), reads=(bd, bc), writes=(b_A[bank],))
                for bank in range(2):
                    Av = psA[bank][:].rearrange("p (c r k) -> p c r k", r=2, k=N1)
                    Are, Aim = Av[:, :, 0, :], Av[:, :, 1, :]
                    twr = TW[:, 0, :].rearrange("p (c k) -> p c k", k=N1)
                    twi = TW[:, 1, :].rearrange("p (c k) -> p c k", k=N1)
                    v = lambda tt: tt[:, 0:256].rearrange("p (c k) -> p c k", k=N1)
                    cs = slice(bank * 256, (bank + 1) * 256)
                    kb.op("dve", lambda e: e.tensor_tensor(out=v(t1), in0=Are, in1=twr, op=ALU.mult), reads=(b_A[bank], bc), writes=(b_t1,))
                    kb.op("dve", lambda e: e.tensor_tensor(out=v(t2), in0=Aim, in1=twi, op=ALU.mult), reads=(b_A[bank], bc), writes=(b_t2,))
                    kb.op("dve", lambda e: e.tensor_tensor(out=Bt[:, 0, cs], in0=t1[:, 0:256], in1=t2[:, 0:256], op=ALU.subtract),
                          reads=(b_t1, b_t2), writes=(b_B,))
                    kb.op("dve", lambda e: e.tensor_tensor(out=v(t1), in0=Are, in1=twi, op=ALU.mult), reads=(b_A[bank], bc), writes=(b_t1,))
                    kb.op("dve", lambda e: e.tensor_tensor(out=v(t2), in0=Aim, in1=twr, op=ALU.mult), reads=(b_A[bank], bc), writes=(b_t2,))
                    kb.op("dve", lambda e: e.tensor_tensor(out=Bt[:, 1, cs], in0=t1[:, 0:256], in1=t2[:, 0:256], op=ALU.add),
                          reads=(b_t1, b_t2), writes=(b_B,))
                kb.op("pe", lambda e: e.matmul(psX[0][:], lhsT=F2[:, 0, :], rhs=Bt[:, 0, :], start=True, stop=False), reads=(b_B, bc), writes=(b_X[0],))
                kb.op("pe", lambda e: e.matmul(psX[0][:], lhsT=F2[:, 2, :], rhs=Bt[:, 1, :], start=False, stop=True), reads=(b_B, bc), writes=(b_X[0],))
                kb.op("pe", lambda e: e.matmul(psX[1][:], lhsT=F2[:, 1, :], rhs=Bt[:, 0, :], start=True, stop=False), reads=(b_B, bc), writes=(b_X[1],))
                kb.op("pe", lambda e: e.matmul(psX[1][:], lhsT=F2[:, 0, :], rhs=Bt[:, 1, :], start=False, stop=True), reads=(b_B, bc), writes=(b_X[1],))

            NG = 512 // G
            for gi in range(NG):
                i = gi % 2
                cc, c0 = (gi * G) // 128, (gi * G) % 128
                for kc in range(KC):
                    kb.dma("sp", dat[i][:, kc, :, :], KFT[cc, c0:c0 + G, kc * K1 * 128:(kc + 1) * K1 * 128].rearrange("c (p j) -> p c j", j=128),
                           reads=(bH,), writes=(b_dat[i],), sem=b_dat[i])
                fwd(dat[i], b_dat[i], K1, KC)
                kb.op("act", lambda e: e.copy(out=Xo[i][:, 0, :], in_=psX[0][:]), reads=(b_X[0],), writes=(b_Xo[i],))
                kb.op("act", lambda e: e.copy(out=Xo[i][:, 1, :], in_=psX[1][:]), reads=(b_X[1],), writes=(b_Xo[i],))
                kb.dma("pool", KFH[gi], Xo[i][:], reads=(b_Xo[i],), writes=(bH,), sem=b_Xo[i])
            for gi in range(NG):
                i = gi % 2
                cc, c0 = (gi * G) // 128, (gi * G) % 128
                kb.dma("sp", dat[i][0:KB1, 0, :, :], UD[cc, c0:c0 + G, :].rearrange("c (p j) -> p c j", j=128), reads=(bH,),
                       writes=(b_dat[i],), sem=b_dat[i])
                kb.dma("sp", Kf[i][:], KFH[gi], reads=(bH,), writes=(b_Kf[i],), sem=b_Kf[i])
                kb.dma("sp", x2[i][:, 0, :, :], X2R[cc, c0:c0 + G, :].rearrange("c (p j) -> p c j", j=128), reads=(bH,), writes=(b_x2[i],), sem=b_x2[i])
                kb.dma("sp", x2[i][:, 1, :, :], X2S[cc, c0:c0 + G, :].rearrange("c (p j) -> p c j", j=128), reads=(bH,), writes=(b_x2[i],), sem=b_x2[i])
                fwd(dat[i], b_dat[i], KB1, 1)
                for (o, a, b_, op) in ((0, 0, 0, None), (0, 1, 1, ALU.subtract), (1, 0, 1, None), (1, 1, 0, ALU.add)):
                    tgt, btgt = (t1, b_t1) if op is None else (t2, b_t2)
                    kb.op("dve", lambda e: e.tensor_tensor(out=tgt[:], in0=psX[a][:], in1=Kf[i][:, b_, :], op=ALU.mult),
                          reads=(b_X[a], b_Kf[i]), writes=(btgt,))
                    if op is not None:
                        kb.op("dve", lambda e: e.tensor_tensor(out=Z[:, o, :], in0=t1[:], in1=t2[:], op=op), reads=(b_t1, b_t2), writes=(b_Z,))
                regs = [(g, kc) for g in range(G) for kc in range(KC)]
                for r0 in range(0, len(regs), 2):
                    bk = (r0 // 2) % 2
                    for ri, (g, kc) in enumerate(regs[r0:r0 + 2]):
                        col = g * N1 + kc * K1
                        kb.op("pe", lambda e: e.matmul(psC[bk][0:K1, ri * 256:(ri + 1) * 256], lhsT=Z[:, 0, col:col + K1], rhs=GA[:, 0, :],
                                                       start=True, stop=False), reads=(b_Z, bc), writes=(b_C[bk],))
                        kb.op("pe", lambda e: e.matmul(psC[bk][0:K1, ri * 256:(ri + 1) * 256], lhsT=Z[:, 1, col:col + K1], rhs=GA[:, 1, :],
                                                       start=False, stop=True), reads=(b_Z, bc), writes=(b_C[bk],))
                    for ri, (g, kc) in enumerate(regs[r0:r0 + 2]):
                        Cre = psC[bk][0:K1, ri * 256:ri * 256 + 128]
                        Cim = psC[bk][0:K1, ri * 256 + 128:ri * 256 + 256]
                        tr, ti = T2[:, 0, kc, :], T2[:, 1, kc, :]
                        dsl = slice(g * 128, (g + 1) * 128)
                        kb.op("dve", lambda e: e.tensor_tensor(out=t1[0:K1, 0:128], in0=Cre, in1=tr, op=ALU.mult), reads=(b_C[bk], bc), writes=(b_t1,))
                        kb.op("dve", lambda e: e.tensor_tensor(out=t2[0:K1, 0:128], in0=Cim, in1=ti, op=ALU.mult), reads=(b_C[bk], bc), writes=(b_t2,))
                        kb.op("dve", lambda e: e.tensor_tensor(out=Dt[:, 0, kc, dsl], in0=t1[0:K1, 0:128], in1=t2[0:K1, 0:128], op=ALU.subtract),
                              reads=(b_t1, b_t2), writes=(b_D,))
                        kb.op("dve", lambda e: e.tensor_tensor(out=t1[0:K1, 0:128], in0=Cre, in1=ti, op=ALU.mult), reads=(b_C[bk], bc), writes=(b_t1,))
                        kb.op("dve", lambda e: e.tensor_tensor(out=t2[0:K1, 0:128], in0=Cim, in1=tr, op=ALU.mult), reads=(b_C[bk], bc), writes=(b_t2,))
                        kb.op("dve", lambda e: e.tensor_tensor(out=Dt[:, 1, kc, dsl], in0=t1[0:K1, 0:128], in1=t2[0:K1, 0:128], op=ALU.add),
                              reads=(b_t1, b_t2), writes=(b_D,))
                for q0 in range(0, G * 128, 512):
                    qn = min(512, G * 128 - q0)
                    yb = (q0 // 512) % 2
                    n_mm = 2 * KC
                    mi = 0
                    for kc in range(KC):
                        for r in range(2):
                            kb.op("pe", lambda e: e.matmul(psY[yb][0:KB1, 0:qn], lhsT=G1[:, r, kc, :], rhs=Dt[:, r, kc, q0:q0 + qn],
                                                           start=(mi == 0), stop=(mi == n_mm - 1)), reads=(b_D, bc), writes=(b_Y[yb],))
                            mi += 1
                    g0, gn = q0 // 128, qn // 128
                    xv = lambda r: x2[i][:, r, g0:g0 + gn, :].rearrange("p c j -> p (c j)")
                    uv = dat[i][0:KB1, 0, g0:g0 + gn, :].rearrange("p c j -> p (c j)")
                    kb.op("dve", lambda e: e.tensor_tensor(out=t1[0:KB1, 0:qn], in0=psY[yb][0:KB1, 0:qn], in1=xv(0), op=ALU.mult),
                          reads=(b_Y[yb], b_x2[i]), writes=(b_t1,))
                    kb.op("dve", lambda e: e.tensor_tensor(out=t2[0:KB1, 0:qn], in0=uv, in1=xv(1), op=ALU.mult), reads=(b_dat[i], b_x2[i]), writes=(b_t2,))
                    kb.op("dve", lambda e: e.tensor_tensor(out=yo[i][:, g0:g0 + gn, :].rearrange("p c j -> p (c j)"), in0=t1[0:KB1, 0:qn],
                                                           in1=t2[0:KB1, 0:qn], op=ALU.add), reads=(b_t1, b_t2), writes=(b_yo[i],))
                kb.dma("pool", self.YT[sn][4 + cc, c0:c0 + G, :].rearrange("c (p j) -> p c j", j=128), yo[i][:], reads=(b_yo[i],),
                       writes=(bYT,), sem=b_yo[i])
        self.full_barrier()


    def phase_gla(self, seq, l):
        nc, kb, c = self.nc, self.kb, self.c
        L, sn = seq.L, seq.name
        ext = self.ext
        bP1, bYT = self.b_P1[sn], self.b_YT[sn]
        OF = self.OF[sn]
        bOF = kb.buf("gl_of" + sn)
        NBK = L // 128
        with ExitStack() as es:
            S = lambda n, shp, dt: es.enter_context(self.sbt("pg_" + n, shp, dt))
            P = lambda n, dt=F32: es.enter_context(self.pst("pg_" + n, [128, 512], dt))
            bc = kb.buf("pg_c", dma=True)
            def ldc(name, shape, dt, src, q="sp"):
                t = S(name, shape, dt)
                kb.dma(q, t[:], src, writes=(bc,), sem=bc, allow_slow_non_contiguous=True)
                return t
            tri = ldc("tri", [128, 2, 128], F32, ext["c_tri"].ap().rearrange("r p n -> p r n"))
            msk = ldc("msk", [128, 2, 4, 128], F32, ext["c_msk"].ap().rearrange("r p h n -> p r h n"))
            gng = ldc("gng", [128, 512], F32, ext["c_gng"][l])
            gu = S("gu", [32, 2, 256], F32)
            kb.op("dve", lambda e: e.memset(gu[:], 0.0), writes=(bc,))
            for d in range(2):
                kb.dma("sp", gu[0:16, d, :], ext["gla_gate_up"][l, d], writes=(bc,), sem=bc)
                kb.dma("sp", gu[16:17, d, :], ext["gla_gate_b"][l, d:d + 1, :], writes=(bc,), sem=bc)
            one1 = S("one1", [128, 1], F32)
            kb.op("dve", lambda e: e.memset(one1[:], 1.0), writes=(bc,))
            lo = [S(f"lo{i}", [32, 128], F32) for i in range(2)]
            qk = [S(f"qk{i}", [128, 2, 2, 128], F32) for i in range(2)]
            vt = [S(f"vt{i}", [128, 512], BF16) for i in range(2)]
            b_in = [kb.buf(f"pg_in{i}", dma=True) for i in range(2)]
            for i in range(2):
                kb.op("dve", lambda e: e.memset(lo[i][:], 1.0), writes=(b_in[i],))
            ex = S("ex", [128, 256], F32); b_ex = kb.buf("ex")
            lsp = S("lsp", [128, 256], F32); b_lsp = kb.buf("lsp")
            bsm = S("bsm", [128, 2, 2, 2, 2], F32); b_bsm = kb.buf("bsm")
            eb = S("eb", [128, 2, 2], F32); b_eb = kb.buf("eb")
            E = S("E", [128, 4, 2, 128], F32); b_E = kb.buf("E")
            Dd = S("Dd", [128, 2, 2, 128], F32); b_Dd = kb.buf("Dd")
            qkt = S("qkt", [128, 4, 2, 128], BF16); b_qkt = kb.buf("qkt")
            AT = S("AT", [128, 4, 128], BF16); b_AT = kb.buf("AT")
            qz = S("qz", [128, 2, 2, 128], BF16); b_qz = kb.buf("qz")
            kb.op("dve", lambda e: e.memset(qz[:], 0.0), writes=(b_qz,))
            khat = S("khat", [128, 2, 128], BF16); b_khat = kb.buf("khat")
            St = S("St", [128, 2, 128], F32); b_S = kb.buf("S")
            Sb = S("Sb", [128, 2, 128], BF16); b_Sb = kb.buf("Sb")
            osb = [S(f"osb{i}", [128, 512], F32) for i in range(2)]
            b_osb = [kb.buf(f"pg_osb{i}", dma=True) for i in range(2)]
            ofl = [S(f"ofl{i}", [128, 512], F32) for i in range(2)]
            ogt = [S(f"ogt{i}", [128, 512], F32) for i in range(2)]
            b_ofl = [kb.buf(f"pg_ofl{i}", dma=True) for i in range(2)]
            sq = S("sq", [128, 512], F32); b_sq = kb.buf("sq")
            ss = S("ss", [128, 4], F32); b_ss = kb.buf("ss")
            yb = S("yb", [128, 512], BF16); b_yb = kb.buf("yb")
            yT = [S(f"yT{i}", [128, 4, 128], BF16) for i in range(2)]
            b_yT = [kb.buf(f"pg_yT{i}", dma=True) for i in range(2)]
            ps_l, ps_b, ps_A, ps_O, ps_U = P("l"), P("b"), P("A"), P("O"), P("U")
            ps_T, ps_Y = P("T", BF16), P("Y", BF16)
            b_pl, b_pb, b_pA, b_pO, b_pU, b_pT, b_pY = (kb.buf("pg_p" + n) for n in "lbAOUTY")
            blk_i = 0
            for d in range(2):
                kb.op("dve", lambda e: e.memset(St[:], 0.0), writes=(b_S,))
                kb.op("dve", lambda e: e.memset(Sb[:], 0.0), writes=(b_Sb,))
                order = range(NBK) if d == 0 else range(NBK - 1, -1, -1)
                mid, last = (31, 63) if d == 0 else (32, 0)
                for blk in order:
                    i = blk_i % 2
                    blk_i += 1
                    tsl = slice(blk * 128, (blk + 1) * 128)
                    kb.dma("sp", lo[i][0:16, :], self.LOT[sn][d, :, tsl], reads=(bP1,), writes=(b_in[i],), sem=b_in[i])
                    kb.dma("sp", qk[i][:, 0, :, :], self.GQT[sn][:, :, tsl].rearrange("h p n -> p h n"), reads=(bP1,), writes=(b_in[i],), sem=b_in[i])
                    kb.dma("sp", qk[i][:, 1, :, :], self.GKT[sn][:, :, tsl].rearrange("h p n -> p h n"), reads=(bP1,), writes=(b_in[i],), sem=b_in[i])
                    kb.dma("sp", vt[i][:], self.GV[sn][tsl, :], reads=(bP1,), writes=(b_in[i],), sem=b_in[i])
                    if d == 1:
                        kb.dma("sp", ofl[i][:], OF[tsl, :], reads=(bOF,), writes=(b_ofl[i],), sem=b_ofl[i])
                        kb.dma("sp", ogt[i][:], self.GOG[sn][tsl, :], reads=(bP1,), writes=(b_ofl[i],), sem=b_ofl[i])
                    kb.op("pe", lambda e: e.matmul(ps_l[:, 0:256], lhsT=lo[i][:], rhs=gu[:, d, :], start=True, stop=True),
                          reads=(b_in[i], bc), writes=(b_pl,))
                    kb.op("act", lambda e: e.activation(out=ex[:], in_=ps_l[:, 0:256], func=AF.Exp, scale=-1.0), reads=(b_pl,), writes=(b_ex,))
                    kb.op("dve", lambda e: e.tensor_scalar_add(out=ex[:], in0=ex[:], scalar1=1.0), reads=(b_ex,), writes=(b_ex,))
                    kb.op("act", lambda e: e.activation(out=lsp[:], in_=ex[:], func=AF.Ln), reads=(b_ex,), writes=(b_lsp,))
                    for hp in range(2):
                        kb.op("pe", lambda e: e.matmul(ps_b[:, hp * 128:(hp + 1) * 128], lhsT=lsp[:, hp * 128:(hp + 1) * 128], rhs=tri[:, d, :],
                                                       start=True, stop=True), reads=(b_lsp, bc), writes=(b_pb,))
                    bv = ps_b[:, 0:256].rearrange("p (h c n) -> p h c n", h=2, c=2)
                    for w_, col in enumerate((mid, last)):
                        kb.op("dve", lambda e: e.tensor_scalar_mul(out=bsm[:, 0, :, :, w_], in0=bv[:, :, :, col], scalar1=-1.0), reads=(b_pb,), writes=(b_bsm,))
                        kb.op("dve", lambda e: e.tensor_copy(out=bsm[:, 1, :, :, w_], in_=bv[:, :, :, col]), reads=(b_pb,), writes=(b_bsm,))
                    kb.op("act", lambda e: e.activation(out=eb[:], in_=bsm[:, 1, :, :, 1], func=AF.Exp), reads=(b_bsm,), writes=(b_eb,))
                    for hp in range(2):
                        for ci in range(2):
                            cs = slice(ci * 64, (ci + 1) * 64)
                            src = ps_b[:, hp * 128 + ci * 64:hp * 128 + (ci + 1) * 64]
                            for w_ in range(2):
                                kb.op("dve", lambda e: e.tensor_scalar(out=Dd[:, w_, hp, cs], in0=src, scalar1=bsm[:, 0, hp, ci, w_:w_ + 1],
                                                                       scalar2=None, op0=ALU.add), reads=(b_pb, b_bsm), writes=(b_Dd,))
                    pbv = ps_b[:, 0:256].rearrange("p (h n) -> p h n", h=2)
                    kb.op("act", lambda e: e.activation(out=E[:, 0, :, :], in_=Dd[:, 0, :, :], func=AF.Exp), reads=(b_Dd,), writes=(b_E,))
                    kb.op("act", lambda e: e.activation(out=E[:, 1, :, :], in_=Dd[:, 0, :, :], func=AF.Exp, scale=-1.0), reads=(b_Dd,), writes=(b_E,))
                    kb.op("act", lambda e: e.activation(out=E[:, 2, :, :], in_=pbv, func=AF.Exp), reads=(b_pb,), writes=(b_E,))
                    kb.op("act", lambda e: e.activation(out=E[:, 3, :, :], in_=Dd[:, 1, :, :], func=AF.Exp, scale=-1.0), reads=(b_Dd,), writes=(b_E,))
                    for ci in range(2):
                        cs = slice(ci * 64, (ci + 1) * 64)
                        kb.op("dve", lambda e: e.scalar_tensor_tensor(out=qz[:, ci, :, cs], in0=qk[i][:, 0, :, cs], scalar=0.125, in1=E[:, 2, :, cs],
                                                                      op0=ALU.mult, op1=ALU.mult), reads=(b_in[i], b_E), writes=(b_qz,))
                    for which in (0, 1, 3):
                        src = qk[i][:, which % 2, :, :]
                        if which % 2 == 0:
                            kb.op("dve", lambda e: e.scalar_tensor_tensor(out=qkt[:, which, :, :], in0=src, scalar=0.125, in1=E[:, which, :, :],
                                                                          op0=ALU.mult, op1=ALU.mult), reads=(b_in[i], b_E), writes=(b_qkt,))
                        else:
                            kb.op("dve", lambda e: e.tensor_tensor(out=qkt[:, which, :, :], in0=src, in1=E[:, which, :, :], op=ALU.mult),
                                  reads=(b_in[i], b_E), writes=(b_qkt,))
                    if "DBG1" in self.taps and d == 0 and blk == 0 and sn == "p":
                        bdbg = kb.buf("dbg", dma=True)
                        kb.dma("sp", self.dbg["DBG1"][:, :], lsp[:], reads=(b_lsp,), writes=(), sem=bdbg)
                        kb.dma("sp", self.dbg["DBG2"][:, :], E[:].rearrange("p a b c -> p (a b c)"), reads=(b_E,), writes=(), sem=bdbg)
                        kb.dma("sp", self.dbg["DBG3"][:, :], qkt[:].rearrange("p a b c -> p (a b c)"), reads=(b_qkt,), writes=(), sem=bdbg)
                    for h in range(4):
                        hp, e_ = h // 2, h % 2
                        rows = slice(64 * e_, 64 * e_ + 64)
                        kb.op("pe", lambda e: e.matmul(ps_A[:, h * 128:(h + 1) * 128], lhsT=qkt[rows, 1, hp, :], rhs=qkt[rows, 0, hp, :],
                                                       start=True, stop=True), reads=(b_qkt,), writes=(b_pA,))
                    kb.op("dve", lambda e: e.tensor_tensor(out=AT[:].rearrange("p h n -> p (h n)"), in0=ps_A[:],
                                                           in1=msk[:, d, :, :].rearrange("p h n -> p (h n)"), op=ALU.mult),
                          reads=(b_pA, bc), writes=(b_AT,))
                    if "DBG1" in self.taps and d == 0 and blk == 0 and sn == "p":
                        kb.dma("sp", self.dbg["DBG4"][:, :], AT[:].rearrange("p a b -> p (a b)"), reads=(b_AT,), writes=(), sem=bdbg)
                    for hp in range(2):
                        kb.op("pe", lambda e: e.transpose(out=ps_T[:, hp * 128:(hp + 1) * 128], in_=qkt[:, 3, hp, :], identity=c["ident_b"][:]),
                              reads=(b_qkt, self.c_sem), writes=(b_pT,))
                    kb.op("act", lambda e: e.copy(out=khat[:].rearrange("p h n -> p (h n)"), in_=ps_T[:, 0:256]), reads=(b_pT,), writes=(b_khat,))
                    for h in range(4):
                        kb.op("pe", lambda e: e.matmul(ps_O[:, h * 128:(h + 1) * 128], lhsT=AT[:, h, :], rhs=vt[i][:, h * 128:(h + 1) * 128],
                                                       start=(h == 0), stop=False), reads=(b_AT, b_in[i]), writes=(b_pO,))
                    for cidx, ci in enumerate((0, 1) if d == 0 else (1, 0)):
                        crow = slice(ci * 64, (ci + 1) * 64)
                        for h in range(4):
                            hp, e_ = h // 2, h % 2
                            rows = slice(64 * e_, 64 * e_ + 64)
                            kb.op("pe", lambda e: e.matmul(ps_O[:, h * 128:(h + 1) * 128], lhsT=qz[rows, ci, hp, :], rhs=Sb[rows, hp, :],
                                                           start=False, stop=(cidx == 1)), reads=(b_qz, b_Sb), writes=(b_pO,))
                        for h in range(4):
                            hp = h // 2
                            kb.op("pe", lambda e: e.matmul(ps_U[:, h * 128:(h + 1) * 128], lhsT=khat[crow, hp, :], rhs=vt[i][crow, h * 128:(h + 1) * 128],
                                                           start=True, stop=True), reads=(b_khat, b_in[i]), writes=(b_pU,))
                        for h in range(4):
                            hp, e_ = h // 2, h % 2
                            rows = slice(64 * e_, 64 * e_ + 64)
                            kb.op("dve", lambda e: e.scalar_tensor_tensor(out=St[rows, hp, :], in0=St[rows, hp, :], scalar=eb[rows, hp, ci:ci + 1],
                                                                          in1=ps_U[rows, h * 128:(h + 1) * 128], op0=ALU.mult, op1=ALU.add),
                                  reads=(b_pU, b_eb, b_S), writes=(b_S,))
                        kb.op("act", lambda e: e.copy(out=Sb[:], in_=St[:]), reads=(b_S,), writes=(b_Sb,))
                    if d == 0:
                        kb.op("dve", lambda e: e.tensor_copy(out=osb[i][:], in_=ps_O[:]), reads=(b_pO,), writes=(b_osb[i],))
                        kb.dma("pool", OF[tsl, :], osb[i][:], reads=(b_osb[i],), writes=(bOF,), sem=b_osb[i])
                    else:
                        o = osb[i]
                        kb.op("dve", lambda e: e.tensor_tensor(out=o[:], in0=ps_O[:], in1=ofl[i][:], op=ALU.add), reads=(b_pO, b_ofl[i]), writes=(b_osb[i],))
                        kb.op("act", lambda e: e.activation(out=sq[:], in_=o[:], func=AF.Square), reads=(b_osb[i],), writes=(b_sq,))
                        kb.op("dve", lambda e: e.reduce_sum(out=ss[:], in_=sq[:].rearrange("p (h n) -> p h n", h=4), axis=AX.X), reads=(b_sq,), writes=(b_ss,))
                        kb.op("dve", lambda e: e.tensor_scalar(out=ss[:], in0=ss[:], scalar1=1.0 / 128, scalar2=EPS, op0=ALU.mult, op1=ALU.add),
                              reads=(b_ss,), writes=(b_ss,))
                        kb.op("act", lambda e: e.activation(out=ss[:], in_=ss[:], func=AF.Ln), reads=(b_ss,), writes=(b_ss,))
                        kb.op("act", lambda e: e.activation(out=ss[:], in_=ss[:], func=AF.Exp, scale=-0.5), reads=(b_ss,), writes=(b_ss,))
                        kb.op("act", lambda e: e.activation(out=sq[:], in_=ogt[i][:], func=AF.Silu), reads=(b_ofl[i], b_sq), writes=(b_sq,))
                        kb.op("dve", lambda e: e.tensor_tensor(out=sq[:], in0=sq[:], in1=gng[:], op=ALU.mult), reads=(b_sq, bc), writes=(b_sq,))
                        for h in range(4):
                            hs = slice(h * 128, (h + 1) * 128)
                            kb.op("dve", lambda e: e.scalar_tensor_tensor(out=yb[:, hs], in0=o[:, hs], scalar=ss[:, h:h + 1], in1=sq[:, hs],
                                                                          op0=ALU.mult, op1=ALU.mult), reads=(b_osb[i], b_ss, b_sq), writes=(b_yb,))
                        for h in range(4):
                            kb.op("pe", lambda e: e.transpose(out=ps_Y[:, h * 128:(h + 1) * 128], in_=yb[:, h * 128:(h + 1) * 128], identity=c["ident_b"][:]),
                                  reads=(b_yb, self.c_sem), writes=(b_pY,))
                        kb.op("act", lambda e: e.copy(out=yT[i][:].rearrange("p h n -> p (h n)"), in_=ps_Y[:]), reads=(b_pY,), writes=(b_yT[i],))
                        kb.dma("pool", self.YT[sn][8:12, :, tsl].rearrange("h p n -> p h n"), yT[i][:], reads=(b_yT[i],), writes=(bYT,), sem=b_yT[i])
        self.full_barrier()

    def phase_dense(self, seq, l):
        nc, kb, c = self.nc, self.kb, self.c
        L = seq.L
        XT, YT = self.XT[seq.name], self.YT[seq.name]
        bXT, bYT = self.b_XT[seq.name], self.b_YT[seq.name]
        wb = self.wb
        with ExitStack() as es:
            S = lambda n, shp, dt: es.enter_context(self.sbt("pd_" + n, shp, dt))
            xt = [S(f"xt{i}", [128, 8, TT], F32) for i in range(2)]
            yt = [S(f"yt{i}", [128, 12, TT], BF16) for i in range(2)]
            hT = S("hT", [128, 8, TT], BF16)
            sq = S("sq", [128, 8, TT], BF16)
            rstd = S("rstd", [128, TT], F32)
            mg = S("mg", [128, 8, TT], F32)
            mgb = S("mgb", [128, 8, TT], BF16)
            act = S("act", [128, NFC, TT], BF16)
            sg = [S(f"sg{i}", [128, TT], F32) for i in range(2)]
            tm = [S(f"tm{i}", [128, TT], F32) for i in range(2)]
            NW = 4
            wsl = [S(f"w{i}", [128, 8, 1024], BF16) for i in range(NW)]
            ps = [es.enter_context(self.pst(f"pd_ps{i}", [128, 512], F32)) for i in range(8)]
            b_xt = [kb.buf(f"pd_xt{i}", dma=True) for i in range(2)]
            b_yt = [kb.buf(f"pd_yt{i}", dma=True) for i in range(2)]
            b_w = [kb.buf(f"pd_w{i}", dma=True) for i in range(NW)]
            b_ps = [kb.buf(f"pd_ps{i}") for i in range(8)]
            b_hT, b_sq, b_rstd, b_mg, b_mgb, b_act = (kb.buf(n) for n in ["hT", "sq", "rstd", "mg", "mgb", "act"])
            b_sg = [kb.buf("sg0"), kb.buf("sg1")]
            b_tm = [kb.buf("tm0"), kb.buf("tm1")]
            st = {"w": 0, "ps": 0}

            def wload(src_ap, shape3):
                i = st["w"] % NW
                st["w"] += 1
                a, b = shape3
                view = wsl[i][:].rearrange("p a b -> p (a b)")[:, 0:a * b].rearrange("p (a b) -> p a b", a=a)
                kb.dma("sp", view, src_ap, reads=(self.b_wb,), writes=(b_w[i],), sem=b_w[i])
                return view, b_w[i]

            def nps():
                i = st["ps"] % 6
                st["ps"] += 1
                return ps[i], b_ps[i]

            ntile = L // TT
            def load_tile(t):
                i = t % 2
                ts = slice(t * TT, (t + 1) * TT)
                kb.dma("sp", xt[i][:], XT[:, :, ts].rearrange("k p n -> p k n"), reads=(bXT,), writes=(b_xt[i],), sem=b_xt[i])
                kb.dma("sp", yt[i][:], YT[:, :, ts].rearrange("k p n -> p k n"), reads=(bYT,), writes=(b_yt[i],), sem=b_yt[i])

            load_tile(0)
            for t in range(ntile):
                i = t % 2
                ts = slice(t * TT, (t + 1) * TT)
                X, bX = xt[i], b_xt[i]
                if t + 1 < ntile:
                    load_tile(t + 1)
                self.rmsnorm_T(X, bX, hT, b_hT, c["g_mix"][:, l, :], sq, b_sq, ps[7], b_ps[7], rstd, b_rstd)
                for n in range(3):
                    wg, bwg = wload(wb["w_gates"][l, :, n * 1024:(n + 1) * 1024].rearrange("(k p) n -> p k n", p=128), (8, 1024))
                    wbr, bwbr = wload(wb["w_branch"][l, n * 512:(n + 1) * 512, :].rearrange("(k p) n -> p k n", p=128), (4, 1024))
                    for dc in range(8):
                        pa, bpa = nps()
                        for k in range(8):
                            kb.op("pe", lambda e: e.matmul(pa[:], lhsT=wg[:, k, dc * 128:(dc + 1) * 128], rhs=hT[:, k, :],
                                                           start=(k == 0), stop=(k == 7)), reads=(bwg, b_hT), writes=(bpa,))
                        pb, bpb = nps()
                        for k in range(4):
                            kb.op("pe", lambda e: e.matmul(pb[:], lhsT=wbr[:, k, dc * 128:(dc + 1) * 128], rhs=yt[i][:, n * 4 + k, :],
                                                           start=(k == 0), stop=(k == 3)), reads=(bwbr, b_yt[i]), writes=(bpb,))
                        j = dc % 2
                        kb.op("act", lambda e: e.activation(out=sg[j][:], in_=pa[:], func=AF.Sigmoid), reads=(bpa,), writes=(b_sg[j],))
                        if n == 0:
                            kb.op("dve", lambda e: e.tensor_tensor(out=mg[:, dc, :], in0=sg[j][:], in1=pb[:], op=ALU.mult),
                                  reads=(b_sg[j], bpb), writes=(b_mg,))
                        else:
                            kb.op("dve", lambda e: e.tensor_tensor(out=tm[j][:], in0=sg[j][:], in1=pb[:], op=ALU.mult),
                                  reads=(b_sg[j], bpb), writes=(b_tm[j],))
                            if n == 1:
                                kb.op("dve", lambda e: e.tensor_tensor(out=mg[:, dc, :], in0=mg[:, dc, :], in1=tm[j][:], op=ALU.add),
                                      reads=(b_tm[j], b_mg), writes=(b_mg,))
                            else:
                                kb.op("dve", lambda e: e.tensor_tensor(out=mgb[:, dc, :], in0=mg[:, dc, :], in1=tm[j][:], op=ALU.add),
                                      reads=(b_tm[j], b_mg), writes=(b_mgb,))
                wo, bwo = wload(wb["w_out"][l].rearrange("(k p) n -> p k n", p=128), (8, 1024))
                for dc in range(8):
                    pa, bpa = nps()
                    for k in range(8):
                        kb.op("pe", lambda e: e.matmul(pa[:], lhsT=wo[:, k, dc * 128:(dc + 1) * 128], rhs=mgb[:, k, :],
                                                       start=(k == 0), stop=(k == 7)), reads=(bwo, b_mgb), writes=(bpa,))
                    kb.op("dve", lambda e: e.tensor_tensor(out=X[:, dc, :], in0=X[:, dc, :], in1=pa[:], op=ALU.add),
                          reads=(bpa, bX), writes=(bX,))
                self.rmsnorm_T(X, bX, hT, b_hT, c["g_ffn"][:, l, :], sq, b_sq, ps[7], b_ps[7], rstd, b_rstd)
                for g in range(0, NFC, 4):
                    nf = min(4, NFC - g)
                    cs = slice(g * 128, (g + nf) * 128)
                    wgt, bwgt = wload(wb["w_ffn_gate"][l, :, cs].rearrange("(k p) n -> p k n", p=128), (8, nf * 128))
                    wup, bwup = wload(wb["w_ffn_up"][l, :, cs].rearrange("(k p) n -> p k n", p=128), (8, nf * 128))
                    for f in range(nf):
                        pa, bpa = nps()
                        for k in range(8):
                            kb.op("pe", lambda e: e.matmul(pa[:], lhsT=wgt[:, k, f * 128:(f + 1) * 128], rhs=hT[:, k, :],
                                                           start=(k == 0), stop=(k == 7)), reads=(bwgt, b_hT), writes=(bpa,))
                        pb, bpb = nps()
                        for k in range(8):
                            kb.op("pe", lambda e: e.matmul(pb[:], lhsT=wup[:, k, f * 128:(f + 1) * 128], rhs=hT[:, k, :],
                                                           start=(k == 0), stop=(k == 7)), reads=(bwup, b_hT), writes=(bpb,))
                        j = f % 2
                        kb.op("act", lambda e: e.activation(out=sg[j][:], in_=pa[:], func=AF.Silu), reads=(bpa,), writes=(b_sg[j],))
                        kb.op("dve", lambda e: e.tensor_tensor(out=act[:, g + f, :], in0=sg[j][:], in1=pb[:], op=ALU.mult),
                              reads=(b_sg[j], bpb), writes=(b_act,))
                pd = []
                for half in range(2):
                    pd = [nps() for _ in range(4)]
                    for g in range(0, NFC, 8):
                        nf = min(8, NFC - g)
                        wd, bwd = wload(wb["w_ffn_down"][l, g * 128:(g + nf) * 128, half * 512:(half + 1) * 512]
                                        .rearrange("(k p) n -> p k n", p=128), (nf, 512))
                        for dq in range(4):
                            for f in range(nf):
                                kb.op("pe", lambda e: e.matmul(pd[dq][0][:], lhsT=wd[:, f, dq * 128:(dq + 1) * 128], rhs=act[:, g + f, :],
                                                               start=(g + f == 0), stop=(g + f == NFC - 1)),
                                      reads=(bwd, b_act), writes=(pd[dq][1],))
                    for dq in range(4):
                        dc = half * 4 + dq
                        kb.op("dve", lambda e: e.tensor_tensor(out=X[:, dc, :], in0=X[:, dc, :], in1=pd[dq][0][:], op=ALU.add),
                              reads=(pd[dq][1], bX), writes=(bX,))
                kb.dma("pool", XT[:, :, ts].rearrange("k p n -> p k n"), X[:], reads=(bX,), writes=(bXT,), sem=bX)
        self.full_barrier()

    def phase_out(self, seq):
        nc, kb, c = self.nc, self.kb, self.c
        L = seq.L
        XT = self.XT[seq.name]
        bXT = self.b_XT[seq.name]
        out = self.out[seq.name]
        with ExitStack() as es:
            S = lambda n, shp, dt: es.enter_context(self.sbt("po_" + n, shp, dt))
            xt = [S(f"xt{i}", [128, 8, TT], F32) for i in range(2)]
            hT = S("hT", [128, 8, TT], F32)
            sq = S("sq", [128, 8, TT], BF16)
            rstd = S("rstd", [128, TT], F32)
            ot = [S(f"ot{i}", [128, D], F32) for i in range(2)]
            ps = [es.enter_context(self.pst(f"po_ps{i}", [128, 512], F32)) for i in range(5)]
            b_xt = [kb.buf(f"po_xt{i}", dma=True) for i in range(2)]
            b_ot = [kb.buf(f"po_ot{i}", dma=True) for i in range(2)]
            b_ps = [kb.buf(f"po_ps{i}") for i in range(5)]
            b_hT, b_sq, b_rstd = kb.buf("hT"), kb.buf("sq"), kb.buf("rstd")
            cnt = 0
            for t in range(L // TT):
                i = t % 2
                ts = slice(t * TT, (t + 1) * TT)
                kb.dma("sp", xt[i][:], XT[:, :, ts].rearrange("k p n -> p k n"), reads=(bXT,), writes=(b_xt[i],), sem=b_xt[i])
                self.rmsnorm_T(xt[i], b_xt[i], hT, b_hT, c["g_fin"][:, :], sq, b_sq, ps[4], b_ps[4], rstd, b_rstd)
                for sb in range(TT // 128):
                    o = cnt % 2
                    for hf in range(2):
                        pb = (cnt * 2 + hf) % 4
                        for k4 in range(4):
                            k = hf * 4 + k4
                            kb.op("pe", lambda e: e.transpose(out=ps[pb][:, k4 * 128:(k4 + 1) * 128],
                                                              in_=hT[:, k, sb * 128:(sb + 1) * 128], identity=c["ident_f"][:]),
                                  reads=(b_hT, self.c_sem), writes=(b_ps[pb],))
                        dst = ot[o][:, hf * 512:(hf + 1) * 512]
                        if hf == 0:
                            kb.op("dve", lambda e: e.tensor_copy(out=dst, in_=ps[pb][:]), reads=(b_ps[pb],), writes=(b_ot[o],))
                        else:
                            kb.op("act", lambda e: e.copy(out=dst, in_=ps[pb][:]), reads=(b_ps[pb],), writes=(b_ot[o],))
                    r0 = t * TT + sb * 128
                    kb.dma("pool", out[r0:r0 + 128, :], ot[o][:], reads=(b_ot[o],), writes=(self.b_out,), sem=b_ot[o])
                    cnt += 1
        self.full_barrier()

    def zero_YT(self, seq):
        kb = self.kb
        bz = kb.buf("zz" + seq.name, dma=True)
        for k in range(12):
            kb.dma("pool", self.YT[seq.name][k], self.ext["dbg_yt_" + seq.name][k], writes=(self.b_YT[seq.name],), sem=bz)
        self.full_barrier()

    def build(self, mixers=True, stages=("attn", "hyena", "gla")):
        with ExitStack() as es:
            self.consts(es)
            self.prepass_weights()
            for s in self.seqs:
                self.phase_in(s)
            for l in range(self.depth):
                for s in self.seqs:
                    if not mixers:
                        if l == 0:
                            self.zero_YT(s)
                    else:
                        if "dbg_yt_" + s.name in self.ext and l == 0:
                            self.zero_YT(s)
                        self.phase_proj(s, l)
                        if "attn" in stages:
                            self.phase_attn(s)
                        if "hyena" in stages:
                            self.phase_hyena(s, l)
                        if "gla" in stages:
                            self.phase_gla(s, l)
                    self.phase_dense(s, l)
            for s in self.seqs:
                self.phase_out(s)
            self.full_barrier()
        self.kb.close()
        return self.nc


def rope_tables(L):
    rows = L // 64
    r = np.repeat(np.arange(rows, dtype=np.float32), 64)
    cc = np.tile(np.arange(64, dtype=np.float32), rows)
    inv = (10000.0 ** (-np.arange(0, 32, 2, dtype=np.float32) / 32)).astype(np.float32)
    ang = np.concatenate([r[:, None] * inv, cc[:, None] * inv], axis=-1).astype(np.float32)
    cos, sin = np.cos(ang).T, np.sin(ang).T
    c128 = np.concatenate([cos, cos, cos, cos], 0).astype(np.float32)
    s128 = np.concatenate([sin, sin, sin, sin], 0).astype(np.float32)
    return np.ascontiguousarray(c128), np.ascontiguousarray(s128)


def hyena_consts(L, tag):
    f32 = np.float32
    NB = L // 128; N1 = 2 * NB; K1 = min(N1, 128); N = 2 * L
    t = np.linspace(0.0, 1.0, L, dtype=f32)
    w = (2.0 * math.pi * np.arange(L, dtype=f32) / L).astype(f32)
    fbands = np.linspace(1e-4, 15, 16, dtype=f32)
    ph = w[:, None] * fbands
    feats = np.concatenate([t[:, None], np.cos(ph), -np.sin(ph)], axis=-1).astype(f32)
    idx = np.concatenate([np.arange(L), [0], L - np.arange(1, L)])
    featsT = np.ascontiguousarray(feats[idx].T)
    tn = t[idx].copy(); tn[L] = 1e4
    tnb = np.ascontiguousarray(np.broadcast_to(tn[None, :], (128, 2 * L))).astype(f32)
    n1 = np.arange(N1)[:, None].astype(np.float64); k1 = np.arange(N1)[None, :].astype(np.float64)
    a = 2 * math.pi * n1 * k1 / N1
    F1 = np.concatenate([np.cos(a), -np.sin(a)], axis=1).astype(f32)
    n2 = np.arange(128)[:, None].astype(np.float64)
    a = 2 * math.pi * n2 * k1 / N
    cpb = 256 // N1
    TW = np.stack([np.tile(np.cos(a), (1, cpb)), np.tile(-np.sin(a), (1, cpb))]).astype(f32)
    a = (2 * math.pi * np.arange(N1)[:, None].astype(np.float64) * np.arange(128)[None, :] / N)
    T2 = np.stack([np.cos(a), np.sin(a)]).astype(f32)
    a = 2 * math.pi * np.arange(N1)[:, None].astype(np.float64) * np.arange(NB)[None, :] / N1
    G1 = np.stack([np.cos(a) / N, -np.sin(a) / N]).astype(f32)
    return {"c_feats_" + tag: featsT, "c_tnb_" + tag: tnb, "c_F1_" + tag: F1, "c_TW_" + tag: TW, "c_T2_" + tag: T2, "c_G1_" + tag: G1}


def hyena_consts_common():
    f32 = np.float32
    a = 2 * math.pi * np.arange(128)[:, None].astype(np.float64) * np.arange(128)[None, :] / 128
    F2 = np.stack([np.cos(a), -np.sin(a), np.sin(a)]).astype(f32)
    GA = np.stack([np.concatenate([np.cos(a), np.sin(a)], 1), np.concatenate([-np.sin(a), np.cos(a)], 1)]).astype(f32)
    deltas = np.linspace(abs(math.log(1e-2) / 1.5), abs(math.log(1e-2) / 0.3), 512, dtype=f32)
    ndel = np.ascontiguousarray((-deltas).reshape(4, 128).T)
    return {"c_F2": F2, "c_GA": GA, "c_ndelta": ndel}

def host_consts(LP, LS):
    rot = np.zeros((128, 128), np.float32)
    for hb in (0, 64):
        for m in range(32):
            rot[hb + m + 32, hb + m] = -1.0
            rot[hb + m, hb + m + 32] = 1.0
    bo = np.zeros((128, 128), np.float32)
    bo[:64, :64] = 1.0
    bo[64:, 64:] = 1.0
    out = {"c_ident": np.eye(128, dtype=np.float32), "c_ones": np.ones((128, 128), np.float32), "c_rot": rot, "c_bo64": bo}
    for nm, L in (("p", LP), ("s", LS)):
        out["c_cos_" + nm], out["c_sin_" + nm] = rope_tables(L)
        out.update(hyena_consts(L, nm))
    out.update(hyena_consts_common())
    j = np.arange(128)[:, None]; i = np.arange(128)[None, :]
    same = (j // 64) == (i // 64)
    tri = np.stack([np.where(same & (j <= i), -1.0 / 16, 0.0), np.where(same & (j >= i), -1.0 / 16, 0.0)]).astype(np.float32)
    mf = np.where(same & (j <= i), 1.0, 0.0); mb = np.where(same & (j > i), 1.0, 0.0)
    msk = np.stack([np.repeat(mf[:, None, :], 4, 1), np.repeat(mb[:, None, :], 4, 1)]).astype(np.float32)
    out["c_tri"], out["c_msk"] = tri, msk
    return out


def gng_layout(g):
    NL = g.shape[0]
    return np.ascontiguousarray(np.broadcast_to(np.tile(g, (1, 4))[:, None, :], (NL, 128, 512))).astype(np.float32)


_CACHE = {}


def kernel(**inputs):
    LP, LS, DEPTH = 16384, 2048, 4
    inp = {k: np.ascontiguousarray(np.asarray(v)) for k, v in inputs.items()}
    hc = host_consts(LP, LS)
    m = {"x_p": inp["x_prompt"][0], **hc}
    for n in WNAMES:
        m[n] = inp[n]
    m["c_gng"] = gng_layout(inp["gla_norm_g"])
    shapes = {k: v.shape for k, v in m.items()}
    shapes["x_s"] = (LS, 1024)
    b = Builder(LP, LS, DEPTH, shapes)
    nc = b.build(mixers=True)
    maps = [dict(m, x_s=inp["x_sample"][c]) for c in range(8)]
    res = run_bass_kernel_spmd(nc, maps, core_ids=list(range(8)))
    yp = np.concatenate([res.results[c]["y_p"][c * 2048:(c + 1) * 2048] for c in range(8)], axis=0)[None]
    ys = np.stack([res.results[c]["y_s"] for c in range(8)], axis=0)
    return (yp.astype(np.float32), ys.astype(np.float32))
```

```python
import os
import numpy as np
import concourse.bass as bass
import concourse.mybir as mybir

F32 = mybir.dt.float32
BF16 = mybir.dt.bfloat16
AF = mybir.ActivationFunctionType
ALU = mybir.AluOpType
AX = mybir.AxisListType


class Buf:
    __slots__ = ("name", "lw", "rd", "sem", "semv")

    def __init__(self, name, sem=None):
        self.name = name
        self.lw = None
        self.rd = {}
        self.sem = sem
        self.semv = 0


class KB:
    def __init__(self, nc):
        self.nc = nc
        self.E = {"pe": nc.tensor, "act": nc.scalar, "dve": nc.vector, "pool": nc.gpsimd, "sp": nc.sync}
        self.sems = {}
        self.cnt = {}
        self.waited = {e: {} for e in self.E}
        self._ctx = []
        for e in self.E:
            self._newsem("E_" + e)
        self.n_ins = 0
        self.epoch = 0
        self.free_d = []
        self.phase_d = []
        self.LIMIT = int(os.environ.get("KB_LIMIT", "150000"))

    def _newsem(self, key):
        while key in self.sems:
            key = key + "_"
        cm = self.nc.semaphore(key)
        h = cm.__enter__()
        self._ctx.append(cm)
        self.sems[key] = h
        self.cnt[key] = 0
        return key

    def dsem(self, name):
        if self.free_d:
            k = self.free_d.pop()
        else:
            k = self._newsem("D_" + name)
        self.phase_d.append(k)
        return k

    def phase_release(self):
        self.free_d.extend(self.phase_d)
        self.phase_d = []

    def full_wait(self):
        tgt = {k: v for k, v in self.cnt.items() if v > 0}
        for e in self.E:
            self._wait(e, dict(tgt))

    def epoch_barrier(self):
        self.full_wait()
        self.nc.all_engine_barrier()
        for k, h in self.sems.items():
            if self.cnt[k] > 0 and k.startswith("E_"):
                self.E["pool"].sem_clear(h)
                self.cnt[k] = 0
        self.nc.all_engine_barrier()
        for e in self.E:
            for k in list(self.waited[e]):
                if k.startswith("E_"):
                    del self.waited[e][k]
        self.epoch += 1

    def buf(self, name, dma=False):
        return Buf(name, self.dsem(name) if dma else None)

    def _deps(self, reads, writes):
        deps = {}
        def add(ev):
            if ev is None:
                return
            k, v, ep = ev
            if ep != self.epoch and k.startswith("E_"):
                return
            if k.startswith("D_"):
                v = self.cnt[k]
            if deps.get(k, 0) < v:
                deps[k] = v
        for b in reads:
            add(b.lw)
        for b in writes:
            add(b.lw)
            for k, (v, ep) in b.rd.items():
                add((k, v, ep))
        return deps

    def _wait(self, eng, deps):
        w = self.waited[eng]
        for k, v in deps.items():
            if w.get(k, 0) < v:
                self.E[eng].wait_ge(self.sems[k], v)
                w[k] = v

    def _mark(self, ev, reads, writes):
        k, v = ev
        ep = self.epoch
        for b in writes:
            b.lw = (k, v, ep)
            b.rd = {}
        for b in reads:
            o = b.rd.get(k)
            if o is None or o[1] != ep or o[0] < v:
                b.rd[k] = (v, ep)

    def op(self, eng, fn, reads=(), writes=()):
        if self.cnt["E_" + eng] >= self.LIMIT:
            self.epoch_barrier()
        deps = self._deps(reads, writes)
        if eng == "pe" and os.environ.get("KB_PE_NOSELF", "0") == "1":
            deps.pop("E_pe", None)
        self._wait(eng, deps)
        ins = fn(self.E[eng])
        k = "E_" + eng
        self.cnt[k] += 1
        ins.then_inc(self.sems[k], 1)
        self._mark((k, self.cnt[k]), reads, writes)
        self.n_ins += 1
        return ins

    def dma(self, q, out, in_, reads=(), writes=(), sem=None, **kw):
        self._wait(q, self._deps(reads, writes))
        ins = self.E[q].dma_start(out=out, in_=in_, **kw)
        k = sem.sem
        self.cnt[k] += 16
        ins.then_inc(self.sems[k], 16)
        self._mark((k, self.cnt[k]), reads, writes)
        self.n_ins += 1
        return ins

    def wait_all(self, eng, bufs):
        self._wait(eng, self._deps(bufs, ()))

    def barrier(self, bufs=()):
        pass

    def close(self):
        for cm in reversed(self._ctx):
            cm.__exit__(None, None, None)


import math
from contextlib import ExitStack
import numpy as np
import ml_dtypes
import concourse.bass as bass
import concourse.mybir as mybir
from concourse.bass_utils import run_bass_kernel_spmd

D = 1024
NH, NKV, HD = 8, 2, 64
D_FF = 2816
NFC = D_FF // 128
IN_COLS = 6944
EPS = 1e-6
TT = 512
P_Q, P_K0, P_K1, P_V, P_HZ, P_GQ, P_GK, P_GV, P_GOG, P_LO = 0, 512, 640, 768, 896, 2432, 2688, 2944, 3456, 3968
NCP = 4000
O_AQ, O_AK, O_AV, O_HZ, O_GQ, O_GK, O_GV, O_GOG, O_GLO, O_GATE = 0, 512, 640, 768, 2304, 2560, 2816, 3328, 3840, 3872

WNAMES = ["norm_mix_g", "w_in", "q_norm_g", "k_norm_g", "hy_conv_w", "hy_conv_b", "hy_w1", "hy_b1", "hy_f1",
          "hy_w2", "hy_b2", "hy_f2", "hy_w3", "hy_skip", "gla_gate_up", "gla_gate_b", "gla_norm_g", "w_branch",
          "w_out", "norm_ffn_g", "w_ffn_gate", "w_ffn_up", "w_ffn_down", "final_norm_g"]


class Seq:
    def __init__(self, name, L):
        self.name, self.L = name, L


class Builder:
    def __init__(self, LP, LS, depth, shapes, debug_taps=()):
        self.LP, self.LS, self.depth = LP, LS, depth
        nc = self.nc = bass.Bass("TRN2", target_bir_lowering=False)
        self.kb = KB(nc)
        self.taps = debug_taps
        self.seqs = [Seq("p", LP), Seq("s", LS)]
        self.ext = {}
        for nm, shp in shapes.items():
            self.ext[nm] = nc.dram_tensor(nm, list(shp), F32, kind="ExternalInput")
        self.out = {"p": nc.dram_tensor("y_p", [LP, D], F32, kind="ExternalOutput"),
                    "s": nc.dram_tensor("y_s", [LS, D], F32, kind="ExternalOutput")}
        self.xin = {"p": self.ext["x_p"], "s": self.ext["x_s"]}
        def scratch(name, shape, dt):
            kind = {"kind": "ExternalOutput"} if name in debug_taps else {}
            return nc.dram_tensor(name, list(shape), dt, **kind)
        self.XT = {s.name: scratch("XT_" + s.name, [8, 128, s.L], F32) for s in self.seqs}
        self.YT = {s.name: scratch("YT_" + s.name, [12, 128, s.L], BF16) for s in self.seqs}
        dp = depth
        self.NL = shapes["norm_mix_g"][0]
        self.QT = {s.name: scratch("QT_" + s.name, [4, 128, s.L], BF16) for s in self.seqs}
        self.KT = {s.name: scratch("KT_" + s.name, [2, 128, s.L], BF16) for s in self.seqs}
        self.VA = {s.name: scratch("VA_" + s.name, [s.L // 128, 128, 2, 128], BF16) for s in self.seqs}
        self.HZT = {s.name: scratch("HZT_" + s.name, [12, 128, s.L], F32) for s in self.seqs}
        self.GQT = {s.name: scratch("GQT_" + s.name, [2, 128, s.L], F32) for s in self.seqs}
        self.GKT = {s.name: scratch("GKT_" + s.name, [2, 128, s.L], F32) for s in self.seqs}
        self.GV = {s.name: scratch("GV_" + s.name, [s.L, 512], BF16) for s in self.seqs}
        self.GOG = {s.name: scratch("GOG_" + s.name, [s.L, 512], F32) for s in self.seqs}
        self.LOT = {s.name: scratch("LOT_" + s.name, [2, 16, s.L], F32) for s in self.seqs}
        self.b_P1 = {s.name: self.kb.buf("P1" + s.name) for s in self.seqs}
        self.dbg = {}
        if "DBG1" in debug_taps:
            self.dbg = {"DBG1": scratch("DBG1", [128, 256], F32), "DBG2": scratch("DBG2", [128, 1024], F32),
                        "DBG3": scratch("DBG3", [128, 1024], BF16), "DBG4": scratch("DBG4", [128, 512], BF16)}
        self.OF = {s.name: scratch("OF_" + s.name, [s.L, 512], F32) for s in self.seqs}
        self.KFT = {s.name: scratch("KFT_" + s.name, [4, 128, 2 * s.L], BF16) for s in self.seqs}
        self.KFH = {s.name: scratch("KFH_" + s.name, [512 // self.hy_cfg(s.L)[4], 128, 2, 512], BF16) for s in self.seqs}
        self.UD = {s.name: scratch("UD_" + s.name, [4, 128, s.L], BF16) for s in self.seqs}
        self.X2R = {s.name: scratch("X2R_" + s.name, [4, 128, s.L], F32) for s in self.seqs}
        self.X2S = {s.name: scratch("X2S_" + s.name, [4, 128, s.L], F32) for s in self.seqs}
        self.wb_in = scratch("wb_in", [dp, D, NCP], BF16)
        self.wb = {
            "w_gates": scratch("wb_gates", [dp, D, 3072], BF16),
            "w_branch": scratch("wb_branch", [dp, 1536, D], BF16),
            "w_out": scratch("wb_out", [dp, D, D], BF16),
            "w_ffn_gate": scratch("wb_fg", [dp, D, D_FF], BF16),
            "w_ffn_up": scratch("wb_fu", [dp, D, D_FF], BF16),
            "w_ffn_down": scratch("wb_fd", [dp, D_FF, D], BF16),
        }
        kb = self.kb
        self.b_XT = {s.name: kb.buf("XT" + s.name) for s in self.seqs}
        self.b_YT = {s.name: kb.buf("YT" + s.name) for s in self.seqs}
        self.b_wb = kb.buf("wb", dma=True)
        self.b_out = kb.buf("out")

    def sbt(self, name, shape, dt):
        self._uid = getattr(self, "_uid", 0) + 1
        return self.nc.sbuf_tensor(f"{name}_{self._uid}", shape, dt)

    def pst(self, name, shape, dt):
        self._uid = getattr(self, "_uid", 0) + 1
        return self.nc.psum_tensor(f"{name}_{self._uid}", shape, dt)

    def full_barrier(self):
        self.kb.full_wait()
        self.kb.phase_release()

    def consts(self, es):
        nc, kb = self.nc, self.kb
        dp = self.NL
        self.c_sem = kb.buf("consts", dma=True)
        c = self.c = {}
        def ld(name, shape, dt, src_ap, q="sp"):
            t = es.enter_context(self.sbt("c_" + name, shape, dt))
            kb.dma(q, t[:], src_ap, reads=(), writes=(self.c_sem,), sem=self.c_sem, allow_slow_non_contiguous=True)
            c[name] = t
            return t
        ld("ident_f", [128, 128], F32, self.ext["c_ident"][:, :])
        ld("ident_b", [128, 128], BF16, self.ext["c_ident"][:, :], q="pool")
        ld("ones_b", [128, 128], BF16, self.ext["c_ones"][:, :], q="pool")
        ld("g_mix", [128, dp, 8], F32, self.ext["norm_mix_g"].ap().rearrange("l (k p) -> p l k", p=128))
        ld("g_ffn", [128, dp, 8], F32, self.ext["norm_ffn_g"].ap().rearrange("l (k p) -> p l k", p=128))
        ld("rot", [128, 128], BF16, self.ext["c_rot"][:, :], q="pool")
        ld("bo64", [128, 128], BF16, self.ext["c_bo64"][:, :], q="pool")
        for nm, src in (("gq", "q_norm_g"), ("gk", "k_norm_g")):
            t = es.enter_context(self.sbt("c_" + nm, [128, dp], F32))
            for hh in range(2):
                kb.dma("sp", t[hh * 64:(hh + 1) * 64, :], self.ext[src].ap().rearrange("l d -> d l"), writes=(self.c_sem,),
                       sem=self.c_sem, allow_slow_non_contiguous=True)
            c[nm] = t
        ld("g_fin", [128, 8], F32, self.ext["final_norm_g"].ap().rearrange("(k p) -> p k", p=128))

    def prepass_weights(self):
        kb = self.kb
        e = self.ext
        for l in range(self.depth):
            for r in range(8):
                rs = slice(r * 128, (r + 1) * 128)
                def cp(dst0, src0, n):
                    kb.dma("pool", self.wb_in[l, rs, dst0:dst0 + n], e["w_in"][l, rs, src0:src0 + n], writes=(self.b_wb,), sem=self.b_wb)
                cp(P_Q, O_AQ, 512)
                for g in range(2):
                    cp(P_K0 + g * 128, O_AK + g * 64, 64)
                    cp(P_K0 + g * 128 + 64, O_AK + g * 64, 64)
                cp(P_V, O_AV, 128)
                cp(P_HZ, O_HZ, 1536 + 256 + 256 + 512 + 512 + 32)
                kb.dma("pool", self.wb["w_gates"][l, rs, :], e["w_in"][l, rs, O_GATE:O_GATE + 3072],
                       writes=(self.b_wb,), sem=self.b_wb)
                kb.dma("pool", self.wb["w_out"][l, rs, :], e["w_out"][l, rs, :], writes=(self.b_wb,), sem=self.b_wb)
                kb.dma("pool", self.wb["w_ffn_gate"][l, rs, :], e["w_ffn_gate"][l, rs, :], writes=(self.b_wb,), sem=self.b_wb)
                kb.dma("pool", self.wb["w_ffn_up"][l, rs, :], e["w_ffn_up"][l, rs, :], writes=(self.b_wb,), sem=self.b_wb)
            for r in range(NFC):
                rs = slice(r * 128, (r + 1) * 128)
                kb.dma("pool", self.wb["w_ffn_down"][l, rs, :], e["w_ffn_down"][l, rs, :], writes=(self.b_wb,), sem=self.b_wb)
            for n in range(3):
                for r in range(4):
                    kb.dma("pool", self.wb["w_branch"][l, n * 512 + r * 128:n * 512 + (r + 1) * 128, :],
                           e["w_branch"][l, n, r * 128:(r + 1) * 128, :], writes=(self.b_wb,), sem=self.b_wb)

    def phase_in(self, seq):
        nc, kb, c = self.nc, self.kb, self.c
        L = seq.L
        x = self.xin[seq.name]
        with ExitStack() as es:
            xt = [es.enter_context(self.sbt(f"pi_x{i}", [128, D], F32)) for i in range(2)]
            xo = [es.enter_context(self.sbt(f"pi_o{i}", [128, 8, 128], F32)) for i in range(2)]
            ps = [es.enter_context(self.pst(f"pi_ps{i}", [128, 512], F32)) for i in range(4)]
            b_xt = [kb.buf(f"pi_x{i}", dma=True) for i in range(2)]
            b_xo = [kb.buf(f"pi_o{i}", dma=True) for i in range(2)]
            b_ps = [kb.buf(f"pi_ps{i}") for i in range(4)]
            for bl in range(L // 128):
                i = bl % 2
                kb.dma("sp", xt[i][:], x[bl * 128:(bl + 1) * 128, :], writes=(b_xt[i],), sem=b_xt[i])
                for hf in range(2):
                    pb = (bl * 2 + hf) % 4
                    for k4 in range(4):
                        k = hf * 4 + k4
                        kb.op("pe", lambda e: e.transpose(out=ps[pb][:, k4 * 128:(k4 + 1) * 128],
                                                          in_=xt[i][:, k * 128:(k + 1) * 128], identity=c["ident_f"][:]),
                              reads=(b_xt[i], self.c_sem), writes=(b_ps[pb],))
                    eng = "dve" if hf == 0 else "act"
                    dst = xo[i][:, hf * 4:(hf + 1) * 4, :]
                    src = ps[pb][:].rearrange("p (k n) -> p k n", k=4)
                    if eng == "dve":
                        kb.op("dve", lambda e: e.tensor_copy(out=dst, in_=src), reads=(b_ps[pb],), writes=(b_xo[i],))
                    else:
                        kb.op("act", lambda e: e.copy(out=dst, in_=src), reads=(b_ps[pb],), writes=(b_xo[i],))
                kb.dma("pool", self.XT[seq.name][:, :, bl * 128:(bl + 1) * 128].rearrange("k p n -> p k n"), xo[i][:],
                       reads=(b_xo[i],), writes=(self.b_XT[seq.name],), sem=b_xo[i])
        self.full_barrier()

    def rmsnorm_T(self, xt, b_xt, hT, b_hT, gains, sq, b_sq, ps, b_ps, rstd, b_rstd, extra_reads=()):
        kb, c = self.kb, self.c
        kb.op("act", lambda e: e.activation(out=sq[:], in_=xt[:], func=AF.Square), reads=(b_xt,), writes=(b_sq,))
        for k in range(8):
            kb.op("pe", lambda e: e.matmul(ps[:], lhsT=c["ones_b"][:], rhs=sq[:, k, :], start=(k == 0), stop=(k == 7)),
                  reads=(b_sq, self.c_sem), writes=(b_ps,))
        kb.op("dve", lambda e: e.tensor_scalar(out=rstd[:], in0=ps[:], scalar1=1.0 / D, scalar2=EPS, op0=ALU.mult, op1=ALU.add),
              reads=(b_ps,), writes=(b_rstd,))
        kb.op("act", lambda e: e.activation(out=rstd[:], in_=rstd[:], func=AF.Ln), reads=(b_rstd,), writes=(b_rstd,))
        kb.op("act", lambda e: e.activation(out=rstd[:], in_=rstd[:], func=AF.Exp, scale=-0.5), reads=(b_rstd,), writes=(b_rstd,))
        for k in range(8):
            kb.op("dve", lambda e: e.scalar_tensor_tensor(out=hT[:, k, :], in0=xt[:, k, :], scalar=gains[:, k:k + 1], in1=rstd[:],
                                                          op0=ALU.mult, op1=ALU.mult),
                  reads=(b_xt, b_rstd, self.c_sem) + tuple(extra_reads), writes=(b_hT,))


    def phase_proj(self, seq, l):
        nc, kb, c = self.nc, self.kb, self.c
        L, sn = seq.L, seq.name
        XT, bXT = self.XT[sn], self.b_XT[sn]
        bP1 = self.b_P1[sn]
        with ExitStack() as es:
            S = lambda n, shp, dt: es.enter_context(self.sbt("pp_" + n, shp, dt))
            w = S("w", [128, 8, NCP], BF16)
            b_w = kb.buf("pp_w", dma=True)
            for k in range(8):
                kb.dma("sp", w[:, k, :], self.wb_in[l, k * 128:(k + 1) * 128, :], reads=(self.b_wb,), writes=(b_w,), sem=b_w)
            xt = [S(f"xt{i}", [128, 8, TT], F32) for i in range(2)]
            cs = [S(f"cs{i}", [128, 2, TT], F32) for i in range(2)]
            hT = S("hT", [128, 8, TT], BF16)
            sq = S("sq", [128, 8, TT], BF16)
            rstd = S("rstd", [128, TT], F32)
            NO = 4
            ob = [S(f"ob{i}", [128, 2048], F32) for i in range(NO)]
            qn = [S(f"qn{i}", [128, TT], BF16) for i in range(2)]
            q2 = [S(f"q2{i}", [128, TT], BF16) for i in range(2)]
            rq = [S(f"rq{i}", [128, TT], F32) for i in range(2)]
            t1 = [S(f"t1{i}", [128, TT], F32) for i in range(2)]
            va = [S(f"va{i}", [128, 4, 2, 128], BF16) for i in range(2)]
            ps = [es.enter_context(self.pst(f"pp_ps{i}", [128, 512], F32)) for i in range(8)]
            b_xt = [kb.buf(f"pp_xt{i}", dma=True) for i in range(2)]
            b_cs = [kb.buf(f"pp_cs{i}", dma=True) for i in range(2)]
            b_ob = [kb.buf(f"pp_ob{i}", dma=True) for i in range(NO)]
            b_va = [kb.buf(f"pp_va{i}", dma=True) for i in range(2)]
            b_ps = [kb.buf(f"pp_ps{i}") for i in range(8)]
            b_hT, b_sq, b_rstd = kb.buf("hT"), kb.buf("sq"), kb.buf("rstd")
            b_qn, b_rq, b_t1 = ([kb.buf(n + str(i)) for i in range(2)] for n in ("qn", "rq", "t1"))
            b_q2 = [kb.buf(f"pp_q2{i}", dma=True) for i in range(2)]
            for i in range(2):
                kb.op("dve", lambda e: e.memset(va[i][:], 1.0), writes=(b_va[i],))
            st = {"ps": 0, "ob": 0, "j": 0}
            def nps():
                i = st["ps"] % 7
                st["ps"] += 1
                return ps[i], b_ps[i]
            def nob():
                i = st["ob"] % NO
                st["ob"] += 1
                return ob[i], b_ob[i]
            cosT, sinT = self.ext["c_cos_" + sn], self.ext["c_sin_" + sn]
            ntile = L // TT
            def load_tile(t):
                i = t % 2
                ts = slice(t * TT, (t + 1) * TT)
                kb.dma("sp", xt[i][:], XT[:, :, ts].rearrange("k p n -> p k n"), reads=(bXT,), writes=(b_xt[i],), sem=b_xt[i])
                kb.dma("sp", cs[i][:, 0, :], cosT[:, ts], writes=(b_cs[i],), sem=b_cs[i])
                kb.dma("sp", cs[i][:, 1, :], sinT[:, ts], writes=(b_cs[i],), sem=b_cs[i])
            load_tile(0)
            for t in range(ntile):
                i = t % 2
                ts = slice(t * TT, (t + 1) * TT)
                if t + 1 < ntile:
                    load_tile(t + 1)
                self.rmsnorm_T(xt[i], b_xt[i], hT, b_hT, c["g_mix"][:, l, :], sq, b_sq, ps[7], b_ps[7], rstd, b_rstd)

                def fm_mm(col0, M=128):
                    p, bp = nps()
                    for k in range(8):
                        kb.op("pe", lambda e: e.matmul(p[0:M, :], lhsT=w[:, k, col0:col0 + M], rhs=hT[:, k, :], start=(k == 0), stop=(k == 7)),
                              reads=(b_w, b_hT), writes=(bp,))
                    return p, bp

                for ci in range(6):
                    isq = ci < 4
                    col0 = P_Q + ci * 128 if isq else P_K0 + (ci - 4) * 128
                    gain = c["gq"] if isq else c["gk"]
                    p, bp = fm_mm(col0)
                    j = st["j"] % 2
                    st["j"] += 1
                    kb.op("act", lambda e: e.activation(out=q2[j][:], in_=p[:], func=AF.Square), reads=(bp,), writes=(b_q2[j],))
                    p2, bp2 = nps()
                    kb.op("pe", lambda e: e.matmul(p2[:], lhsT=c["bo64"][:], rhs=q2[j][:], start=True, stop=True),
                          reads=(b_q2[j], self.c_sem), writes=(bp2,))
                    kb.op("dve", lambda e: e.tensor_scalar(out=rq[j][:], in0=p2[:], scalar1=1.0 / HD, scalar2=EPS, op0=ALU.mult, op1=ALU.add),
                          reads=(bp2,), writes=(b_rq[j],))
                    kb.op("act", lambda e: e.activation(out=rq[j][:], in_=rq[j][:], func=AF.Ln), reads=(b_rq[j],), writes=(b_rq[j],))
                    kb.op("act", lambda e: e.activation(out=rq[j][:], in_=rq[j][:], func=AF.Exp, scale=-0.5), reads=(b_rq[j],), writes=(b_rq[j],))
                    kb.op("dve", lambda e: e.scalar_tensor_tensor(out=qn[j][:], in0=p[:], scalar=gain[:, l:l + 1], in1=rq[j][:],
                                                                  op0=ALU.mult, op1=ALU.mult),
                          reads=(bp, b_rq[j], self.c_sem), writes=(b_qn[j],))
                    p3, bp3 = nps()
                    kb.op("pe", lambda e: e.matmul(p3[:], lhsT=c["rot"][:], rhs=qn[j][:], start=True, stop=True),
                          reads=(b_qn[j], self.c_sem), writes=(bp3,))
                    kb.op("dve", lambda e: e.tensor_tensor(out=t1[j][:], in0=qn[j][:], in1=cs[i][:, 0, :], op=ALU.mult),
                          reads=(b_qn[j], b_cs[i]), writes=(b_t1[j],))
                    kb.op("dve", lambda e: e.tensor_tensor(out=rq[j][:], in0=p3[:], in1=cs[i][:, 1, :], op=ALU.mult),
                          reads=(bp3, b_cs[i]), writes=(b_rq[j],))
                    kb.op("dve", lambda e: e.tensor_tensor(out=q2[j][:], in0=t1[j][:], in1=rq[j][:], op=ALU.add),
                          reads=(b_t1[j], b_rq[j]), writes=(b_q2[j],))
                    dst = self.QT[sn][ci, :, ts] if isq else self.KT[sn][ci - 4, :, ts]
                    kb.dma("pool", dst, q2[j][:], reads=(b_q2[j],), writes=(bP1,), sem=b_q2[j])
                for ci in range(16):
                    p, bp = fm_mm(P_HZ + ci * 128)
                    o, bo = nob()
                    eng = "act" if ci % 2 else "dve"
                    if eng == "dve":
                        kb.op("dve", lambda e: e.tensor_copy(out=o[:, 0:TT], in_=p[:]), reads=(bp,), writes=(bo,))
                    else:
                        kb.op("act", lambda e: e.copy(out=o[:, 0:TT], in_=p[:]), reads=(bp,), writes=(bo,))
                    if ci < 12:
                        dst = self.HZT[sn][ci, :, ts]
                    elif ci < 14:
                        dst = self.GQT[sn][ci - 12, :, ts]
                    else:
                        dst = self.GKT[sn][ci - 14, :, ts]
                    kb.dma("pool", dst, o[:, 0:TT], reads=(bo,), writes=(bP1,), sem=bo)
                for d in range(2):
                    p, bp = fm_mm(P_LO + d * 16, M=16)
                    o, bo = nob()
                    kb.op("dve", lambda e: e.tensor_copy(out=o[0:16, 0:TT], in_=p[0:16, :]), reads=(bp,), writes=(bo,))
                    kb.dma("pool", self.LOT[sn][d, :, ts], o[0:16, 0:TT], reads=(bo,), writes=(bP1,), sem=bo)
                pv, bpv = nps()
                for sb in range(4):
                    for k in range(8):
                        kb.op("pe", lambda e: e.matmul(pv[:, sb * 128:(sb + 1) * 128], lhsT=hT[:, k, sb * 128:(sb + 1) * 128],
                                                       rhs=w[:, k, P_V:P_V + 128], start=(k == 0), stop=(k == 7)),
                              reads=(b_w, b_hT), writes=(bpv,))
                kb.op("dve", lambda e: e.tensor_copy(out=va[i][:, :, :, 0:64], in_=pv[:].rearrange("p (s g d) -> p s g d", s=4, g=2)),
                      reads=(bpv,), writes=(b_va[i],))
                kb.dma("pool", self.VA[sn][t * 4:(t + 1) * 4].rearrange("s p g d -> p s g d"), va[i][:], reads=(b_va[i],),
                       writes=(bP1,), sem=b_va[i])
                for sb in range(4):
                    r0 = t * TT + sb * 128
                    for which in range(2):
                        col0 = P_GV if which == 0 else P_GOG
                        p, bp = nps()
                        for k in range(8):
                            kb.op("pe", lambda e: e.matmul(p[:], lhsT=hT[:, k, sb * 128:(sb + 1) * 128], rhs=w[:, k, col0:col0 + 512],
                                                           start=(k == 0), stop=(k == 7)), reads=(b_w, b_hT), writes=(bp,))
                        o, bo = nob()
                        if which == 0:
                            ovb = o[:].bitcast(BF16)[:, 0:512]
                            kb.op("act", lambda e: e.copy(out=ovb, in_=p[:]), reads=(bp,), writes=(bo,))
                            kb.dma("pool", self.GV[sn][r0:r0 + 128, :], ovb, reads=(bo,), writes=(bP1,), sem=bo)
                        else:
                            kb.op("dve", lambda e: e.tensor_copy(out=o[:, 0:512], in_=p[:]), reads=(bp,), writes=(bo,))
                            kb.dma("pool", self.GOG[sn][r0:r0 + 128, :], o[:, 0:512], reads=(bo,), writes=(bP1,), sem=bo)
        self.full_barrier()


    def phase_attn(self, seq):
        nc, kb, c = self.nc, self.kb, self.c
        L, sn = seq.L, seq.name
        bP1, bYT = self.b_P1[sn], self.b_YT[sn]
        NKB, NQG = L // 128, L // TT
        with ExitStack() as es:
            S = lambda n, shp, dt: es.enter_context(self.sbt("pa_" + n, shp, dt))
            kt = S("kt", [128, L], BF16)
            vg = S("vg", [128, NKB, 128], BF16)
            qt = [S(f"qt{i}", [128, TT], BF16) for i in range(2)]
            NP = 4
            pT = [S(f"pT{i}", [128, TT], BF16) for i in range(NP)]
            osb = [S(f"osb{i}", [128, TT], F32) for i in range(2)]
            den = [S(f"den{i}", [128, TT], F32) for i in range(2)]
            yo = [S(f"yo{i}", [128, TT], BF16) for i in range(2)]
            ps_s = [es.enter_context(self.pst(f"pa_s{i}", [128, 512], F32)) for i in range(4)]
            ps_o = [es.enter_context(self.pst(f"pa_o{i}", [128, 512], F32)) for i in range(2)]
            b_kt, b_vg = kb.buf("pa_kt", dma=True), kb.buf("pa_vg", dma=True)
            b_qt = [kb.buf(f"pa_qt{i}", dma=True) for i in range(2)]
            b_pT = [kb.buf(f"pa_pT{i}") for i in range(NP)]
            b_osb = [kb.buf(f"pa_osb{i}", dma=True) for i in range(2)]
            b_den = [kb.buf(f"pa_den{i}", dma=True) for i in range(2)]
            b_yo = [kb.buf(f"pa_yo{i}", dma=True) for i in range(2)]
            b_s = [kb.buf(f"pa_s{i}") for i in range(4)]
            b_o = [kb.buf(f"pa_o{i}") for i in range(2)]
            groups = [(g, hh, qg) for g in range(2) for hh in range(4) for qg in range(NQG)]
            def load_q(gi):
                g, hh, qg = groups[gi]
                h = 4 * g + hh
                ch, base = h // 2, 64 * (h % 2)
                j = gi % 2
                kb.dma("sp", qt[j][base:base + 64, :], self.QT[sn][ch, base:base + 64, qg * TT:(qg + 1) * TT], reads=(bP1,),
                       writes=(b_qt[j],), sem=b_qt[j])
            it = 0
            SK = 2
            load_q(0)
            for gi, (g, hh, qg) in enumerate(groups):
                if hh == 0 and qg == 0:
                    kb.dma("sp", kt[:], self.KT[sn][g], reads=(bP1,), writes=(b_kt,), sem=b_kt)
                    kb.dma("sp", vg[:], self.VA[sn][:, :, g, :].rearrange("b p d -> p b d"), reads=(bP1,), writes=(b_vg,), sem=b_vg)
                if gi + 1 < len(groups):
                    load_q(gi + 1)
                h = 4 * g + hh
                ch, base = h // 2, 64 * (h % 2)
                rows = slice(base, base + 64)
                qs = slice(qg * TT, (qg + 1) * TT)
                j = gi % 2
                po, bpo = ps_o[j], b_o[j]
                it0 = it
                for step in range(NKB + SK):
                    if step < NKB:
                        kbk = step
                        s_i, p_i = (it0 + kbk) % 4, (it0 + kbk) % NP
                        kb.op("pe", lambda e: e.matmul(ps_s[s_i][:], lhsT=kt[rows, kbk * 128:(kbk + 1) * 128], rhs=qt[j][rows, :],
                                                       start=True, stop=True), reads=(b_kt, b_qt[j]), writes=(b_s[s_i],))
                        kb.op("act", lambda e: e.activation(out=pT[p_i][:], in_=ps_s[s_i][:], func=AF.Exp, scale=HD ** -0.5),
                              reads=(b_s[s_i],), writes=(b_pT[p_i],))
                    if step >= SK:
                        kbk = step - SK
                        p_i = (it0 + kbk) % NP
                        kb.op("pe", lambda e: e.matmul(po[:], lhsT=vg[:, kbk, :], rhs=pT[p_i][:], start=(kbk == 0), stop=(kbk == NKB - 1)),
                              reads=(b_vg, b_pT[p_i]), writes=(bpo,))
                it += NKB
                kb.op("dve", lambda e: e.tensor_copy(out=osb[j][:], in_=po[:]), reads=(bpo,), writes=(b_osb[j],))
                kb.dma("pool", den[j][0:64, :], osb[j][64:128, :], reads=(b_osb[j],), writes=(b_den[j],), sem=b_den[j])
                kb.op("dve", lambda e: e.reciprocal(out=den[j][0:64, :], in_=den[j][0:64, :]), reads=(b_den[j],), writes=(b_den[j],))
                kb.op("dve", lambda e: e.tensor_tensor(out=yo[j][0:64, :], in0=osb[j][0:64, :], in1=den[j][0:64, :], op=ALU.mult),
                      reads=(b_osb[j], b_den[j]), writes=(b_yo[j],))
                kb.dma("pool", self.YT[sn][ch, rows, qs], yo[j][0:64, :], reads=(b_yo[j],), writes=(bYT,), sem=b_yo[j])
        self.full_barrier()


    def hy_cfg(self, L):
        NB = L // 128
        N1 = 2 * NB
        K1 = min(N1, 128)
        KC = N1 // K1
        G = 512 // N1
        return NB, N1, K1, KC, G

    def phase_hyena(self, seq, l):
        nc, kb, c = self.nc, self.kb, self.c
        L, sn = seq.L, seq.name
        NB, N1, K1, KC, G = self.hy_cfg(L)
        KB1 = min(NB, 128)
        cpb = 256 // N1
        ext = self.ext
        bP1, bYT = self.b_P1[sn], self.b_YT[sn]
        KFT, KFH, UD, X2R, X2S = self.KFT[sn], self.KFH[sn], self.UD[sn], self.X2R[sn], self.X2S[sn]
        bH = kb.buf("hy_dram" + sn)
        PI = math.pi
        with ExitStack() as es:
            S = lambda n, shp, dt: es.enter_context(self.sbt("ph_" + n, shp, dt))
            P = lambda n: es.enter_context(self.pst("ph_" + n, [128, 512], F32))
            bc = kb.buf("ph_c", dma=True)
            def ldc(name, shape, dt, src, q="sp"):
                t = S(name, shape, dt)
                kb.dma(q, t[:], src, writes=(bc,), sem=bc, allow_slow_non_contiguous=True)
                return t
            w1 = ldc("w1", [33, 64], F32, ext["hy_w1"][l])
            w2 = ldc("w2", [64, 64], F32, ext["hy_w2"][l])
            w3 = ldc("w3", [64, 1024], F32, ext["hy_w3"][l])
            pv = S("pv", [64, 4], F32)
            for j, nm in enumerate(("hy_b1", "hy_f1", "hy_b2", "hy_f2")):
                kb.dma("sp", pv[:, j:j + 1], ext[nm].ap()[l:l + 1, :].rearrange("o d -> d o"), writes=(bc,), sem=bc,
                       allow_slow_non_contiguous=True)
            ndel = ldc("ndel", [128, 4], F32, ext["c_ndelta"][:, :])
            skp = ldc("skp", [128, 4], F32, ext["hy_skip"].ap()[l].rearrange("(k p) -> p k", p=128))
            cw = ldc("cw", [128, 3, 12], F32, ext["hy_conv_w"].ap()[l].rearrange("j (k p) -> p j k", p=128))
            cb = ldc("cb", [128, 12], F32, ext["hy_conv_b"].ap()[l].rearrange("(k p) -> p k", p=128))
            fb = S("fb", [64, 2], F32)
            negpi = S("negpi", [128, 1], F32)
            b_fb = kb.buf("fb")
            kb.op("dve", lambda e: e.memset(negpi[:], -PI), writes=(b_fb,))
            kb.op("dve", lambda e: e.tensor_tensor(out=fb[:, 0:1], in0=pv[:, 0:1], in1=pv[:, 1:2], op=ALU.mult), reads=(bc,), writes=(b_fb,))
            kb.op("dve", lambda e: e.tensor_tensor(out=fb[:, 1:2], in0=pv[:, 2:3], in1=pv[:, 3:4], op=ALU.mult), reads=(bc, b_fb), writes=(b_fb,))
            kb.op("dve", lambda e: e.tensor_scalar_add(out=fb[:], in0=fb[:], scalar1=17.0 * PI), reads=(b_fb,), writes=(b_fb,))
            ps = [P(f"ps{i}") for i in range(8)]
            b_ps = [kb.buf(f"ph_ps{i}") for i in range(8)]
            NT = 2 * L // TT
            ft = [S(f"ft{i}", [33, TT], F32) for i in range(2)]
            tn = [S(f"tn{i}", [128, TT], F32) for i in range(2)]
            b_ft = [kb.buf(f"ph_ft{i}", dma=True) for i in range(2)]
            hh = [S(f"hh{i}", [64, TT], F32) for i in range(2)]
            b_hh = [kb.buf(f"hh{i}") for i in range(2)]
            ki = S("ki", [64, TT], mybir.dt.int32); b_ki = kb.buf("ki")
            kff = S("kff", [64, TT], F32); b_kff = kb.buf("kff")
            dec = S("dec", [128, TT], F32); b_dec = kb.buf("dec")
            kf = S("kf", [128, TT], F32); b_kf = kb.buf("kf")
            kfb = [S(f"kfb{i}", [128, TT], BF16) for i in range(2)]
            b_kfb = [kb.buf(f"ph_kfb{i}", dma=True) for i in range(2)]
            ssq = S("ssq", [128, 4, NT], F32); b_ssq = kb.buf("ssq")
            kb.op("dve", lambda e: e.memset(ssq[:], 0.0), writes=(b_ssq,))
            featsT, tnb = ext["c_feats_" + sn], ext["c_tnb_" + sn]
            it = 0
            for t in range(NT):
                i = t % 2
                ts = slice(t * TT, (t + 1) * TT)
                kb.dma("sp", ft[i][:], featsT[:, ts], writes=(b_ft[i],), sem=b_ft[i])
                kb.dma("sp", tn[i][:], tnb[:, ts], writes=(b_ft[i],), sem=b_ft[i])
                src, bsrc, wm, K = ft[i], b_ft[i], w1, 33
                for layer in range(2):
                    kb.op("pe", lambda e: e.matmul(ps[0][0:64, :], lhsT=wm[0:K, :], rhs=src[0:K, :], start=True, stop=True),
                          reads=(bsrc, bc), writes=(b_ps[0],))
                    h, bh = hh[layer], b_hh[layer]
                    fcol = pv[:, 1:2] if layer == 0 else pv[:, 3:4]
                    kb.op("dve", lambda e: e.tensor_scalar(out=h[:], in0=ps[0][0:64, :], scalar1=fcol, scalar2=fb[:, layer:layer + 1],
                                                           op0=ALU.mult, op1=ALU.add), reads=(b_ps[0], bc, b_fb), writes=(bh,))
                    kb.op("dve", lambda e: e.tensor_scalar(out=ki[:], in0=h[:], scalar1=1.0 / (2.0 * PI), scalar2=None, op0=ALU.mult),
                          reads=(bh,), writes=(b_ki,))
                    kb.op("dve", lambda e: e.tensor_copy(out=kff[:], in_=ki[:]), reads=(b_ki,), writes=(b_kff,))
                    kb.op("dve", lambda e: e.scalar_tensor_tensor(out=h[:], in0=kff[:], scalar=-2.0 * PI, in1=h[:], op0=ALU.mult, op1=ALU.add),
                          reads=(b_kff, bh), writes=(bh,))
                    kb.op("dve", lambda e: e.tensor_scalar(out=kff[:], in0=h[:], scalar1=PI, scalar2=-2.0 * PI, op0=ALU.is_gt, op1=ALU.mult),
                          reads=(bh,), writes=(b_kff,))
                    kb.op("dve", lambda e: e.tensor_tensor(out=h[:], in0=h[:], in1=kff[:], op=ALU.add), reads=(bh, b_kff), writes=(bh,))
                    kb.op("act", lambda e: e.activation(out=h[:], in_=h[:], func=AF.Sin, scale=-1.0), reads=(bh,), writes=(bh,))
                    src, bsrc, wm, K = h, bh, w2, 64
                half = 0 if t * TT < L else 1
                for cc in range(4):
                    pb, bpb = ps[1 + cc % 2], b_ps[1 + cc % 2]
                    col0 = half * 512 + cc * 128
                    kb.op("pe", lambda e: e.matmul(pb[:], lhsT=w3[:, col0:col0 + 128], rhs=hh[1][:], start=True, stop=True),
                          reads=(b_hh[1], bc), writes=(bpb,))
                    kb.op("act", lambda e: e.activation(out=dec[:], in_=tn[i][:], func=AF.Exp, scale=ndel[:, cc:cc + 1]),
                          reads=(b_ft[i], bc), writes=(b_dec,))
                    kb.op("dve", lambda e: e.tensor_tensor(out=kf[:], in0=pb[:], in1=dec[:], op=ALU.mult), reads=(bpb, b_dec), writes=(b_kf,))
                    j = it % 2
                    it += 1
                    kb.op("act", lambda e: e.copy(out=kfb[j][:], in_=kf[:]), reads=(b_kf,), writes=(b_kfb[j],))
                    kb.op("act", lambda e: e.activation(out=dec[:], in_=kf[:], func=AF.Square), reads=(b_kf,), writes=(b_dec,))
                    kb.op("dve", lambda e: e.reduce_sum(out=ssq[:, cc, t:t + 1], in_=dec[:], axis=AX.X), reads=(b_dec,), writes=(b_ssq,))
                    kb.dma("pool", KFT[cc, :, ts], kfb[j][:], reads=(b_kfb[j],), writes=(bH,), sem=b_kfb[j])
            rn = S("rn", [128, 4], F32); b_rn = kb.buf("rn")
            kb.op("dve", lambda e: e.reduce_sum(out=rn[:], in_=ssq[:], axis=AX.X), reads=(b_ssq,), writes=(b_rn,))
            kb.op("dve", lambda e: e.tensor_scalar_add(out=rn[:], in0=rn[:], scalar1=EPS), reads=(b_rn,), writes=(b_rn,))
            kb.op("act", lambda e: e.activation(out=rn[:], in_=rn[:], func=AF.Ln), reads=(b_rn,), writes=(b_rn,))
            kb.op("act", lambda e: e.activation(out=rn[:], in_=rn[:], func=AF.Exp, scale=-0.5), reads=(b_rn,), writes=(b_rn,))
            zt = [S(f"zt{i}", [128, 3, TT + 2], F32) for i in range(2)]
            b_zt = [kb.buf(f"ph_zt{i}", dma=True) for i in range(2)]
            zc = S("zc", [128, 3, TT], F32); b_zc = kb.buf("zc")
            ub = [S(f"ub{i}", [128, TT], BF16) for i in range(2)]
            xr = [S(f"xr{i}", [128, 2, TT], F32) for i in range(2)]
            b_ub = [kb.buf(f"ph_ub{i}", dma=True) for i in range(2)]
            b_xr = [kb.buf(f"ph_xr{i}", dma=True) for i in range(2)]
            HZT = self.HZT[sn]
            it = 0
            for cc in range(4):
                for t in range(L // TT):
                    i = it % 2
                    it += 1
                    t0 = t * TT
                    lo, hi = max(t0 - 1, 0), min(t0 + TT + 1, L)
                    if t0 == 0 or t0 + TT == L:
                        kb.op("dve", lambda e: e.memset(zt[i][:], 0.0), writes=(b_zt[i],))
                    for r in range(3):
                        kb.dma("sp", zt[i][:, r, lo - (t0 - 1):hi - (t0 - 1)], HZT[r * 4 + cc, :, lo:hi], reads=(bP1,), writes=(b_zt[i],), sem=b_zt[i])
                    for r in range(3):
                        ch = r * 4 + cc
                        kb.op("dve", lambda e: e.tensor_scalar(out=zc[:, r, :], in0=zt[i][:, r, 1:TT + 1], scalar1=cw[:, 1, ch:ch + 1],
                                                               scalar2=cb[:, ch:ch + 1], op0=ALU.mult, op1=ALU.add),
                              reads=(b_zt[i], bc), writes=(b_zc,))
                        kb.op("dve", lambda e: e.scalar_tensor_tensor(out=zc[:, r, :], in0=zt[i][:, r, 0:TT], scalar=cw[:, 0, ch:ch + 1],
                                                                      in1=zc[:, r, :], op0=ALU.mult, op1=ALU.add),
                              reads=(b_zt[i], bc, b_zc), writes=(b_zc,))
                        kb.op("dve", lambda e: e.scalar_tensor_tensor(out=zc[:, r, :], in0=zt[i][:, r, 2:TT + 2], scalar=cw[:, 2, ch:ch + 1],
                                                                      in1=zc[:, r, :], op0=ALU.mult, op1=ALU.add),
                              reads=(b_zt[i], bc, b_zc), writes=(b_zc,))
                    kb.op("dve", lambda e: e.tensor_tensor(out=ub[i][:], in0=zc[:, 0, :], in1=zc[:, 1, :], op=ALU.mult), reads=(b_zc,), writes=(b_ub[i],))
                    kb.op("dve", lambda e: e.tensor_scalar_mul(out=xr[i][:, 0, :], in0=zc[:, 2, :], scalar1=rn[:, cc:cc + 1]),
                          reads=(b_zc, b_rn), writes=(b_xr[i],))
                    kb.op("dve", lambda e: e.tensor_scalar_mul(out=xr[i][:, 1, :], in0=zc[:, 2, :], scalar1=skp[:, cc:cc + 1]),
                          reads=(b_zc, bc), writes=(b_xr[i],))
                    tsl = slice(t0, t0 + TT)
                    kb.dma("pool", UD[cc, :, tsl], ub[i][:], reads=(b_ub[i],), writes=(bH,), sem=b_ub[i])
                    kb.dma("pool", X2R[cc, :, tsl], xr[i][:, 0, :], reads=(b_xr[i],), writes=(bH,), sem=b_xr[i])
                    kb.dma("pool", X2S[cc, :, tsl], xr[i][:, 1, :], reads=(b_xr[i],), writes=(bH,), sem=b_xr[i])
        self.full_barrier()
        with ExitStack() as es:
            S = lambda n, shp, dt: es.enter_context(self.sbt("pf_" + n, shp, dt))
            P = lambda n: es.enter_context(self.pst("pf_" + n, [128, 512], F32))
            bc = kb.buf("pf_c", dma=True)
            def ldc(name, shape, dt, src, q="pool"):
                t = S(name, shape, dt)
                kb.dma(q, t[:], src, writes=(bc,), sem=bc, allow_slow_non_contiguous=True)
                return t
            F1 = ldc("F1", [K1, KC, 2 * N1], BF16, ext["c_F1_" + sn].ap().rearrange("(kc p) n -> p kc n", p=K1))
            TW = ldc("TW", [128, 2, 256], F32, ext["c_TW_" + sn].ap().rearrange("r p n -> p r n"), q="sp")
            F2 = ldc("F2", [128, 3, 128], BF16, ext["c_F2"].ap().rearrange("r p n -> p r n"))
            GA = ldc("GA", [128, 2, 256], BF16, ext["c_GA"].ap().rearrange("r p n -> p r n"))
            T2 = S("T2", [K1, 2, KC, 128], F32)
            G1 = S("G1", [K1, 2, KC, NB], BF16)
            for r in range(2):
                kb.dma("sp", T2[:, r, :, :], ext["c_T2_" + sn][r].rearrange("(kc p) n -> p kc n", p=K1), writes=(bc,), sem=bc)
                kb.dma("pool", G1[:, r, :, :], ext["c_G1_" + sn][r].rearrange("(kc p) n -> p kc n", p=K1), writes=(bc,), sem=bc)
            psA = [P(f"A{i}") for i in range(2)]
            psX = [P(f"X{i}") for i in range(2)]
            psC = [P(f"C{i}") for i in range(2)]
            psY = [P(f"Y{i}") for i in range(2)]
            b_A, b_X, b_C, b_Y = ([kb.buf(f"pf_{n}{i}") for i in range(2)] for n in "AXCY")
            dat = [S(f"dat{i}", [K1, KC, G, 128], BF16) for i in range(2)]
            b_dat = [kb.buf(f"pf_dat{i}", dma=True) for i in range(2)]
            t1, t2 = S("t1", [128, 512], F32), S("t2", [128, 512], F32)
            b_t1, b_t2 = kb.buf("t1"), kb.buf("t2")
            Bt = S("Bt", [128, 2, 512], BF16); b_B = kb.buf("B")
            Kf = [S(f"Kf{i}", [128, 2, 512], BF16) for i in range(2)]
            b_Kf = [kb.buf(f"pf_Kf{i}", dma=True) for i in range(2)]
            Xo = [S(f"Xo{i}", [128, 2, 512], BF16) for i in range(2)]
            b_Xo = [kb.buf(f"pf_Xo{i}", dma=True) for i in range(2)]
            Z = S("Z", [128, 2, 512], BF16); b_Z = kb.buf("Z")
            Dt = S("Dt", [K1, 2, KC, G * 128], BF16); b_D = kb.buf("D")
            x2 = [S(f"x2{i}", [KB1, 2, G, 128], F32) for i in range(2)]
            b_x2 = [kb.buf(f"pf_x2{i}", dma=True) for i in range(2)]
            yo = [S(f"yo{i}", [KB1, G, 128], BF16) for i in range(2)]
            b_yo = [kb.buf(f"pf_yo{i}", dma=True) for i in range(2)]

            def fwd(d, bd, Kp, kcs):
                for g in range(G):
                    bank, off = g // cpb, (g % cpb) * 2 * N1
                    for kc in range(kcs):
                        kb.op("pe", lambda e: e.matmul(psA[bank][:, off:off + 2 * N1], lhsT=d[0:Kp, kc, g, :], rhs=F1[0:Kp, kc, :],
                                                       start=(kc == 0), stop=(kc == kcs - 1)), reads=(bd, bc), writes=(b_A[bank],))
                for bank in range(2):
                    Av = psA[bank][:].rearrange("p (c r k) -> p c r k", r=2, k=N1)
                    Are, Aim = Av[:, :, 0, :], Av[:, :, 1, :]
                    twr = TW[:, 0, :].rearrange("p (c k) -> p c k", k=N1)
                    twi = TW[:, 1, :].rearrange("p (c k) -> p c k", k=N1)
                    v = lambda tt: tt[:, 0:256].rearrange("p (c k) -> p c k", k=N1)
                    cs = slice(bank * 256, (bank + 1) * 256)
                    kb.op("dve", lambda e: e.tensor_tensor(out=v(t1), in0=Are, in1=twr, op=ALU.mult), reads=(b_A[bank], bc), writes=(b_t1,))
                    kb.op("dve", lambda e: e.tensor_tensor(out=v(t2), in0=Aim, in1=twi, op=ALU.mult), reads=(b_A[bank], bc), writes=(b_t2,))
                    kb.op("dve", lambda e: e.tensor_tensor(out=Bt[:, 0, cs], in0=t1[:, 0:256], in1=t2[:, 0:256], op=ALU.subtract),
                          reads=(b_t1, b_t2), writes=(b_B,))
                    kb.op("dve", lambda e: e.tensor_tensor(out=v(t1), in0=Are, in1=twi, op=ALU.mult), reads=(b_A[bank], bc), writes=(b_t1,))
                    kb.op("dve", lambda e: e.tensor_tensor(out=v(t2), in0=Aim, in1=twr, op=ALU.mult), reads=(b_A[bank], bc), writes=(b_t2,))
                    kb.op("dve", lambda e: e.tensor_tensor(out=Bt[:, 1, cs], in0=t1[:, 0:256], in1=t2[:, 0:256], op=ALU.add),
                          reads=(b_t1, b_t2), writes=(b_B,))
                kb.op("pe", lambda e: e.matmul(psX[0][:], lhsT=F2[:, 0, :], rhs=Bt[:, 0, :], start=True, stop=False), reads=(b_B, bc), writes=(b_X[0],))
                kb.op("pe", lambda e: e.matmul(psX[0][:], lhsT=F2[:, 2, :], rhs=Bt[:, 1, :], start=False, stop=True), reads=(b_B, bc), writes=(b_X[0],))
                kb.op("pe", lambda e: e.matmul(psX[1][:], lhsT=F2[:, 1, :], rhs=Bt[:, 0, :], start=True, stop=False), reads=(b_B, bc), writes=(b_X[1],))
                kb.op("pe", lambda e: e.matmul(psX[1][:], lhsT=F2[:, 0, :], rhs=Bt[:, 1, :], start=False, stop=True), reads=(b_B, bc), writes=(b_X[1],))

            NG = 512 // G
            for gi in range(NG):
                i = gi % 2
                cc, c0 = (gi * G) // 128, (gi * G) % 128
                for kc in range(KC):
                    kb.dma("sp", dat[i][:, kc, :, :], KFT[cc, c0:c0 + G, kc * K1 * 128:(kc + 1) * K1 * 128].rearrange("c (p j) -> p c j", j=128),
                           reads=(bH,), writes=(b_dat[i],), sem=b_dat[i])
                fwd(dat[i], b_dat[i], K1, KC)
                kb.op("act", lambda e: e.copy(out=Xo[i][:, 0, :], in_=psX[0][:]), reads=(b_X[0],), writes=(b_Xo[i],))
                kb.op("act", lambda e: e.copy(out=Xo[i][:, 1, :], in_=psX[1][:]), reads=(b_X[1],), writes=(b_Xo[i],))
                kb.dma("pool", KFH[gi], Xo[i][:], reads=(b_Xo[i],), writes=(bH,), sem=b_Xo[i])
            for gi in range(NG):
                i = gi % 2
                cc, c0 = (gi * G) // 128, (gi * G) % 128
                kb.dma("sp", dat[i][0:KB1, 0, :, :], UD[cc, c0:c0 + G, :].rearrange("c (p j) -> p c j", j=128), reads=(bH,),
                       writes=(b_dat[i],), sem=b_dat[i])
                kb.dma("sp", Kf[i][:], KFH[gi], reads=(bH,), writes=(b_Kf[i],), sem=b_Kf[i])
                kb.dma("sp", x2[i][:, 0, :, :], X2R[cc, c0:c0 + G, :].rearrange("c (p j) -> p c j", j=128), reads=(bH,), writes=(b_x2[i],), sem=b_x2[i])
                kb.dma("sp", x2[i][:, 1, :, :], X2S[cc, c0:c0 + G, :].rearrange("c (p j) -> p c j", j=128), reads=(bH,), writes=(b_x2[i],), sem=b_x2[i])
                fwd(dat[i], b_dat[i], KB1, 1)
                for (o, a, b_, op) in ((0, 0, 0, None), (0, 1, 1, ALU.subtract), (1, 0, 1, None), (1, 1, 0, ALU.add)):
                    tgt, btgt = (t1, b_t1) if op is None else (t2, b_t2)
                    kb.op("dve", lambda e: e.tensor_tensor(out=tgt[:], in0=psX[a][:], in1=Kf[i][:, b_, :], op=ALU.mult),
                          reads=(b_X[a], b_Kf[i]), writes=(btgt,))
                    if op is not None:
                        kb.op("dve", lambda e: e.tensor_tensor(out=Z[:, o, :], in0=t1[:], in1=t2[:], op=op), reads=(b_t1, b_t2), writes=(b_Z,))
                regs = [(g, kc) for g in range(G) for kc in range(KC)]
                for r0 in range(0, len(regs), 2):
                    bk = (r0 // 2) % 2
                    for ri, (g, kc) in enumerate(regs[r0:r0 + 2]):
                        col = g * N1 + kc * K1
                        kb.op("pe", lambda e: e.matmul(psC[bk][0:K1, ri * 256:(ri + 1) * 256], lhsT=Z[:, 0, col:col + K1], rhs=GA[:, 0, :],
                                                       start=True, stop=False), reads=(b_Z, bc), writes=(b_C[bk],))
                        kb.op("pe", lambda e: e.matmul(psC[bk][0:K1, ri * 256:(ri + 1) * 256], lhsT=Z[:, 1, col:col + K1], rhs=GA[:, 1, :],
                                                       start=False, stop=True), reads=(b_Z, bc), writes=(b_C[bk],))
                    for ri, (g, kc) in enumerate(regs[r0:r0 + 2]):
                        Cre = psC[bk][0:K1, ri * 256:ri * 256 + 128]
                        Cim = psC[bk][0:K1, ri * 256 + 128:ri * 256 + 256]
                        tr, ti = T2[:, 0, kc, :], T2[:, 1, kc, :]
                        dsl = slice(g * 128, (g + 1) * 128)
                        kb.op("dve", lambda e: e.tensor_tensor(out=t1[0:K1, 0:128], in0=Cre, in1=tr, op=ALU.mult), reads=(b_C[bk], bc), writes=(b_t1,))
                        kb.op("dve", lambda e: e.tensor_tensor(out=t2[0:K1, 0:128], in0=Cim, in1=ti, op=ALU.mult), reads=(b_C[bk], bc), writes=(b_t2,))
                        kb.op("dve", lambda e: e.tensor_tensor(out=Dt[:, 0, kc, dsl], in0=t1[0:K1, 0:128], in1=t2[0:K1, 0:128], op=ALU.subtract),
                              reads=(b_t1, b_t2), writes=(b_D,))
                        kb.op("dve", lambda e: e.tensor_tensor(out=t1[0:K1, 0:128], in0=Cre, in1=ti, op=ALU.mult), reads=(b_C[bk], bc), writes=(b_t1,))
                        kb.op("dve", lambda e: e.tensor_tensor(out=t2[0:K1, 0:128], in0=Cim, in1=tr, op=ALU.mult), reads=(b_C[bk], bc), writes=(b_t2,))
                        kb.op("dve", lambda e: e.tensor_tensor(out=Dt[:, 1, kc, dsl], in0=t1[0:K1, 0:128], in1=t2[0:K1, 0:128], op=ALU.add),
                              reads=(b_t1, b_t2), writes=(b_D,))
                for q0 in range(0, G * 128, 512):
                    qn = min(512, G * 128 - q0)
                    yb = (q0 // 512) % 2
                    n_mm = 2 * KC
                    mi = 0
                    for kc in range(KC):
                        for r in range(2):
                            kb.op("pe", lambda e: e.matmul(psY[yb][0:KB1, 0:qn], lhsT=G1[:, r, kc, :], rhs=Dt[:, r, kc, q0:q0 + qn],
                                                           start=(mi == 0), stop=(mi == n_mm - 1)), reads=(b_D, bc), writes=(b_Y[yb],))
                            mi += 1
                    g0, gn = q0 // 128, qn // 128
                    xv = lambda r: x2[i][:, r, g0:g0 + gn, :].rearrange("p c j -> p (c j)")
                    uv = dat[i][0:KB1, 0, g0:g0 + gn, :].rearrange("p c j -> p (c j)")
                    kb.op("dve", lambda e: e.tensor_tensor(out=t1[0:KB1, 0:qn], in0=psY[yb][0:KB1, 0:qn], in1=xv(0), op=ALU.mult),
                          reads=(b_Y[yb], b_x2[i]), writes=(b_t1,))
                    kb.op("dve", lambda e: e.tensor_tensor(out=t2[0:KB1, 0:qn], in0=uv, in1=xv(1), op=ALU.mult), reads=(b_dat[i], b_x2[i]), writes=(b_t2,))
                    kb.op("dve", lambda e: e.tensor_tensor(out=yo[i][:, g0:g0 + gn, :].rearrange("p c j -> p (c j)"), in0=t1[0:KB1, 0:qn],
                                                           in1=t2[0:KB1, 0:qn], op=ALU.add), reads=(b_t1, b_t2), writes=(b_yo[i],))
                kb.dma("pool", self.YT[sn][4 + cc, c0:c0 + G, :].rearrange("c (p j) -> p c j", j=128), yo[i][:], reads=(b_yo[i],),
                       writes=(bYT,), sem=b_yo[i])
        self.full_barrier()


    def phase_gla(self, seq, l):
        nc, kb, c = self.nc, self.kb, self.c
        L, sn = seq.L, seq.name
        ext = self.ext
        bP1, bYT = self.b_P1[sn], self.b_YT[sn]
        OF = self.OF[sn]
        bOF = kb.buf("gl_of" + sn)
        NBK = L // 128
        with ExitStack() as es:
            S = lambda n, shp, dt: es.enter_context(self.sbt("pg_" + n, shp, dt))
            P = lambda n, dt=F32: es.enter_context(self.pst("pg_" + n, [128, 512], dt))
            bc = kb.buf("pg_c", dma=True)
            def ldc(name, shape, dt, src, q="sp"):
                t = S(name, shape, dt)
                kb.dma(q, t[:], src, writes=(bc,), sem=bc, allow_slow_non_contiguous=True)
                return t
            tri = ldc("tri", [128, 2, 128], F32, ext["c_tri"].ap().rearrange("r p n -> p r n"))
            msk = ldc("msk", [128, 2, 4, 128], F32, ext["c_msk"].ap().rearrange("r p h n -> p r h n"))
            gng = ldc("gng", [128, 512], F32, ext["c_gng"][l])
            gu = S("gu", [32, 2, 256], F32)
            kb.op("dve", lambda e: e.memset(gu[:], 0.0), writes=(bc,))
            for d in range(2):
                kb.dma("sp", gu[0:16, d, :], ext["gla_gate_up"][l, d], writes=(bc,), sem=bc)
                kb.dma("sp", gu[16:17, d, :], ext["gla_gate_b"][l, d:d + 1, :], writes=(bc,), sem=bc)
            one1 = S("one1", [128, 1], F32)
            kb.op("dve", lambda e: e.memset(one1[:], 1.0), writes=(bc,))
            lo = [S(f"lo{i}", [32, 128], F32) for i in range(2)]
            qk = [S(f"qk{i}", [128, 2, 2, 128], F32) for i in range(2)]
            vt = [S(f"vt{i}", [128, 512], BF16) for i in range(2)]
            b_in = [kb.buf(f"pg_in{i}", dma=True) for i in range(2)]
            for i in range(2):
                kb.op("dve", lambda e: e.memset(lo[i][:], 1.0), writes=(b_in[i],))
            ex = S("ex", [128, 256], F32); b_ex = kb.buf("ex")
            lsp = S("lsp", [128, 256], F32); b_lsp = kb.buf("lsp")
            bsm = S("bsm", [128, 2, 2, 2, 2], F32); b_bsm = kb.buf("bsm")
            eb = S("eb", [128, 2, 2], F32); b_eb = kb.buf("eb")
            E = S("E", [128, 4, 2, 128], F32); b_E = kb.buf("E")
            Dd = S("Dd", [128, 2, 2, 128], F32); b_Dd = kb.buf("Dd")
            qkt = S("qkt", [128, 4, 2, 128], BF16); b_qkt = kb.buf("qkt")
            AT = S("AT", [128, 4, 128], BF16); b_AT = kb.buf("AT")
            qz = S("qz", [128, 2, 2, 128], BF16); b_qz = kb.buf("qz")
            kb.op("dve", lambda e: e.memset(qz[:], 0.0), writes=(b_qz,))
            khat = S("khat", [128, 2, 128], BF16); b_khat = kb.buf("khat")
            St = S("St", [128, 2, 128], F32); b_S = kb.buf("S")
            Sb = S("Sb", [128, 2, 128], BF16); b_Sb = kb.buf("Sb")
            osb = [S(f"osb{i}", [128, 512], F32) for i in range(2)]
            b_osb = [kb.buf(f"pg_osb{i}", dma=True) for i in range(2)]
            ofl = [S(f"ofl{i}", [128, 512], F32) for i in range(2)]
            ogt = [S(f"ogt{i}", [128, 512], F32) for i in range(2)]
            b_ofl = [kb.buf(f"pg_ofl{i}", dma=True) for i in range(2)]
            sq = S("sq", [128, 512], F32); b_sq = kb.buf("sq")
            ss = S("ss", [128, 4], F32); b_ss = kb.buf("ss")
            yb = S("yb", [128, 512], BF16); b_yb = kb.buf("yb")
            yT = [S(f"yT{i}", [128, 4, 128], BF16) for i in range(2)]
            b_yT = [kb.buf(f"pg_yT{i}", dma=True) for i in range(2)]
            ps_l, ps_b, ps_A, ps_O, ps_U = P("l"), P("b"), P("A"), P("O"), P("U")
            ps_T, ps_Y = P("T", BF16), P("Y", BF16)
            b_pl, b_pb, b_pA, b_pO, b_pU, b_pT, b_pY = (kb.buf("pg_p" + n) for n in "lbAOUTY")
            blk_i = 0
            for d in range(2):
                kb.op("dve", lambda e: e.memset(St[:], 0.0), writes=(b_S,))
                kb.op("dve", lambda e: e.memset(Sb[:], 0.0), writes=(b_Sb,))
                order = range(NBK) if d == 0 else range(NBK - 1, -1, -1)
                mid, last = (31, 63) if d == 0 else (32, 0)
                for blk in order:
                    i = blk_i % 2
                    blk_i += 1
                    tsl = slice(blk * 128, (blk + 1) * 128)
                    kb.dma("sp", lo[i][0:16, :], self.LOT[sn][d, :, tsl], reads=(bP1,), writes=(b_in[i],), sem=b_in[i])
                    kb.dma("sp", qk[i][:, 0, :, :], self.GQT[sn][:, :, tsl].rearrange("h p n -> p h n"), reads=(bP1,), writes=(b_in[i],), sem=b_in[i])
                    kb.dma("sp", qk[i][:, 1, :, :], self.GKT[sn][:, :, tsl].rearrange("h p n -> p h n"), reads=(bP1,), writes=(b_in[i],), sem=b_in[i])
                    kb.dma("sp", vt[i][:], self.GV[sn][tsl, :], reads=(bP1,), writes=(b_in[i],), sem=b_in[i])
                    if d == 1:
                        kb.dma("sp", ofl[i][:], OF[tsl, :], reads=(bOF,), writes=(b_ofl[i],), sem=b_ofl[i])
                        kb.dma("sp", ogt[i][:], self.GOG[sn][tsl, :], reads=(bP1,), writes=(b_ofl[i],), sem=b_ofl[i])
                    kb.op("pe", lambda e: e.matmul(ps_l[:, 0:256], lhsT=lo[i][:], rhs=gu[:, d, :], start=True, stop=True),
                          reads=(b_in[i], bc), writes=(b_pl,))
                    kb.op("act", lambda e: e.activation(out=ex[:], in_=ps_l[:, 0:256], func=AF.Exp, scale=-1.0), reads=(b_pl,), writes=(b_ex,))
                    kb.op("dve", lambda e: e.tensor_scalar_add(out=ex[:], in0=ex[:], scalar1=1.0), reads=(b_ex,), writes=(b_ex,))
                    kb.op("act", lambda e: e.activation(out=lsp[:], in_=ex[:], func=AF.Ln), reads=(b_ex,), writes=(b_lsp,))
                    for hp in range(2):
                        kb.op("pe", lambda e: e.matmul(ps_b[:, hp * 128:(hp + 1) * 128], lhsT=lsp[:, hp * 128:(hp + 1) * 128], rhs=tri[:, d, :],
                                                       start=True, stop=True), reads=(b_lsp, bc), writes=(b_pb,))
                    bv = ps_b[:, 0:256].rearrange("p (h c n) -> p h c n", h=2, c=2)
                    for w_, col in enumerate((mid, last)):
                        kb.op("dve", lambda e: e.tensor_scalar_mul(out=bsm[:, 0, :, :, w_], in0=bv[:, :, :, col], scalar1=-1.0), reads=(b_pb,), writes=(b_bsm,))
                        kb.op("dve", lambda e: e.tensor_copy(out=bsm[:, 1, :, :, w_], in_=bv[:, :, :, col]), reads=(b_pb,), writes=(b_bsm,))
                    kb.op("act", lambda e: e.activation(out=eb[:], in_=bsm[:, 1, :, :, 1], func=AF.Exp), reads=(b_bsm,), writes=(b_eb,))
                    for hp in range(2):
                        for ci in range(2):
                            cs = slice(ci * 64, (ci + 1) * 64)
                            src = ps_b[:, hp * 128 + ci * 64:hp * 128 + (ci + 1) * 64]
                            for w_ in range(2):
                                kb.op("dve", lambda e: e.tensor_scalar(out=Dd[:, w_, hp, cs], in0=src, scalar1=bsm[:, 0, hp, ci, w_:w_ + 1],
                                                                       scalar2=None, op0=ALU.add), reads=(b_pb, b_bsm), writes=(b_Dd,))
                    pbv = ps_b[:, 0:256].rearrange("p (h n) -> p h n", h=2)
                    kb.op("act", lambda e: e.activation(out=E[:, 0, :, :], in_=Dd[:, 0, :, :], func=AF.Exp), reads=(b_Dd,), writes=(b_E,))
                    kb.op("act", lambda e: e.activation(out=E[:, 1, :, :], in_=Dd[:, 0, :, :], func=AF.Exp, scale=-1.0), reads=(b_Dd,), writes=(b_E,))
                    kb.op("act", lambda e: e.activation(out=E[:, 2, :, :], in_=pbv, func=AF.Exp), reads=(b_pb,), writes=(b_E,))
                    kb.op("act", lambda e: e.activation(out=E[:, 3, :, :], in_=Dd[:, 1, :, :], func=AF.Exp, scale=-1.0), reads=(b_Dd,), writes=(b_E,))
                    for ci in range(2):
                        cs = slice(ci * 64, (ci + 1) * 64)
                        kb.op("dve", lambda e: e.scalar_tensor_tensor(out=qz[:, ci, :, cs], in0=qk[i][:, 0, :, cs], scalar=0.125, in1=E[:, 2, :, cs],
                                                                      op0=ALU.mult, op1=ALU.mult), reads=(b_in[i], b_E), writes=(b_qz,))
                    for which in (0, 1, 3):
                        src = qk[i][:, which % 2, :, :]
                        if which % 2 == 0:
                            kb.op("dve", lambda e: e.scalar_tensor_tensor(out=qkt[:, which, :, :], in0=src, scalar=0.125, in1=E[:, which, :, :],
                                                                          op0=ALU.mult, op1=ALU.mult), reads=(b_in[i], b_E), writes=(b_qkt,))
                        else:
                            kb.op("dve", lambda e: e.tensor_tensor(out=qkt[:, which, :, :], in0=src, in1=E[:, which, :, :], op=ALU.mult),
                                  reads=(b_in[i], b_E), writes=(b_qkt,))
                    if "DBG1" in self.taps and d == 0 and blk == 0 and sn == "p":
                        bdbg = kb.buf("dbg", dma=True)
                        kb.dma("sp", self.dbg["DBG1"][:, :], lsp[:], reads=(b_lsp,), writes=(), sem=bdbg)
                        kb.dma("sp", self.dbg["DBG2"][:, :], E[:].rearrange("p a b c -> p (a b c)"), reads=(b_E,), writes=(), sem=bdbg)
                        kb.dma("sp", self.dbg["DBG3"][:, :], qkt[:].rearrange("p a b c -> p (a b c)"), reads=(b_qkt,), writes=(), sem=bdbg)
                    for h in range(4):
                        hp, e_ = h // 2, h % 2
                        rows = slice(64 * e_, 64 * e_ + 64)
                        kb.op("pe", lambda e: e.matmul(ps_A[:, h * 128:(h + 1) * 128], lhsT=qkt[rows, 1, hp, :], rhs=qkt[rows, 0, hp, :],
                                                       start=True, stop=True), reads=(b_qkt,), writes=(b_pA,))
                    kb.op("dve", lambda e: e.tensor_tensor(out=AT[:].rearrange("p h n -> p (h n)"), in0=ps_A[:],
                                                           in1=msk[:, d, :, :].rearrange("p h n -> p (h n)"), op=ALU.mult),
                          reads=(b_pA, bc), writes=(b_AT,))
                    if "DBG1" in self.taps and d == 0 and blk == 0 and sn == "p":
                        kb.dma("sp", self.dbg["DBG4"][:, :], AT[:].rearrange("p a b -> p (a b)"), reads=(b_AT,), writes=(), sem=bdbg)
                    for hp in range(2):
                        kb.op("pe", lambda e: e.transpose(out=ps_T[:, hp * 128:(hp + 1) * 128], in_=qkt[:, 3, hp, :], identity=c["ident_b"][:]),
                              reads=(b_qkt, self.c_sem), writes=(b_pT,))
                    kb.op("act", lambda e: e.copy(out=khat[:].rearrange("p h n -> p (h n)"), in_=ps_T[:, 0:256]), reads=(b_pT,), writes=(b_khat,))
                    for h in range(4):
                        kb.op("pe", lambda e: e.matmul(ps_O[:, h * 128:(h + 1) * 128], lhsT=AT[:, h, :], rhs=vt[i][:, h * 128:(h + 1) * 128],
                                                       start=(h == 0), stop=False), reads=(b_AT, b_in[i]), writes=(b_pO,))
                    for cidx, ci in enumerate((0, 1) if d == 0 else (1, 0)):
                        crow = slice(ci * 64, (ci + 1) * 64)
                        for h in range(4):
                            hp, e_ = h // 2, h % 2
                            rows = slice(64 * e_, 64 * e_ + 64)
                            kb.op("pe", lambda e: e.matmul(ps_O[:, h * 128:(h + 1) * 128], lhsT=qz[rows, ci, hp, :], rhs=Sb[rows, hp, :],
                                                           start=False, stop=(cidx == 1)), reads=(b_qz, b_Sb), writes=(b_pO,))
                        for h in range(4):
                            hp = h // 2
                            kb.op("pe", lambda e: e.matmul(ps_U[:, h * 128:(h + 1) * 128], lhsT=khat[crow, hp, :], rhs=vt[i][crow, h * 128:(h + 1) * 128],
                                                           start=True, stop=True), reads=(b_khat, b_in[i]), writes=(b_pU,))
                        for h in range(4):
                            hp, e_ = h // 2, h % 2
                            rows = slice(64 * e_, 64 * e_ + 64)
                            kb.op("dve", lambda e: e.scalar_tensor_tensor(out=St[rows, hp, :], in0=St[rows, hp, :], scalar=eb[rows, hp, ci:ci + 1],
                                                                          in1=ps_U[rows, h * 128:(h + 1) * 128], op0=ALU.mult, op1=ALU.add),
                                  reads=(b_pU, b_eb, b_S), writes=(b_S,))
                        kb.op("act", lambda e: e.copy(out=Sb[:], in_=St[:]), reads=(b_S,), writes=(b_Sb,))
                    if d == 0:
                        kb.op("dve", lambda e: e.tensor_copy(out=osb[i][:], in_=ps_O[:]), reads=(b_pO,), writes=(b_osb[i],))
                        kb.dma("pool", OF[tsl, :], osb[i][:], reads=(b_osb[i],), writes=(bOF,), sem=b_osb[i])
                    else:
                        o = osb[i]
                        kb.op("dve", lambda e: e.tensor_tensor(out=o[:], in0=ps_O[:], in1=ofl[i][:], op=ALU.add), reads=(b_pO, b_ofl[i]), writes=(b_osb[i],))
                        kb.op("act", lambda e: e.activation(out=sq[:], in_=o[:], func=AF.Square), reads=(b_osb[i],), writes=(b_sq,))
                        kb.op("dve", lambda e: e.reduce_sum(out=ss[:], in_=sq[:].rearrange("p (h n) -> p h n", h=4), axis=AX.X), reads=(b_sq,), writes=(b_ss,))
                        kb.op("dve", lambda e: e.tensor_scalar(out=ss[:], in0=ss[:], scalar1=1.0 / 128, scalar2=EPS, op0=ALU.mult, op1=ALU.add),
                              reads=(b_ss,), writes=(b_ss,))
                        kb.op("act", lambda e: e.activation(out=ss[:], in_=ss[:], func=AF.Ln), reads=(b_ss,), writes=(b_ss,))
                        kb.op("act", lambda e: e.activation(out=ss[:], in_=ss[:], func=AF.Exp, scale=-0.5), reads=(b_ss,), writes=(b_ss,))
                        kb.op("act", lambda e: e.activation(out=sq[:], in_=ogt[i][:], func=AF.Silu), reads=(b_ofl[i], b_sq), writes=(b_sq,))
                        kb.op("dve", lambda e: e.tensor_tensor(out=sq[:], in0=sq[:], in1=gng[:], op=ALU.mult), reads=(b_sq, bc), writes=(b_sq,))
                        for h in range(4):
                            hs = slice(h * 128, (h + 1) * 128)
                            kb.op("dve", lambda e: e.scalar_tensor_tensor(out=yb[:, hs], in0=o[:, hs], scalar=ss[:, h:h + 1], in1=sq[:, hs],
                                                                          op0=ALU.mult, op1=ALU.mult), reads=(b_osb[i], b_ss, b_sq), writes=(b_yb,))
                        for h in range(4):
                            kb.op("pe", lambda e: e.transpose(out=ps_Y[:, h * 128:(h + 1) * 128], in_=yb[:, h * 128:(h + 1) * 128], identity=c["ident_b"][:]),
                                  reads=(b_yb, self.c_sem), writes=(b_pY,))
                        kb.op("act", lambda e: e.copy(out=yT[i][:].rearrange("p h n -> p (h n)"), in_=ps_Y[:]), reads=(b_pY,), writes=(b_yT[i],))
                        kb.dma("pool", self.YT[sn][8:12, :, tsl].rearrange("h p n -> p h n"), yT[i][:], reads=(b_yT[i],), writes=(bYT,), sem=b_yT[i])
        self.full_barrier()

    def phase_dense(self, seq, l):
        nc, kb, c = self.nc, self.kb, self.c
        L = seq.L
        XT, YT = self.XT[seq.name], self.YT[seq.name]
        bXT, bYT = self.b_XT[seq.name], self.b_YT[seq.name]
        wb = self.wb
        with ExitStack() as es:
            S = lambda n, shp, dt: es.enter_context(self.sbt("pd_" + n, shp, dt))
            xt = [S(f"xt{i}", [128, 8, TT], F32) for i in range(2)]
            yt = [S(f"yt{i}", [128, 12, TT], BF16) for i in range(2)]
            hT = S("hT", [128, 8, TT], BF16)
            sq = S("sq", [128, 8, TT], BF16)
            rstd = S("rstd", [128, TT], F32)
            mg = S("mg", [128, 8, TT], F32)
            mgb = S("mgb", [128, 8, TT], BF16)
            act = S("act", [128, NFC, TT], BF16)
            sg = [S(f"sg{i}", [128, TT], F32) for i in range(2)]
            tm = [S(f"tm{i}", [128, TT], F32) for i in range(2)]
            NW = 4
            wsl = [S(f"w{i}", [128, 8, 1024], BF16) for i in range(NW)]
            ps = [es.enter_context(self.pst(f"pd_ps{i}", [128, 512], F32)) for i in range(8)]
            b_xt = [kb.buf(f"pd_xt{i}", dma=True) for i in range(2)]
            b_yt = [kb.buf(f"pd_yt{i}", dma=True) for i in range(2)]
            b_w = [kb.buf(f"pd_w{i}", dma=True) for i in range(NW)]
            b_ps = [kb.buf(f"pd_ps{i}") for i in range(8)]
            b_hT, b_sq, b_rstd, b_mg, b_mgb, b_act = (kb.buf(n) for n in ["hT", "sq", "rstd", "mg", "mgb", "act"])
            b_sg = [kb.buf("sg0"), kb.buf("sg1")]
            b_tm = [kb.buf("tm0"), kb.buf("tm1")]
            st = {"w": 0, "ps": 0}

            def wload(src_ap, shape3):
                i = st["w"] % NW
                st["w"] += 1
                a, b = shape3
                view = wsl[i][:].rearrange("p a b -> p (a b)")[:, 0:a * b].rearrange("p (a b) -> p a b", a=a)
                kb.dma("sp", view, src_ap, reads=(self.b_wb,), writes=(b_w[i],), sem=b_w[i])
                return view, b_w[i]

            def nps():
                i = st["ps"] % 6
                st["ps"] += 1
                return ps[i], b_ps[i]

            ntile = L // TT
            def load_tile(t):
                i = t % 2
                ts = slice(t * TT, (t + 1) * TT)
                kb.dma("sp", xt[i][:], XT[:, :, ts].rearrange("k p n -> p k n"), reads=(bXT,), writes=(b_xt[i],), sem=b_xt[i])
                kb.dma("sp", yt[i][:], YT[:, :, ts].rearrange("k p n -> p k n"), reads=(bYT,), writes=(b_yt[i],), sem=b_yt[i])

            load_tile(0)
            for t in range(ntile):
                i = t % 2
                ts = slice(t * TT, (t + 1) * TT)
                X, bX = xt[i], b_xt[i]
                if t + 1 < ntile:
                    load_tile(t + 1)
                self.rmsnorm_T(X, bX, hT, b_hT, c["g_mix"][:, l, :], sq, b_sq, ps[7], b_ps[7], rstd, b_rstd)
                for n in range(3):
                    wg, bwg = wload(wb["w_gates"][l, :, n * 1024:(n + 1) * 1024].rearrange("(k p) n -> p k n", p=128), (8, 1024))
                    wbr, bwbr = wload(wb["w_branch"][l, n * 512:(n + 1) * 512, :].rearrange("(k p) n -> p k n", p=128), (4, 1024))
                    for dc in range(8):
                        pa, bpa = nps()
                        for k in range(8):
                            kb.op("pe", lambda e: e.matmul(pa[:], lhsT=wg[:, k, dc * 128:(dc + 1) * 128], rhs=hT[:, k, :],
                                                           start=(k == 0), stop=(k == 7)), reads=(bwg, b_hT), writes=(bpa,))
                        pb, bpb = nps()
                        for k in range(4):
                            kb.op("pe", lambda e: e.matmul(pb[:], lhsT=wbr[:, k, dc * 128:(dc + 1) * 128], rhs=yt[i][:, n * 4 + k, :],
                                                           start=(k == 0), stop=(k == 3)), reads=(bwbr, b_yt[i]), writes=(bpb,))
                        j = dc % 2
                        kb.op("act", lambda e: e.activation(out=sg[j][:], in_=pa[:], func=AF.Sigmoid), reads=(bpa,), writes=(b_sg[j],))
                        if n == 0:
                            kb.op("dve", lambda e: e.tensor_tensor(out=mg[:, dc, :], in0=sg[j][:], in1=pb[:], op=ALU.mult),
                                  reads=(b_sg[j], bpb), writes=(b_mg,))
                        else:
                            kb.op("dve", lambda e: e.tensor_tensor(out=tm[j][:], in0=sg[j][:], in1=pb[:], op=ALU.mult),
                                  reads=(b_sg[j], bpb), writes=(b_tm[j],))
                            if n == 1:
                                kb.op("dve", lambda e: e.tensor_tensor(out=mg[:, dc, :], in0=mg[:, dc, :], in1=tm[j][:], op=ALU.add),
                                      reads=(b_tm[j], b_mg), writes=(b_mg,))
                            else:
                                kb.op("dve", lambda e: e.tensor_tensor(out=mgb[:, dc, :], in0=mg[:, dc, :], in1=tm[j][:], op=ALU.add),
                                      reads=(b_tm[j], b_mg), writes=(b_mgb,))
                wo, bwo = wload(wb["w_out"][l].rearrange("(k p) n -> p k n", p=128), (8, 1024))
                for dc in range(8):
                    pa, bpa = nps()
                    for k in range(8):
                        kb.op("pe", lambda e: e.matmul(pa[:], lhsT=wo[:, k, dc * 128:(dc + 1) * 128], rhs=mgb[:, k, :],
                                                       start=(k == 0), stop=(k == 7)), reads=(bwo, b_mgb), writes=(bpa,))
                    kb.op("dve", lambda e: e.tensor_tensor(out=X[:, dc, :], in0=X[:, dc, :], in1=pa[:], op=ALU.add),
                          reads=(bpa, bX), writes=(bX,))
                self.rmsnorm_T(X, bX, hT, b_hT, c["g_ffn"][:, l, :], sq, b_sq, ps[7], b_ps[7], rstd, b_rstd)
                for g in range(0, NFC, 4):
                    nf = min(4, NFC - g)
                    cs = slice(g * 128, (g + nf) * 128)
                    wgt, bwgt = wload(wb["w_ffn_gate"][l, :, cs].rearrange("(k p) n -> p k n", p=128), (8, nf * 128))
                    wup, bwup = wload(wb["w_ffn_up"][l, :, cs].rearrange("(k p) n -> p k n", p=128), (8, nf * 128))
                    for f in range(nf):
                        pa, bpa = nps()
                        for k in range(8):
                            kb.op("pe", lambda e: e.matmul(pa[:], lhsT=wgt[:, k, f * 128:(f + 1) * 128], rhs=hT[:, k, :],
                                                           start=(k == 0), stop=(k == 7)), reads=(bwgt, b_hT), writes=(bpa,))
                        pb, bpb = nps()
                        for k in range(8):
                            kb.op("pe", lambda e: e.matmul(pb[:], lhsT=wup[:, k, f * 128:(f + 1) * 128], rhs=hT[:, k, :],
                                                           start=(k == 0), stop=(k == 7)), reads=(bwup, b_hT), writes=(bpb,))
                        j = f % 2
                        kb.op("act", lambda e: e.activation(out=sg[j][:], in_=pa[:], func=AF.Silu), reads=(bpa,), writes=(b_sg[j],))
                        kb.op("dve", lambda e: e.tensor_tensor(out=act[:, g + f, :], in0=sg[j][:], in1=pb[:], op=ALU.mult),
                              reads=(b_sg[j], bpb), writes=(b_act,))
                pd = []
                for half in range(2):
                    pd = [nps() for _ in range(4)]
                    for g in range(0, NFC, 8):
                        nf = min(8, NFC - g)
                        wd, bwd = wload(wb["w_ffn_down"][l, g * 128:(g + nf) * 128, half * 512:(half + 1) * 512]
                                        .rearrange("(k p) n -> p k n", p=128), (nf, 512))
                        for dq in range(4):
                            for f in range(nf):
                                kb.op("pe", lambda e: e.matmul(pd[dq][0][:], lhsT=wd[:, f, dq * 128:(dq + 1) * 128], rhs=act[:, g + f, :],
                                                               start=(g + f == 0), stop=(g + f == NFC - 1)),
                                      reads=(bwd, b_act), writes=(pd[dq][1],))
                    for dq in range(4):
                        dc = half * 4 + dq
                        kb.op("dve", lambda e: e.tensor_tensor(out=X[:, dc, :], in0=X[:, dc, :], in1=pd[dq][0][:], op=ALU.add),
                              reads=(pd[dq][1], bX), writes=(bX,))
                kb.dma("pool", XT[:, :, ts].rearrange("k p n -> p k n"), X[:], reads=(bX,), writes=(bXT,), sem=bX)
        self.full_barrier()

    def phase_out(self, seq):
        nc, kb, c = self.nc, self.kb, self.c
        L = seq.L
        XT = self.XT[seq.name]
        bXT = self.b_XT[seq.name]
        out = self.out[seq.name]
        with ExitStack() as es:
            S = lambda n, shp, dt: es.enter_context(self.sbt("po_" + n, shp, dt))
            xt = [S(f"xt{i}", [128, 8, TT], F32) for i in range(2)]
            hT = S("hT", [128, 8, TT], F32)
            sq = S("sq", [128, 8, TT], BF16)
            rstd = S("rstd", [128, TT], F32)
            ot = [S(f"ot{i}", [128, D], F32) for i in range(2)]
            ps = [es.enter_context(self.pst(f"po_ps{i}", [128, 512], F32)) for i in range(5)]
            b_xt = [kb.buf(f"po_xt{i}", dma=True) for i in range(2)]
            b_ot = [kb.buf(f"po_ot{i}", dma=True) for i in range(2)]
            b_ps = [kb.buf(f"po_ps{i}") for i in range(5)]
            b_hT, b_sq, b_rstd = kb.buf("hT"), kb.buf("sq"), kb.buf("rstd")
            cnt = 0
            for t in range(L // TT):
                i = t % 2
                ts = slice(t * TT, (t + 1) * TT)
                kb.dma("sp", xt[i][:], XT[:, :, ts].rearrange("k p n -> p k n"), reads=(bXT,), writes=(b_xt[i],), sem=b_xt[i])
                self.rmsnorm_T(xt[i], b_xt[i], hT, b_hT, c["g_fin"][:, :], sq, b_sq, ps[4], b_ps[4], rstd, b_rstd)
                for sb in range(TT // 128):
                    o = cnt % 2
                    for hf in range(2):
                        pb = (cnt * 2 + hf) % 4
                        for k4 in range(4):
                            k = hf * 4 + k4
                            kb.op("pe", lambda e: e.transpose(out=ps[pb][:, k4 * 128:(k4 + 1) * 128],
                                                              in_=hT[:, k, sb * 128:(sb + 1) * 128], identity=c["ident_f"][:]),
                                  reads=(b_hT, self.c_sem), writes=(b_ps[pb],))
                        dst = ot[o][:, hf * 512:(hf + 1) * 512]
                        if hf == 0:
                            kb.op("dve", lambda e: e.tensor_copy(out=dst, in_=ps[pb][:]), reads=(b_ps[pb],), writes=(b_ot[o],))
                        else:
                            kb.op("act", lambda e: e.copy(out=dst, in_=ps[pb][:]), reads=(b_ps[pb],), writes=(b_ot[o],))
                    r0 = t * TT + sb * 128
                    kb.dma("pool", out[r0:r0 + 128, :], ot[o][:], reads=(b_ot[o],), writes=(self.b_out,), sem=b_ot[o])
                    cnt += 1
        self.full_barrier()

    def zero_YT(self, seq):
        kb = self.kb
        bz = kb.buf("zz" + seq.name, dma=True)
        for k in range(12):
            kb.dma("pool", self.YT[seq.name][k], self.ext["dbg_yt_" + seq.name][k], writes=(self.b_YT[seq.name],), sem=bz)
        self.full_barrier()

    def build(self, mixers=True, stages=("attn", "hyena", "gla")):
        with ExitStack() as es:
            self.consts(es)
            self.prepass_weights()
            for s in self.seqs:
                self.phase_in(s)
            for l in range(self.depth):
                for s in self.seqs:
                    if not mixers:
                        if l == 0:
                            self.zero_YT(s)
                    else:
                        if "dbg_yt_" + s.name in self.ext and l == 0:
                            self.zero_YT(s)
                        self.phase_proj(s, l)
                        if "attn" in stages:
                            self.phase_attn(s)
                        if "hyena" in stages:
                            self.phase_hyena(s, l)
                        if "gla" in stages:
                            self.phase_gla(s, l)
                    self.phase_dense(s, l)
            for s in self.seqs:
                self.phase_out(s)
            self.full_barrier()
        self.kb.close()
        return self.nc


def rope_tables(L):
    rows = L // 64
    r = np.repeat(np.arange(rows, dtype=np.float32), 64)
    cc = np.tile(np.arange(64, dtype=np.float32), rows)
    inv = (10000.0 ** (-np.arange(0, 32, 2, dtype=np.float32) / 32)).astype(np.float32)
    ang = np.concatenate([r[:, None] * inv, cc[:, None] * inv], axis=-1).astype(np.float32)
    cos, sin = np.cos(ang).T, np.sin(ang).T
    c128 = np.concatenate([cos, cos, cos, cos], 0).astype(np.float32)
    s128 = np.concatenate([sin, sin, sin, sin], 0).astype(np.float32)
    return np.ascontiguousarray(c128), np.ascontiguousarray(s128)


def hyena_consts(L, tag):
    f32 = np.float32
    NB = L // 128; N1 = 2 * NB; K1 = min(N1, 128); N = 2 * L
    t = np.linspace(0.0, 1.0, L, dtype=f32)
    w = (2.0 * math.pi * np.arange(L, dtype=f32) / L).astype(f32)
    fbands = np.linspace(1e-4, 15, 16, dtype=f32)
    ph = w[:, None] * fbands
    feats = np.concatenate([t[:, None], np.cos(ph), -np.sin(ph)], axis=-1).astype(f32)
    idx = np.concatenate([np.arange(L), [0], L - np.arange(1, L)])
    featsT = np.ascontiguousarray(feats[idx].T)
    tn = t[idx].copy(); tn[L] = 1e4
    tnb = np.ascontiguousarray(np.broadcast_to(tn[None, :], (128, 2 * L))).astype(f32)
    n1 = np.arange(N1)[:, None].astype(np.float64); k1 = np.arange(N1)[None, :].astype(np.float64)
    a = 2 * math.pi * n1 * k1 / N1
    F1 = np.concatenate([np.cos(a), -np.sin(a)], axis=1).astype(f32)
    n2 = np.arange(128)[:, None].astype(np.float64)
    a = 2 * math.pi * n2 * k1 / N
    cpb = 256 // N1
    TW = np.stack([np.tile(np.cos(a), (1, cpb)), np.tile(-np.sin(a), (1, cpb))]).astype(f32)
    a = (2 * math.pi * np.arange(N1)[:, None].astype(np.float64) * np.arange(128)[None, :] / N)
    T2 = np.stack([np.cos(a), np.sin(a)]).astype(f32)
    a = 2 * math.pi * np.arange(N1)[:, None].astype(np.float64) * np.arange(NB)[None, :] / N1
    G1 = np.stack([np.cos(a) / N, -np.sin(a) / N]).astype(f32)
    return {"c_feats_" + tag: featsT, "c_tnb_" + tag: tnb, "c_F1_" + tag: F1, "c_TW_" + tag: TW, "c_T2_" + tag: T2, "c_G1_" + tag: G1}


def hyena_consts_common():
    f32 = np.float32
    a = 2 * math.pi * np.arange(128)[:, None].astype(np.float64) * np.arange(128)[None, :] / 128
    F2 = np.stack([np.cos(a), -np.sin(a), np.sin(a)]).astype(f32)
    GA = np.stack([np.concatenate([np.cos(a), np.sin(a)], 1), np.concatenate([-np.sin(a), np.cos(a)], 1)]).astype(f32)
    deltas = np.linspace(abs(math.log(1e-2) / 1.5), abs(math.log(1e-2) / 0.3), 512, dtype=f32)
    ndel = np.ascontiguousarray((-deltas).reshape(4, 128).T)
    return {"c_F2": F2, "c_GA": GA, "c_ndelta": ndel}

def host_consts(LP, LS):
    rot = np.zeros((128, 128), np.float32)
    for hb in (0, 64):
        for m in range(32):
            rot[hb + m + 32, hb + m] = -1.0
            rot[hb + m, hb + m + 32] = 1.0
    bo = np.zeros((128, 128), np.float32)
    bo[:64, :64] = 1.0
    bo[64:, 64:] = 1.0
    out = {"c_ident": np.eye(128, dtype=np.float32), "c_ones": np.ones((128, 128), np.float32), "c_rot": rot, "c_bo64": bo}
    for nm, L in (("p", LP), ("s", LS)):
        out["c_cos_" + nm], out["c_sin_" + nm] = rope_tables(L)
        out.update(hyena_consts(L, nm))
    out.update(hyena_consts_common())
    j = np.arange(128)[:, None]; i = np.arange(128)[None, :]
    same = (j // 64) == (i // 64)
    tri = np.stack([np.where(same & (j <= i), -1.0 / 16, 0.0), np.where(same & (j >= i), -1.0 / 16, 0.0)]).astype(np.float32)
    mf = np.where(same & (j <= i), 1.0, 0.0); mb = np.where(same & (j > i), 1.0, 0.0)
    msk = np.stack([np.repeat(mf[:, None, :], 4, 1), np.repeat(mb[:, None, :], 4, 1)]).astype(np.float32)
    out["c_tri"], out["c_msk"] = tri, msk
    return out


def gng_layout(g):
    NL = g.shape[0]
    return np.ascontiguousarray(np.broadcast_to(np.tile(g, (1, 4))[:, None, :], (NL, 128, 512))).astype(np.float32)


_CACHE = {}


def kernel(**inputs):
    LP, LS, DEPTH = 16384, 2048, 4
    inp = {k: np.ascontiguousarray(np.asarray(v)) for k, v in inputs.items()}
    hc = host_consts(LP, LS)
    m = {"x_p": inp["x_prompt"][0], **hc}
    for n in WNAMES:
        m[n] = inp[n]
    m["c_gng"] = gng_layout(inp["gla_norm_g"])
    shapes = {k: v.shape for k, v in m.items()}
    shapes["x_s"] = (LS, 1024)
    b = Builder(LP, LS, DEPTH, shapes)
    nc = b.build(mixers=True)
    maps = [dict(m, x_s=inp["x_sample"][c]) for c in range(8)]
    res = run_bass_kernel_spmd(nc, maps, core_ids=list(range(8)))
    yp = np.concatenate([res.results[c]["y_p"][c * 2048:(c + 1) * 2048] for c in range(8)], axis=0)[None]
    ys = np.stack([res.results[c]["y_s"] for c in range(8)], axis=0)
    return (yp.astype(np.float32), ys.astype(np.float32))
```
